# Optimizing a Trainium2 kernel written in Bass

```python
import math
import jax, jax.numpy as jnp
from jax import lax
import numpy as np

D_MODEL = 1024
BATCH = 8
SEQ = 4096
DEPTH = 2
DEC_BATCH = 32
DEC_SEQ = 2048
PAST_LEN = 128

RET_HEADS = 8
RET_DK = 64
RET_DV = 64
RET_WIDTH = RET_HEADS * RET_DV
RET_QK = RET_HEADS * RET_DK
DIFF_HEADS = 4
DIFF_DH = 64
DIFF_DV = 2 * DIFF_DH
DIFF_WIDTH = DIFF_HEADS * DIFF_DV
DIFF_QK = DIFF_HEADS * 2 * DIFF_DH
MIX_WIDTH = RET_WIDTH + DIFF_WIDTH
IN_WIDTH = 2 * RET_QK + 2 * RET_WIDTH + 2 * DIFF_QK + DIFF_WIDTH
D_FF = 2816
CHUNK = 128
Q_BLOCK = 128
EPS = 1e-5
DEEPNORM_ALPHA = (2 * DEPTH) ** 0.25
DEEPNORM_BETA = (8 * DEPTH) ** -0.25

kernel_name = "hymba_retnet_diffattn_macaron_encoder"


def layer_norm(x, g, b):
    xf = x.astype(jnp.float32)
    mu = jnp.mean(xf, axis=-1, keepdims=True)
    var = jnp.mean(jnp.square(xf - mu), axis=-1, keepdims=True)
    y = (xf - mu) * lax.rsqrt(var + EPS) * g.astype(jnp.float32) + b.astype(jnp.float32)
    return y.astype(x.dtype)


def swiglu_ffn(x, w13, w2):
    a, b = jnp.split(x @ w13, 2, axis=-1)
    return (jax.nn.silu(a) * b) @ w2


def lambda_init(layer):
    return 0.8 - 0.6 * math.exp(-0.3 * layer)


def alibi_slopes():
    return jnp.asarray([2.0 ** (-8.0 * (h + 1) / DIFF_HEADS) for h in range(DIFF_HEADS)], jnp.float32)


def retention_direction(q, k, v, log_gamma, strict):
    B, H, T, dk = q.shape
    dv = v.shape[-1]
    n = T // CHUNK
    dt = q.dtype
    idx = jnp.arange(CHUNK, dtype=jnp.float32)
    dist = idx[:, None] - idx[None, :]
    mask = dist > 0 if strict else dist >= 0
    intra = jnp.where(mask[None], jnp.exp(log_gamma[:, None, None] * jnp.maximum(dist, 0.0)[None]), 0.0)
    q_decay = jnp.exp(log_gamma[:, None] * (idx + 1.0)[None])
    k_decay = jnp.exp(log_gamma[:, None] * (CHUNK - 1.0 - idx)[None])
    chunk_decay = jnp.exp(log_gamma * CHUNK)
    intra_d = intra.astype(dt)
    q_decay_d = q_decay[None, :, :, None].astype(dt)
    k_decay_d = k_decay[None, :, :, None].astype(dt)

    def to_chunks(a):
        return jnp.moveaxis(a.reshape(B, H, n, CHUNK, a.shape[-1]), 2, 0)

    def step(state, xs):
        qi, ki, vi = xs
        s = jnp.einsum('bhid,bhjd->bhij', qi, ki) * intra_d
        inner = jnp.einsum('bhij,bhjv->bhiv', s, vi)
        cross = jnp.einsum('bhid,bhdv->bhiv', qi * q_decay_d, state.astype(dt))
        new_state = state * chunk_decay[None, :, None, None] + jnp.einsum(
            'bhjd,bhjv->bhdv', ki * k_decay_d, vi).astype(jnp.float32)
        return new_state, inner + cross

    state0 = jnp.zeros((B, H, dk, dv), jnp.float32)
    _, out = lax.scan(step, state0, (to_chunks(q), to_chunks(k), to_chunks(v)))
    return jnp.moveaxis(out, 0, 2).reshape(B, H, T, dv)


def bidirectional_retention(q, k, v, decay_f, decay_b):
    lg_f = jax.nn.log_sigmoid(decay_f.astype(jnp.float32))
    lg_b = jax.nn.log_sigmoid(decay_b.astype(jnp.float32))
    fwd = retention_direction(q, k, v, lg_f, strict=False)
    bwd = retention_direction(jnp.flip(q, 2), jnp.flip(k, 2), jnp.flip(v, 2), lg_b, strict=True)
    return fwd + jnp.flip(bwd, 2)


def head_group_norm(o, g, b):
    B, H, T, dv = o.shape
    of = o.astype(jnp.float32)
    mu = jnp.mean(of, axis=-1, keepdims=True)
    var = jnp.mean(jnp.square(of - mu), axis=-1, keepdims=True)
    y = ((of - mu) * lax.rsqrt(var + EPS)).transpose(0, 2, 1, 3).reshape(B, T, H * dv)
    return (y * g.astype(jnp.float32) + b.astype(jnp.float32)).astype(o.dtype)


def differential_attention(q, k, v, lam, slopes):
    B, H, _, T, d = q.shape
    nq = T // Q_BLOCK
    scale = d ** -0.5
    qb = jnp.moveaxis(q.reshape(B, H, 2, nq, Q_BLOCK, d), 3, 0)
    starts = jnp.arange(nq, dtype=jnp.float32) * Q_BLOCK
    kpos = jnp.arange(T, dtype=jnp.float32)

    def block(xs):
        qi, start = xs
        s = jnp.einsum('bhmqd,bhmkd->bhmqk', qi, k).astype(jnp.float32) * scale
        qpos = start + jnp.arange(Q_BLOCK, dtype=jnp.float32)
        bias = -slopes[:, None, None] * jnp.abs(qpos[:, None] - kpos[None, :])[None]
        p = jax.nn.softmax(s + bias[None, :, None], axis=-1)
        a = p[:, :, 0] - lam * p[:, :, 1]
        return jnp.einsum('bhqk,bhkv->bhqv', a.astype(v.dtype), v)

    out = lax.map(block, (qb, starts))
    return jnp.moveaxis(out, 0, 2).reshape(B, H, T, v.shape[-1])


def hybrid_mixer(h, layer, w_in, w_out, decay_f, decay_b, gn_g, gn_b, lq1, lk1, lq2, lk2, subln_g):
    B, T, _ = h.shape
    splits = [RET_QK, 2 * RET_QK, 2 * RET_QK + RET_WIDTH, 2 * RET_QK + 2 * RET_WIDTH,
              2 * RET_QK + 2 * RET_WIDTH + DIFF_QK, 2 * RET_QK + 2 * RET_WIDTH + 2 * DIFF_QK]
    rq, rk, rv, rg, dq, dk, dv = jnp.split(h @ w_in, splits, axis=-1)

    rq = rq.reshape(B, T, RET_HEADS, RET_DK).transpose(0, 2, 1, 3)
    rk = rk.reshape(B, T, RET_HEADS, RET_DK).transpose(0, 2, 1, 3) * (RET_DK ** -0.5)
    rv = rv.reshape(B, T, RET_HEADS, RET_DV).transpose(0, 2, 1, 3)
    ret = bidirectional_retention(rq, rk, rv, decay_f, decay_b)
    ret_out = head_group_norm(ret, gn_g, gn_b) * jax.nn.silu(rg)

    dq = dq.reshape(B, T, DIFF_HEADS, 2, DIFF_DH).transpose(0, 2, 3, 1, 4)
    dk = dk.reshape(B, T, DIFF_HEADS, 2, DIFF_DH).transpose(0, 2, 3, 1, 4)
    dv = dv.reshape(B, T, DIFF_HEADS, DIFF_DV).transpose(0, 2, 1, 3)
    lam0 = lambda_init(layer)
    f32 = jnp.float32
    lam = (jnp.exp(jnp.sum(lq1.astype(f32) * lk1.astype(f32)))
           - jnp.exp(jnp.sum(lq2.astype(f32) * lk2.astype(f32))) + lam0)
    att = differential_attention(dq, dk, dv, lam, alibi_slopes()).astype(f32)
    att = att * lax.rsqrt(jnp.mean(jnp.square(att), axis=-1, keepdims=True) + EPS)
    att = att * subln_g.astype(f32) * (1.0 - lam0)
    diff_out = att.transpose(0, 2, 1, 3).reshape(B, T, DIFF_WIDTH).astype(h.dtype)

    return jnp.concatenate([ret_out, diff_out], axis=-1) @ w_out


def encoder_trunk(x, w_in, w_out, ret_decay_f, ret_decay_b, ret_gn_g, ret_gn_b,
                  diff_lq1, diff_lk1, diff_lq2, diff_lk2, diff_subln_g,
                  ffn1_w13, ffn1_w2, ffn2_w13, ffn2_w2,
                  ln1_g, ln1_b, ln2_g, ln2_b, ln3_g, ln3_b):
    for l in range(DEPTH):
        x = layer_norm(DEEPNORM_ALPHA * x + 0.5 * swiglu_ffn(x, ffn1_w13[l], ffn1_w2[l]), ln1_g[l], ln1_b[l])
        mix = hybrid_mixer(x, l, w_in[l], w_out[l], ret_decay_f[l], ret_decay_b[l], ret_gn_g[l], ret_gn_b[l],
                           diff_lq1[l], diff_lk1[l], diff_lq2[l], diff_lk2[l], diff_subln_g[l])
        x = layer_norm(DEEPNORM_ALPHA * x + mix, ln2_g[l], ln2_b[l])
        x = layer_norm(DEEPNORM_ALPHA * x + 0.5 * swiglu_ffn(x, ffn2_w13[l], ffn2_w2[l]), ln3_g[l], ln3_b[l])
    return x


def setup_inputs(seed: int = 0) -> dict:
    key = jax.random.key(seed)
    ks = jax.random.split(key, 32)
    f32 = jnp.float32
    nrm = lambda k, shape, s: jax.random.normal(k, shape, f32) * s
    col_scale = jnp.concatenate([
        jnp.ones((2 * RET_QK,), f32), jnp.full((RET_WIDTH,), DEEPNORM_BETA, f32),
        jnp.ones((RET_WIDTH + 2 * DIFF_QK,), f32), jnp.full((DIFF_WIDTH,), DEEPNORM_BETA, f32)])
    base_decay = jnp.asarray(np.log(2.0 ** (5.0 + np.arange(RET_HEADS)) - 1.0), f32)
    return {
        "x_prompt": jax.random.normal(ks[0], (BATCH, SEQ, D_MODEL), f32),
        "x_sample": jax.random.normal(ks[1], (DEC_BATCH, DEC_SEQ, D_MODEL), f32),
        "w_in": nrm(ks[2], (DEPTH, D_MODEL, IN_WIDTH), D_MODEL ** -0.5) * col_scale,
        "w_out": nrm(ks[3], (DEPTH, MIX_WIDTH, D_MODEL), DEEPNORM_BETA * MIX_WIDTH ** -0.5),
        "ret_decay_f": base_decay + nrm(ks[4], (DEPTH, RET_HEADS), 0.05),
        "ret_decay_b": base_decay + nrm(ks[5], (DEPTH, RET_HEADS), 0.05),
        "ret_gn_g": 1.0 + nrm(ks[6], (DEPTH, RET_WIDTH), 0.02),
        "ret_gn_b": nrm(ks[7], (DEPTH, RET_WIDTH), 0.02),
        "diff_lq1": nrm(ks[8], (DEPTH, DIFF_DH), 0.1),
        "diff_lk1": nrm(ks[9], (DEPTH, DIFF_DH), 0.1),
        "diff_lq2": nrm(ks[10], (DEPTH, DIFF_DH), 0.1),
        "diff_lk2": nrm(ks[11], (DEPTH, DIFF_DH), 0.1),
        "diff_subln_g": 1.0 + nrm(ks[12], (DEPTH, DIFF_DV), 0.02),
        "ffn1_w13": nrm(ks[13], (DEPTH, D_MODEL, 2 * D_FF), D_MODEL ** -0.5),
        "ffn1_w2": nrm(ks[14], (DEPTH, D_FF, D_MODEL), DEEPNORM_BETA * D_FF ** -0.5),
        "ffn2_w13": nrm(ks[15], (DEPTH, D_MODEL, 2 * D_FF), D_MODEL ** -0.5),
        "ffn2_w2": nrm(ks[16], (DEPTH, D_FF, D_MODEL), DEEPNORM_BETA * D_FF ** -0.5),
        "ln1_g": 1.0 + nrm(ks[17], (DEPTH, D_MODEL), 0.02),
        "ln1_b": nrm(ks[18], (DEPTH, D_MODEL), 0.02),
        "ln2_g": 1.0 + nrm(ks[19], (DEPTH, D_MODEL), 0.02),
        "ln2_b": nrm(ks[20], (DEPTH, D_MODEL), 0.02),
        "ln3_g": 1.0 + nrm(ks[21], (DEPTH, D_MODEL), 0.02),
        "ln3_b": nrm(ks[22], (DEPTH, D_MODEL), 0.02),
    }


def reference(x_prompt, x_sample, w_in, w_out, ret_decay_f, ret_decay_b, ret_gn_g, ret_gn_b,
              diff_lq1, diff_lk1, diff_lq2, diff_lk2, diff_subln_g,
              ffn1_w13, ffn1_w2, ffn2_w13, ffn2_w2,
              ln1_g, ln1_b, ln2_g, ln2_b, ln3_g, ln3_b):
    y_prompt = encoder_trunk(x_prompt, w_in, w_out, ret_decay_f, ret_decay_b, ret_gn_g, ret_gn_b,
                             diff_lq1, diff_lk1, diff_lq2, diff_lk2, diff_subln_g,
                             ffn1_w13, ffn1_w2, ffn2_w13, ffn2_w2,
                             ln1_g, ln1_b, ln2_g, ln2_b, ln3_g, ln3_b)
    y_sample = encoder_trunk(x_sample, w_in, w_out, ret_decay_f, ret_decay_b, ret_gn_g, ret_gn_b,
                             diff_lq1, diff_lk1, diff_lq2, diff_lk2, diff_subln_g,
                             ffn1_w13, ffn1_w2, ffn2_w13, ffn2_w2,
                             ln1_g, ln1_b, ln2_g, ln2_b, ln3_g, ln3_b)
    return (y_prompt, y_sample)
```

```python
import math
from contextlib import ExitStack

import numpy as np
import concourse.bass as bass
import concourse.mybir as mybir
from concourse.bass_utils import run_bass_kernel_spmd

F32 = mybir.dt.float32
BF16 = mybir.dt.bfloat16
U8 = mybir.dt.uint8
AF = mybir.ActivationFunctionType
ALU = mybir.AluOpType
AX = mybir.AxisListType

D = 1024
DFF = 2816
NFF = DFF // 128
INW = 3584
DEPTH = 2
EPS = 1e-5
ALPHA = (2 * DEPTH) ** 0.25
N_CORES = 8
SEQ_LENS = (4096, 2048, 2048, 2048, 2048)
ENGS = ("pe", "act", "dve", "pool", "sp")


def lambda_init(layer):
    return 0.8 - 0.6 * math.exp(-0.3 * layer)


class Op:
    __slots__ = ("eng", "fn", "dma_key", "deps", "inc", "val", "idx", "is_dma", "sset", "kind", "ndma")

    def __init__(self, eng, fn, dma_key):
        self.eng = eng
        self.fn = fn
        self.dma_key = dma_key
        self.is_dma = dma_key is not None
        self.deps = []
        self.inc = False
        self.val = None
        self.sset = 0
        self.kind = "op"
        self.ndma = 1


class Sched:
    def __init__(self):
        self.ops = []
        self.last_w = {}
        self.readers = {}
        self.final_dma = []
        self.sset = 0
        self.last_on_eng = {}
        self.last_dma = {}

    def add(self, eng, fn, reads=(), writes=(), dma_key=None, final=False, ndma=1):
        op = Op(eng, fn, dma_key)
        op.ndma = ndma
        op.idx = len(self.ops)
        op.sset = self.sset
        deps = {}
        for k in reads:
            w = self.last_w.get(k)
            if w is not None:
                deps[w.idx] = (w, True)
        for k in writes:
            w = self.last_w.get(k)
            if w is not None and w.idx not in deps:
                deps[w.idx] = (w, False)
            for r in self.readers.get(k, ()):
                if r.idx not in deps:
                    deps[r.idx] = (r, False)
        op.deps = [deps[i] for i in sorted(deps)]
        for k in reads:
            self.readers.setdefault(k, []).append(op)
        for k in writes:
            self.last_w[k] = op
            self.readers[k] = []
        self.ops.append(op)
        self.last_on_eng[eng] = op
        if op.is_dma:
            self.last_dma[dma_key] = op
        if final:
            self.final_dma.append(op)
        return op

    def barrier(self, new_set=False):
        drains = []
        for e in ENGS:
            op = self.add(e, lambda eng: eng.drain())
            op.kind = "drain"
            drains.append(op)
        dmas = list(self.last_dma.values())
        for e in ENGS:
            op = self.add(e, None)
            op.kind = "join"
            op.deps = [(d, True) for d in drains if d.eng != e] + [(d, True) for d in dmas]
        self.last_w = {}
        self.readers = {}
        self.last_dma = {}
        if new_set:
            self.sset += 1

    @staticmethod
    def _need(op, d, raw):
        if d.is_dma or d.eng != op.eng:
            return True
        if op.eng == "pe":
            return False
        return raw

    def emit(self, nc):
        for op in self.ops:
            for d, raw in op.deps:
                if self._need(op, d, raw):
                    d.inc = True
        for op in self.final_dma:
            op.inc = True
        nsets = self.sset + 1
        cnt = {(e, s): 0 for e in ENGS for s in range(nsets)}
        dcnt = {}
        for op in self.ops:
            if op.is_dma:
                dcnt[op.dma_key] = dcnt.get(op.dma_key, 0) + 16 * op.ndma
                op.val = dcnt[op.dma_key]
            elif op.inc:
                cnt[(op.eng, op.sset)] += 1
                op.val = cnt[(op.eng, op.sset)]
        self.stats = dict(maxcnt=max(cnt.values()), ndma_keys=len(dcnt), nops=len(self.ops),
                          maxd=max(dcnt.values()) if dcnt else 0)
        with ExitStack() as st:
            esem = {k: st.enter_context(nc.semaphore("s_%s%d" % k)) for k in cnt}
            dsem = {k: st.enter_context(nc.semaphore("d%d" % i)) for i, k in enumerate(dcnt)}
            block = st.enter_context(nc.Block())

            def semof(o):
                return dsem[o.dma_key] if o.is_dma else esem[(o.eng, o.sset)]

            def run(engname, e):
                waited = {}
                for op in self.ops:
                    if op.eng != engname:
                        continue
                    need = {}
                    for d, raw in op.deps:
                        if not self._need(op, d, raw):
                            continue
                        s = semof(d)
                        key = id(s)
                        if key not in need or need[key][1] < d.val:
                            need[key] = (s, d.val)
                    for key, (s, v) in need.items():
                        if waited.get(key, 0) >= v:
                            continue
                        waited[key] = v
                        e.wait_ge(s, v)
                    if op.fn is None:
                        continue
                    ins = op.fn(e)
                    if op.is_dma:
                        for i_ in (ins if isinstance(ins, list) else [ins]):
                            i_.then_inc(dsem[op.dma_key], 16)
                    elif op.inc:
                        ins.then_inc(esem[(op.eng, op.sset)], 1)
                if engname == "sp":
                    for k, v in dcnt.items():
                        e.wait_ge(dsem[k], v)

            block.tensor(lambda e: run("pe", e))
            block.scalar(lambda e: run("act", e))
            block.vector(lambda e: run("dve", e))
            block.gpsimd(lambda e: run("pool", e))
            block.sync(lambda e: run("sp", e))


class Arena:
    def __init__(self, handle, nbytes):
        self.h = handle
        self.n = nbytes
        self.off = 0
        self.mark_ = 0

    def alloc(self, free_shape, dt):
        esz = 2 if dt == BF16 else 4
        n = 1
        for s in free_shape:
            n *= s
        nb = (n * esz + 31) // 32 * 32
        assert self.off + nb <= self.n, ("arena overflow", self.off, nb, self.n)
        v = self.h[:, self.off:self.off + n * esz].bitcast(dt)
        self.off += nb
        if len(free_shape) == 2:
            v = v.rearrange("p (a b) -> p a b", a=free_shape[0])
        elif len(free_shape) == 3:
            v = v.rearrange("p (a b c) -> p a b c", a=free_shape[0], b=free_shape[1])
        return v

    def mark(self):
        self.mark_ = self.off

    def reset(self):
        self.off = self.mark_


class Builder:
    def __init__(self, seq_lens, depth=DEPTH, do_mixer=True, do_ffn2=True, mix_stage=3):
        self.mix_stage = mix_stage
        self.seq_lens = tuple(seq_lens)
        self.NT = sum(seq_lens)
        self.depth = depth
        self.do_mixer = do_mixer
        self.do_ffn2 = do_ffn2
        self.nc = bass.Bass("TRN2", target_bir_lowering=False)
        self.S = Sched()

    def dram_in(self, name, shape, dt=F32):
        return self.nc.dram_tensor(name, list(shape), dt, kind="ExternalInput").ap()

    def build(self):
        nc, S = self.nc, self.S
        NT = self.NT
        self.x = self.dram_in("x", [NT, D])
        self.y = nc.dram_tensor("y", [NT, D], F32, kind="ExternalOutput").ap()
        self.w_in = self.dram_in("w_in", [DEPTH, D, INW])
        self.w_out = self.dram_in("w_out", [DEPTH, D, D])
        self.dec_f = self.dram_in("ret_decay_f", [DEPTH, 8])
        self.dec_b = self.dram_in("ret_decay_b", [DEPTH, 8])
        self.gn_g = self.dram_in("ret_gn_g", [DEPTH, 512])
        self.gn_b = self.dram_in("ret_gn_b", [DEPTH, 512])
        self.lq1 = self.dram_in("diff_lq1", [DEPTH, 64])
        self.lk1 = self.dram_in("diff_lk1", [DEPTH, 64])
        self.lq2 = self.dram_in("diff_lq2", [DEPTH, 64])
        self.lk2 = self.dram_in("diff_lk2", [DEPTH, 64])
        self.subln = self.dram_in("diff_subln_g", [DEPTH, 128])
        self.dec_all = self.dram_in("dec_all", [DEPTH, 16])
        self.dec_pairs = self.dram_in("dec_pairs", [DEPTH, 2, 8])
        self.f1w13 = self.dram_in("ffn1_w13", [DEPTH, D, 2 * DFF])
        self.f1w2 = self.dram_in("ffn1_w2", [DEPTH, DFF, D])
        self.f2w13 = self.dram_in("ffn2_w13", [DEPTH, D, 2 * DFF])
        self.f2w2 = self.dram_in("ffn2_w2", [DEPTH, DFF, D])
        self.ln = {}
        for i in (1, 2, 3):
            self.ln[(i, "g")] = self.dram_in("ln%d_g" % i, [DEPTH, D])
            self.ln[(i, "b")] = self.dram_in("ln%d_b" % i, [DEPTH, D])
        self.XA = nc.dram_tensor("xa_scr", [NT, D], F32, kind="Internal").ap()
        self.XB = nc.dram_tensor("xb_scr", [NT, D], F32, kind="Internal").ap()
        self.MO = nc.dram_tensor("mo_scr", [8, 128, NT], BF16, kind="Internal").ap()

        with ExitStack() as st:
            nbytes = 206000
            arena_h = st.enter_context(nc.sbuf_tensor("arena", [128, nbytes], U8))
            self.A = Arena(arena_h, nbytes)
            self.ps = st.enter_context(nc.psum_tensor("ps", [128, 8, 512], F32))
            self.psb = self.ps[:].bitcast(BF16)
            self.setup_consts()
            self.A.mark()
            cur = self.x
            for l in range(self.depth):
                last = (l == self.depth - 1)
                self.phase_ffn(l, self.f1w13, self.f1w2, self.ln[(1, "g")], self.ln[(1, "b")], cur, self.XA)
                cur = self.XA
                if self.do_mixer:
                    self.phase_mixer(l, self.XA, self.XB)
                    cur = self.XB
                if self.do_ffn2:
                    dst = self.y if last else self.XA
                    src = cur
                    if src is self.XA:
                        dst = self.y if last else self.XB
                    self.phase_ffn(l, self.f2w13, self.f2w2, self.ln[(3, "g")], self.ln[(3, "b")], src, dst)
                    cur = dst
            if cur is not self.y:
                self.copy_out(cur)
            S.emit(nc)
        return nc

    def setup_consts(self):
        S, A = self.S, self.A
        self.ident = A.alloc([128], BF16)
        identf = A.alloc([128], F32)
        self.ones_bf = A.alloc([128], BF16)
        self.ones_f = A.alloc([128], F32)
        self.eps_c = A.alloc([1], F32)
        S.add("pool", lambda e: e.memset(identf, 0.0), writes=["identf"])
        S.add("pool", lambda e: e.affine_select(out=identf, in_=identf, pattern=[[-1, 128]],
                                                compare_op=ALU.not_equal, fill=1.0, base=0,
                                                channel_multiplier=1), reads=["identf"], writes=["identf"])
        S.add("pool", lambda e: e.tensor_copy(out=self.ident, in_=identf), reads=["identf"], writes=["ident"])
        S.add("pool", lambda e: e.memset(self.ones_bf, 1.0), writes=["ones_bf"])
        S.add("pool", lambda e: e.memset(self.ones_f, 1.0), writes=["ones_f"])
        S.add("pool", lambda e: e.memset(self.eps_c, EPS), writes=["eps_c"])
        S.barrier()

    def copy_out(self, cur):
        S, A = self.S, self.A
        A.reset()
        buf = [A.alloc([D], F32) for _ in range(2)]
        for blk in range(self.NT // 128):
            s = blk % 2
            rows = slice(blk * 128, (blk + 1) * 128)
            S.add("sp", lambda e, s=s, rows=rows: e.dma_start(out=buf[s], in_=cur[rows, :]),
                  writes=[("cb", s)], dma_key=("cb", s))
            S.add("sp", lambda e, s=s, rows=rows: e.dma_start(out=self.y[rows, :], in_=buf[s]),
                  reads=[("cb", s)], dma_key=("co", s), final=True)
        S.barrier()

    def ln_block(self, ysb, slot, lng, lnb, scr, pre):
        S = self.S
        st, mv, rs = scr["st"], scr["mv"], scr["rs"]
        ky = (pre + "y", slot)
        kst, kmv, krs = (pre + "st", slot), (pre + "mv", slot), (pre + "rs", slot)
        S.add("dve", lambda e: e.bn_stats(out=st[:, 0, :], in_=ysb[:, 0:512]), reads=[ky], writes=[(kst, 0)])
        S.add("dve", lambda e: e.bn_stats(out=st[:, 1, :], in_=ysb[:, 512:1024]), reads=[ky], writes=[(kst, 1)])
        S.add("dve", lambda e: e.bn_aggr(out=mv, in_=st), reads=[(kst, 0), (kst, 1)], writes=[kmv])
        S.add("dve", lambda e: e.tensor_scalar(out=rs, in0=mv[:, 1:2], scalar1=EPS, scalar2=None, op0=ALU.add),
              reads=[kmv], writes=[krs])
        S.add("act", lambda e: e.activation(out=rs, in_=rs, func=AF.Sqrt), reads=[krs], writes=[krs])
        S.add("dve", lambda e: e.reciprocal(out=rs, in_=rs), reads=[krs], writes=[krs])
        S.add("dve", lambda e: e.tensor_scalar(out=ysb, in0=ysb, scalar1=mv[:, 0:1], scalar2=rs,
                                               op0=ALU.subtract, op1=ALU.mult),
              reads=[ky, kmv, krs], writes=[ky])
        S.add("pool", lambda e: e.tensor_tensor(out=ysb, in0=ysb, in1=lng, op=ALU.mult),
              reads=[ky, "lng"], writes=[ky])
        S.add("pool", lambda e: e.tensor_tensor(out=ysb, in0=ysb, in1=lnb, op=ALU.add),
              reads=[ky, "lnb"], writes=[ky])

    def phase_ffn(self, l, w13_d, w2_d, g_d, b_d, src, dst):
        S, A, ps, psb = self.S, self.A, self.ps, self.psb
        A.reset()
        W13 = A.alloc([8, 2 * DFF], BF16)
        W2 = A.alloc([NFF, D], BF16)
        lng = A.alloc([D], F32)
        lnb = A.alloc([D], F32)
        xres = [A.alloc([D], F32) for _ in range(2)]
        xbf = [A.alloc([D], BF16) for _ in range(4)]
        xT = A.alloc([8, 512], BF16)
        gT = A.alloc([NFF, 512], BF16)
        sa = [A.alloc([512], F32) for _ in range(2)]
        yb = [A.alloc([D], F32) for _ in range(2)]
        scr = [dict(st=A.alloc([2, 6], F32), mv=A.alloc([2], F32), rs=A.alloc([1], F32)) for _ in range(2)]
        ident = self.ident
        final = dst is self.y

        for j in range(11):
            cs = slice(j * 512, (j + 1) * 512)
            S.add("pool", lambda e, cs=cs: e.dma_start(
                out=W13[:, :, cs], in_=w13_d[l, :, cs].rearrange("(k p) n -> p k n", p=128)),
                writes=[("W13", j)], dma_key=("W13", j))
        for j in range(11):
            S.add("pool", lambda e, j=j: e.dma_start(
                out=W2[:, 2 * j:2 * j + 2, :],
                in_=w2_d[l, 256 * j:256 * j + 256, :].rearrange("(k p) n -> p k n", p=128)),
                writes=[("W2", j)], dma_key=("W2", j))
        S.add("sp", lambda e: e.dma_start(out=lng, in_=g_d[l:l + 1, :].broadcast_to([128, D])),
              writes=["lng"], dma_key="lng")
        S.add("sp", lambda e: e.dma_start(out=lnb, in_=b_d[l:l + 1, :].broadcast_to([128, D])),
              writes=["lnb"], dma_key="lnb")

        ntile = self.NT // 512

        def load_tile(t):
            for b in range(4):
                rows = slice((4 * t + b) * 128, (4 * t + b + 1) * 128)
                S.add("pool", lambda e, b=b, rows=rows: e.dma_start(out=xbf[b], in_=src[rows, :]),
                      writes=[("xbf", b)], dma_key=("xbf", b))

        def transposes(t):
            for k in range(8):
                bank = 6 + (k % 2)

                def tr(e, k=k, bank=bank):
                    for b in range(4):
                        ins = e.transpose(out=psb[:, bank, b * 128:(b + 1) * 128],
                                          in_=xbf[b][:, k * 128:(k + 1) * 128], identity=ident)
                    return ins
                S.add("pe", tr, reads=[("xbf", b) for b in range(4)] + ["ident"], writes=[("ps", bank)])
                if k % 2 == 0:
                    S.add("act", lambda e, k=k, bank=bank: e.activation(out=xT[:, k, :], in_=psb[:, bank, 0:512],
                                                                       func=AF.Copy),
                          reads=[("ps", bank)], writes=[("xT", k)])
                else:
                    S.add("dve", lambda e, k=k, bank=bank: e.tensor_copy(out=xT[:, k, :], in_=psb[:, bank, 0:512]),
                          reads=[("ps", bank)], writes=[("xT", k)])

        def ffn1(t):
            for c in range(NFF):
                q = c % 2
                ba, bb = 2 * q, 2 * q + 1

                def mm(e, c=c, ba=ba, bb=bb):
                    for k in range(8):
                        e.matmul(ps[:, ba, :], lhsT=W13[:, k, c * 128:(c + 1) * 128], rhs=xT[:, k, :],
                                 start=(k == 0), stop=(k == 7))
                    for k in range(8):
                        ins = e.matmul(ps[:, bb, :], lhsT=W13[:, k, DFF + c * 128:DFF + (c + 1) * 128],
                                       rhs=xT[:, k, :], start=(k == 0), stop=(k == 7))
                    return ins
                S.add("pe", mm, reads=[("xT", k) for k in range(8)] + [("W13", c // 4), ("W13", (NFF + c) // 4)],
                      writes=[("ps", ba), ("ps", bb)])
                S.add("act", lambda e, q=q, ba=ba: e.activation(out=sa[q], in_=ps[:, ba, :], func=AF.Silu),
                      reads=[("ps", ba)], writes=[("sa", q)])
                S.add("dve", lambda e, c=c, q=q, bb=bb: e.tensor_tensor(out=gT[:, c, :], in0=sa[q], in1=ps[:, bb, :],
                                                                       op=ALU.mult),
                      reads=[("sa", q), ("ps", bb)], writes=[("gT", c)])

        def ffn2(t):
            for b in range(4):
                blk = 4 * t + b
                slot = blk % 2
                rows = slice(blk * 128, (blk + 1) * 128)
                b0 = 4 if b % 2 == 0 else 6
                S.add("sp", lambda e, slot=slot, rows=rows: e.dma_start(out=xres[slot], in_=src[rows, :]),
                      writes=[("xres", slot)], dma_key=("xres", slot))

                def mm(e, b=b, b0=b0):
                    for half in range(2):
                        for c in range(NFF):
                            ins = e.matmul(ps[:, b0 + half, :], lhsT=gT[:, c, b * 128:(b + 1) * 128],
                                           rhs=W2[:, c, half * 512:(half + 1) * 512],
                                           start=(c == 0), stop=(c == NFF - 1))
                    return ins
                S.add("pe", mm, reads=[("gT", c) for c in range(NFF)] + [("W2", j) for j in range(11)],
                      writes=[("ps", b0), ("ps", b0 + 1)])
                ysb = yb[slot]
                pso = ps[:, b0:b0 + 2, :].rearrange("p a b -> p (a b)")
                S.add("act", lambda e, ysb=ysb, pso=pso: e.activation(out=ysb, in_=pso, func=AF.Identity, scale=0.5),
                      reads=[("ps", b0), ("ps", b0 + 1)], writes=[("fy", slot)])
                S.add("dve", lambda e, ysb=ysb, slot=slot: e.scalar_tensor_tensor(
                    out=ysb, in0=xres[slot], scalar=ALPHA, in1=ysb, op0=ALU.mult, op1=ALU.add),
                    reads=[("fy", slot), ("xres", slot)], writes=[("fy", slot)])
                self.ln_block(ysb, slot, lng, lnb, scr[slot], "f")
                S.add("pool", lambda e, ysb=ysb, rows=rows: e.dma_start(out=dst[rows, :], in_=ysb),
                      reads=[("fy", slot)], dma_key=("yst", slot), final=final)

        load_tile(0)
        transposes(0)
        for t in range(ntile):
            if t + 1 < ntile:
                load_tile(t + 1)
            ffn1(t)
            if t + 1 < ntile:
                transposes(t + 1)
            ffn2(t)
        S.barrier(new_set=True)

    def phase_mixer(self, l, src, dst):
        S, A, ps, psb = self.S, self.A, self.ps, self.psb
        A.reset()
        ident = self.ident
        TM = max(self.seq_lens)
        NBM = TM // 128
        I32 = mybir.dt.int32
        lam0 = lambda_init(l)
        SCALE = 64 ** -0.5
        hT = A.alloc([8, TM], BF16)
        wout = A.alloc([8, D], BF16)
        lng = A.alloc([D], F32)
        lnb = A.alloc([D], F32)
        DT = A.alloc([8, 128], F32)
        QF = A.alloc([4, 128], F32)
        QB = A.alloc([4, 128], F32)
        Eq = A.alloc([512], F32)
        Dg = A.alloc([4, 512], F32)
        gng = A.alloc([512], F32)
        gnb = A.alloc([512], F32)
        lgall = A.alloc([16], F32)
        lgp = A.alloc([8], F32)
        KFB = A.alloc([16], F32)
        GC = A.alloc([8], F32)
        c127 = A.alloc([1], F32)
        cp = A.alloc([1], F32)
        neglam = A.alloc([1], F32)
        sg = A.alloc([1], F32)
        lqk = A.alloc([4, 64], F32)
        lsum = A.alloc([2], F32)
        tI = A.alloc([512], I32)
        tE = A.alloc([128], F32)
        tP = A.alloc([128], F32)
        tN = A.alloc([128], F32)
        tM = A.alloc([128], F32)
        tI1 = A.alloc([128], F32)

        S.add("pool", lambda e: e.dma_start(out=wout, in_=self.w_out[l].rearrange("(k p) n -> p k n", p=128)),
              writes=["wout"], dma_key="wout")
        S.add("sp", lambda e: e.dma_start(out=lng, in_=self.ln[(2, "g")][l:l + 1, :].broadcast_to([128, D])),
              writes=["lng"], dma_key="lng")
        S.add("sp", lambda e: e.dma_start(out=lnb, in_=self.ln[(2, "b")][l:l + 1, :].broadcast_to([128, D])),
              writes=["lnb"], dma_key="lnb")
        S.add("sp", lambda e: e.dma_start(out=gng, in_=self.gn_g[l:l + 1, :].broadcast_to([128, 512])),
              writes=["gng"], dma_key="gng")
        S.add("sp", lambda e: e.dma_start(out=gnb, in_=self.gn_b[l:l + 1, :].broadcast_to([128, 512])),
              writes=["gnb"], dma_key="gnb")
        S.add("sp", lambda e: e.dma_start(out=lgall, in_=self.dec_all[l:l + 1, :].broadcast_to([128, 16])),
              writes=["lgall"], dma_key="lgall")
        for half in range(2):
            S.add("sp", lambda e, half=half: e.dma_start(
                out=lgp[64 * half:64 * half + 64, :], in_=self.dec_pairs[l, half:half + 1, :].broadcast_to([64, 8])),
                writes=[("lgp", half)], dma_key=("lgp", half))
        for i, t in enumerate((self.lq1, self.lk1, self.lq2, self.lk2)):
            S.add("sp", lambda e, i=i, t=t: e.dma_start(out=lqk[:, i, :], in_=t[l:l + 1, :].broadcast_to([128, 64])),
                  writes=[("lqk", i)], dma_key=("lqk", i))
        S.add("sp", lambda e: e.dma_start(out=sg, in_=self.subln[l, :].rearrange("(p o) -> p o", o=1)),
              writes=["sg"], dma_key="sg")
        S.barrier()
        for t_, nm in ((lgall, "lgall"), (lgp, "lgp")):
            S.add("act", lambda e, t_=t_: e.activation(out=t_, in_=t_, func=AF.Exp, scale=-1.0), reads=[nm], writes=[nm])
            S.add("act", lambda e, t_=t_: e.activation(out=t_, in_=t_, func=AF.Ln, bias=self.ones_f[:, 0:1]),
                  reads=[nm], writes=[nm])
            S.add("pool", lambda e, t_=t_: e.tensor_scalar(out=t_, in0=t_, scalar1=-1.0, scalar2=None, op0=ALU.mult),
                  reads=[nm], writes=[nm])
        S.add("pool", lambda e: e.iota(out=tI, pattern=[[1, 512]], base=0, channel_multiplier=-1), writes=["tI"])
        S.add("dve", lambda e: e.tensor_copy(out=Eq, in_=tI), reads=["tI"], writes=["Eq"])
        S.add("dve", lambda e: e.tensor_copy(out=tE, in_=Eq[:, 0:128]), reads=["Eq"], writes=["tE"])
        S.add("dve", lambda e: e.tensor_scalar(out=tP, in0=tE, scalar1=0.0, scalar2=None, op0=ALU.max),
              reads=["tE"], writes=["tP"])
        S.add("dve", lambda e: e.tensor_tensor(out=tN, in0=tP, in1=tE, op=ALU.subtract), reads=["tP", "tE"], writes=["tN"])
        for d_ in range(4):
            S.add("act", lambda e, d_=d_: e.activation(out=Dg[:, d_, :], in_=Eq, func=AF.Abs, bias=float(-128 * d_)),
                  reads=["Eq"], writes=[("Dg", d_)])
        S.add("dve", lambda e: e.tensor_copy(out=c127, in_=Eq[:, 127:128]), reads=["Eq"], writes=["c127"])
        S.add("dve", lambda e: e.tensor_scalar(out=cp, in0=Eq[:, 0:1], scalar1=-1.0, scalar2=None, op0=ALU.mult),
              reads=["Eq"], writes=["cp"])
        S.add("dve", lambda e: e.tensor_scalar(out=tI1, in0=Eq[:, 0:128], scalar1=cp, scalar2=1.0, op0=ALU.add, op1=ALU.add),
              reads=["Eq", "cp"], writes=["tI1"])
        S.barrier()
        for h in range(8):
            S.add("act", lambda e, h=h: e.activation(out=tM, in_=tP, func=AF.Exp, scale=lgall[:, h:h + 1]),
                  reads=["tM"], writes=["tM"])
            S.add("pool", lambda e, h=h: e.affine_select(out=DT[:, h, :], in_=tM, pattern=[[1, 128]], compare_op=ALU.is_ge,
                                                         fill=0.0, base=0, channel_multiplier=-1),
                  reads=["tM"], writes=[("DT", h)])
            S.add("act", lambda e, h=h: e.activation(out=tM, in_=tN, func=AF.Exp, scale=lgall[:, 8 + h:9 + h]),
                  reads=["tM"], writes=["tM"])
            S.add("pool", lambda e, h=h: e.affine_select(out=tM, in_=tM, pattern=[[-1, 128]], compare_op=ALU.is_gt,
                                                         fill=0.0, base=0, channel_multiplier=1),
                  reads=["tM"], writes=["tM"])
            S.add("pool", lambda e, h=h: e.tensor_tensor(out=DT[:, h, :], in0=DT[:, h, :], in1=tM, op=ALU.add),
                  reads=["tM", ("DT", h)], writes=[("DT", h)])
        for p in range(4):
            S.add("act", lambda e, p=p: e.activation(out=QF[:, p, :], in_=tI1, func=AF.Exp, scale=lgp[:, p:p + 1]),
                  writes=[("QF", p)])
            S.add("dve", lambda e, p=p: e.tensor_scalar(out=QB[:, p, :], in0=tI1, scalar1=-1.0, scalar2=129.0,
                                                       op0=ALU.mult, op1=ALU.add), writes=[("QB", p)])
            S.add("act", lambda e, p=p: e.activation(out=QB[:, p, :], in_=QB[:, p, :], func=AF.Exp,
                                                     scale=lgp[:, 4 + p:5 + p]), reads=[("QB", p)], writes=[("QB", p)])
        S.add("act", lambda e: e.activation(out=KFB[:, 0:8], in_=lgall[:, 0:8], func=AF.Exp, scale=c127), writes=["KF"])
        S.add("act", lambda e: e.activation(out=KFB[:, 8:16], in_=lgall[:, 8:16], func=AF.Exp, scale=cp), writes=["KB"])
        S.add("act", lambda e: e.activation(out=GC, in_=lgp, func=AF.Exp, scale=128.0), writes=["GC"])
        S.add("dve", lambda e: e.tensor_tensor(out=lqk[:, 0, :], in0=lqk[:, 0, :], in1=lqk[:, 1, :], op=ALU.mult), writes=["lq0"])
        S.add("dve", lambda e: e.tensor_tensor(out=lqk[:, 2, :], in0=lqk[:, 2, :], in1=lqk[:, 3, :], op=ALU.mult), writes=["lq2"])
        S.add("dve", lambda e: e.reduce_sum(out=lsum[:, 0:1], in_=lqk[:, 0, :], axis=AX.X), reads=["lq0"], writes=["ls0"])
        S.add("dve", lambda e: e.reduce_sum(out=lsum[:, 1:2], in_=lqk[:, 2, :], axis=AX.X), reads=["lq2"], writes=["ls1"])
        S.add("act", lambda e: e.activation(out=lsum, in_=lsum, func=AF.Exp), reads=["ls0", "ls1"], writes=["lse"])
        S.add("dve", lambda e: e.tensor_tensor(out=neglam, in0=lsum[:, 1:2], in1=lsum[:, 0:1], op=ALU.subtract),
              reads=["lse"], writes=["nl"])
        S.add("dve", lambda e: e.tensor_scalar(out=neglam, in0=neglam, scalar1=-lam0, scalar2=None, op0=ALU.add),
              reads=["nl"], writes=["nl"])
        S.add("pool", lambda e: e.tensor_scalar(out=sg, in0=sg, scalar1=1.0 - lam0, scalar2=None, op0=ALU.mult), writes=["sg2"])
        S.barrier()
        A2 = A.off

        slopes = [2.0 ** (-8.0 * (h + 1) / 4) for h in range(4)]
        def do_b1(T, tok0):
            nblk = T // 128
            ntile = T // 512
            A.off = A2
            xbf = [A.alloc([D], BF16) for _ in range(8)]
            for t in range(ntile):
                for b in range(4):
                    sl = (t % 2) * 4 + b
                    rows = slice(tok0 + (4 * t + b) * 128, tok0 + (4 * t + b + 1) * 128)
                    S.add("pool", lambda e, sl=sl, rows=rows: e.dma_start(out=xbf[sl], in_=src[rows, :]),
                          writes=[("xbf", sl)], dma_key=("xbf", sl))
                for k in range(8):
                    bank = 6 + (k % 2)

                    def tr(e, k=k, bank=bank, t=t):
                        for b in range(4):
                            ins = e.transpose(out=psb[:, bank, b * 128:(b + 1) * 128],
                                              in_=xbf[(t % 2) * 4 + b][:, k * 128:(k + 1) * 128], identity=ident)
                        return ins
                    S.add("pe", tr, reads=[("xbf", (t % 2) * 4 + b) for b in range(4)], writes=[("ps", bank)])
                    dsl = hT[:, k, t * 512:(t + 1) * 512]
                    if k % 2 == 0:
                        S.add("act", lambda e, dsl=dsl, bank=bank: e.activation(out=dsl, in_=psb[:, bank, 0:512], func=AF.Copy),
                              reads=[("ps", bank)], writes=[("hT", t)])
                    else:
                        S.add("dve", lambda e, dsl=dsl, bank=bank: e.tensor_copy(out=dsl, in_=psb[:, bank, 0:512]),
                              reads=[("ps", bank)], writes=[("hT", t)])
            S.barrier()

        def do_ret(p, T, tok0):
            nblk = T // 128
            ntile = T // 512
            if True:
                A.off = A2
                wr = A.alloc([8, 512], BF16)
                rqT = A.alloc([T], BF16)
                rkTa = A.alloc([T], BF16)
                rkTb = A.alloc([T], BF16)
                rk_tm = A.alloc([nblk, 128], BF16)
                rv_tm = A.alloc([nblk, 128], BF16)
                gate = A.alloc([nblk, 128], BF16)
                Sb16a = A.alloc([nblk, 64], BF16)
                Sb16b = A.alloc([nblk, 64], BF16)
                moT = A.alloc([T], BF16)
                Sf32 = A.alloc([64], F32)
                Sb32 = A.alloc([64], F32)
                Sf16a = [A.alloc([64], BF16) for _ in range(2)]
                Sf16b = [A.alloc([64], BF16) for _ in range(2)]
                qf = [A.alloc([128], BF16) for _ in range(2)]
                qb = [A.alloc([128], BF16) for _ in range(2)]
                kd = [A.alloc([128], BF16) for _ in range(2)]
                AT = [A.alloc([256], BF16) for _ in range(2)]
                ob = [A.alloc([128], F32) for _ in range(2)]
                rob = [A.alloc([128], BF16) for _ in range(2)]
                gst = [A.alloc([2, 6], F32) for _ in range(2)]
                gmv = [A.alloc([2, 2], F32) for _ in range(2)]
                grs = [A.alloc([2], F32) for _ in range(2)]
                for j, c0 in enumerate((128 * p, 512 + 128 * p, 1024 + 128 * p, 1536 + 128 * p)):
                    S.add("pool", lambda e, j=j, c0=c0: [e.dma_start(
                        out=wr[:, k, 128 * j:128 * j + 128],
                        in_=self.w_in[l, k * 128:(k + 1) * 128, c0:c0 + 128]) for k in range(8)],
                        writes=[("wr", j)], dma_key=("wsl", j), ndma=8)
                for t in range(ntile):
                    for j, (dstT, nm, sc) in enumerate(((rqT, "rqT", 1.0), (rkTa, "rkT", 0.125))):
                        bank = (2 * t + j) % 4

                        def mm(e, t=t, j=j, bank=bank):
                            for k in range(8):
                                ins = e.matmul(ps[:, bank, :], lhsT=wr[:, k, 128 * j:128 * j + 128],
                                               rhs=hT[:, k, t * 512:(t + 1) * 512], start=(k == 0), stop=(k == 7))
                            return ins
                        S.add("pe", mm, reads=[("hT", t), ("wr", j)], writes=[("ps", bank)])
                        S.add("act", lambda e, dstT=dstT, t=t, bank=bank, sc=sc: e.activation(
                            out=dstT[:, t * 512:(t + 1) * 512], in_=ps[:, bank, :], func=AF.Identity, scale=sc),
                            reads=[("ps", bank)], writes=[(nm, t)])
                        if j == 1:
                            S.add("act", lambda e, t=t, bank=bank, sc=sc: e.activation(
                                out=rkTb[:, t * 512:(t + 1) * 512], in_=ps[:, bank, :], func=AF.Identity, scale=sc),
                                reads=[("ps", bank)], writes=[("rkTb", t)])
                            S.add("dve", lambda e, t=t: e.memset(rkTa[64:128, t * 512:(t + 1) * 512], 0.0),
                                  reads=[("rkT", t)], writes=[("rkT", t)])
                            S.add("dve", lambda e, t=t: e.memset(rkTb[0:64, t * 512:(t + 1) * 512], 0.0),
                                  reads=[("rkTb", t)], writes=[("rkTb", t)])
                for blk in range(nblk):
                    bank = 4 + blk % 4

                    def mm(e, blk=blk, bank=bank):
                        for k in range(8):
                            ins = e.matmul(ps[:, bank, 0:384], lhsT=hT[:, k, blk * 128:(blk + 1) * 128],
                                           rhs=wr[:, k, 128:512], start=(k == 0), stop=(k == 7))
                        return ins
                    S.add("pe", mm, reads=[("hT", blk // 4), ("wr", 1), ("wr", 2), ("wr", 3)], writes=[("ps", bank)])
                    S.add("act", lambda e, blk=blk, bank=bank: e.activation(
                        out=rk_tm[:, blk, :], in_=ps[:, bank, 0:128], func=AF.Identity, scale=0.125),
                        reads=[("ps", bank)], writes=[("rk_tm", blk)])
                    S.add("act", lambda e, blk=blk, bank=bank: e.activation(out=rv_tm[:, blk, :], in_=ps[:, bank, 128:256],
                                                                           func=AF.Copy),
                          reads=[("ps", bank)], writes=[("rv_tm", blk)])
                    S.add("act", lambda e, blk=blk, bank=bank: e.activation(out=gate[:, blk, :], in_=ps[:, bank, 256:384],
                                                                           func=AF.Silu),
                          reads=[("ps", bank)], writes=[("gate", blk)])
                S.add("pool", lambda e: e.memset(Sb32, 0.0), writes=["Sb32"])
                S.add("pool", lambda e: e.memset(Sf32, 0.0), writes=["Sf32"])
                S.add("pool", lambda e: e.memset(Sb16a, 0.0), writes=["Sb16a"])
                S.add("pool", lambda e: e.memset(Sb16b, 0.0), writes=["Sb16b"])
                for s_ in range(2):
                    S.add("pool", lambda e, s_=s_: e.memset(Sf16a[s_], 0.0), writes=[("Sf16", s_)])
                    S.add("pool", lambda e, s_=s_: e.memset(Sf16b[s_], 0.0), writes=[("Sf16", s_)])

                def state_update(c, S32, nm, kcol, gcol, ubank):
                    s = c % 2
                    for hh in range(2):
                        S.add("dve", lambda e, s=s, hh=hh, c=c: e.tensor_scalar(
                            out=kd[s][:, 64 * hh:64 * hh + 64], in0=rk_tm[:, c, 64 * hh:64 * hh + 64],
                            scalar1=KFB[:, kcol + 2 * p + hh:kcol + 2 * p + hh + 1], scalar2=None, op0=ALU.mult),
                            reads=[("rk_tm", c)], writes=[("kd", s, hh)])
                    S.add("pe", lambda e, s=s, c=c: e.matmul(ps[:, ubank, 0:128], lhsT=kd[s], rhs=rv_tm[:, c, :],
                                                              start=True, stop=True),
                          reads=[("kd", s, 0), ("kd", s, 1), ("rv_tm", c)], writes=[("ps", ubank)])
                    for hh in range(2):
                        rs_ = slice(64 * hh, 64 * hh + 64)
                        S.add("dve", lambda e, rs_=rs_, hh=hh: e.scalar_tensor_tensor(
                            out=S32[rs_, :], in0=S32[rs_, :], scalar=GC[rs_, gcol + p:gcol + p + 1],
                            in1=ps[rs_, ubank, 64 * hh:64 * hh + 64], op0=ALU.mult, op1=ALU.add),
                            reads=[("ps", ubank), (nm, hh)], writes=[(nm, hh)])

                import os as _os2
                RL = int(_os2.environ.get("RET_LEVEL", "2"))
                for c in (range(nblk - 1, -1, -1) if RL >= 1 else []):
                    S.add("dve", lambda e, c=c: e.tensor_copy(out=Sb16a[0:64, c, :], in_=Sb32[0:64, :]),
                          reads=[("Sb32", 0), ("Sb32", 1), "Sb32", "Sb16a"], writes=[("Sb16", c)])
                    S.add("dve", lambda e, c=c: e.tensor_copy(out=Sb16b[64:128, c, :], in_=Sb32[64:128, :]),
                          reads=[("Sb32", 0), ("Sb32", 1), "Sb32", "Sb16b"], writes=[("Sb16", c)])
                    if c > 0:
                        state_update(c, Sb32, "Sb32", 8, 4, 4 + c % 2)
                def stage1(c):
                    s = c % 2
                    cs = slice(c * 128, (c + 1) * 128)
                    S.add("dve", lambda e, s=s: e.tensor_copy(out=Sf16a[s][0:64, :], in_=Sf32[0:64, :]),
                          reads=[("Sf32", 0), ("Sf32", 1), "Sf32"], writes=[("Sf16", s)])
                    S.add("dve", lambda e, s=s: e.tensor_copy(out=Sf16b[s][64:128, :], in_=Sf32[64:128, :]),
                          reads=[("Sf32", 0), ("Sf32", 1), "Sf32"], writes=[("Sf16", s)])
                    S.add("pool", lambda e, s=s, cs=cs: e.tensor_tensor(out=qf[s], in0=rqT[:, cs], in1=QF[:, p, :], op=ALU.mult),
                          reads=[("rqT", c // 4)], writes=[("qf", s)])
                    S.add("pool", lambda e, s=s, cs=cs: e.tensor_tensor(out=qb[s], in0=rqT[:, cs], in1=QB[:, p, :], op=ALU.mult),
                          reads=[("rqT", c // 4)], writes=[("qb", s)])
                    sbank = s

                    def mmS(e, cs=cs, sbank=sbank):
                        e.matmul(ps[:, sbank, 0:128], lhsT=rkTa[:, cs], rhs=rqT[:, cs], start=True, stop=True)
                        return e.matmul(ps[:, sbank, 128:256], lhsT=rkTb[:, cs], rhs=rqT[:, cs], start=True, stop=True)
                    S.add("pe", mmS, reads=[("rqT", c // 4), ("rkT", c // 4), ("rkTb", c // 4)], writes=[("ps", sbank)])
                    S.add("dve", lambda e, s=s, sbank=sbank: e.tensor_tensor(
                        out=AT[s], in0=ps[:, sbank, 0:256], in1=DT[:, 2 * p:2 * p + 2, :].rearrange("p a b -> p (a b)"),
                        op=ALU.mult), reads=[("ps", sbank)], writes=[("AT", s)])
                    obank = 2 + s

                    def mmO(e, s=s, c=c, cs=cs, obank=obank):
                        for hh in range(2):
                            rs_ = slice(64 * hh, 64 * hh + 64)
                            o_ = ps[:, obank, 64 * hh:64 * hh + 64]
                            e.matmul(o_, lhsT=AT[s][:, 128 * hh:128 * hh + 128], rhs=rv_tm[:, c, 64 * hh:64 * hh + 64],
                                     start=True, stop=False)
                            e.matmul(o_, lhsT=qf[s], rhs=(Sf16a if hh == 0 else Sf16b)[s], start=False, stop=False)
                            ins = e.matmul(o_, lhsT=qb[s], rhs=(Sb16a if hh == 0 else Sb16b)[:, c, :], start=False, stop=True)
                        return ins
                    S.add("pe", mmO, reads=[("AT", s), ("rv_tm", c), ("qf", s), ("qb", s), ("Sf16", s), ("Sb16", c)],
                          writes=[("ps", obank)])
                    if c < nblk - 1:
                        state_update(c, Sf32, "Sf32", 0, 0, 4 + s)

                def stage2(c):
                    s = c % 2
                    cs = slice(c * 128, (c + 1) * 128)
                    obank = 2 + s
                    S.add("act", lambda e, s=s, obank=obank: e.activation(out=ob[s], in_=ps[:, obank, 0:128], func=AF.Copy),
                          reads=[("ps", obank)], writes=[("ob", s)])
                    for hh in range(2):
                        S.add("dve", lambda e, s=s, hh=hh: e.bn_stats(out=gst[s][:, hh, :], in_=ob[s][:, 64 * hh:64 * hh + 64]),
                              reads=[("ob", s)], writes=[("gst", s, hh)])
                        S.add("dve", lambda e, s=s, hh=hh: e.bn_aggr(out=gmv[s][:, hh, :], in_=gst[s][:, hh, :]),
                              reads=[("gst", s, hh)], writes=[("gmv", s, hh)])
                    S.add("dve", lambda e, s=s: e.tensor_scalar(out=grs[s], in0=gmv[s][:, :, 1], scalar1=EPS, scalar2=None,
                                                                op0=ALU.add),
                          reads=[("gmv", s, 0), ("gmv", s, 1)], writes=[("grs", s)])
                    S.add("act", lambda e, s=s: e.activation(out=grs[s], in_=grs[s], func=AF.Ln), reads=[("grs", s)], writes=[("grs", s)])
                    S.add("act", lambda e, s=s: e.activation(out=grs[s], in_=grs[s], func=AF.Exp, scale=-0.5),
                          reads=[("grs", s)], writes=[("grs", s)])
                    for hh in range(2):
                        S.add("dve", lambda e, s=s, hh=hh: e.tensor_scalar(
                            out=ob[s][:, 64 * hh:64 * hh + 64], in0=ob[s][:, 64 * hh:64 * hh + 64],
                            scalar1=gmv[s][:, hh, 0:1], scalar2=grs[s][:, hh:hh + 1], op0=ALU.subtract, op1=ALU.mult),
                            reads=[("ob", s), ("grs", s), ("gmv", s, hh)], writes=[("ob", s)])
                    S.add("pool", lambda e, s=s: e.tensor_tensor(out=ob[s], in0=ob[s], in1=gng[:, 128 * p:128 * p + 128], op=ALU.mult),
                          reads=[("ob", s)], writes=[("ob", s)])
                    S.add("pool", lambda e, s=s: e.tensor_tensor(out=ob[s], in0=ob[s], in1=gnb[:, 128 * p:128 * p + 128], op=ALU.add),
                          reads=[("ob", s)], writes=[("ob", s)])
                    S.add("pool", lambda e, s=s, c=c: e.tensor_tensor(out=rob[s], in0=ob[s], in1=gate[:, c, :], op=ALU.mult),
                          reads=[("ob", s), ("gate", c)], writes=[("rob", s)])
                    tbank = 6 + s
                    S.add("pe", lambda e, s=s, tbank=tbank: e.transpose(out=psb[:, tbank, 0:128], in_=rob[s], identity=ident),
                          reads=[("rob", s)], writes=[("ps", tbank)])
                    S.add("act", lambda e, cs=cs, tbank=tbank: e.activation(out=moT[:, cs], in_=psb[:, tbank, 0:128], func=AF.Copy),
                          reads=[("ps", tbank)], writes=[("moT", c // 4)])

                if RL >= 2:
                    stage1(0)
                    for c in range(nblk):
                        if c + 1 < nblk:
                            stage1(c + 1)
                        stage2(c)
                import os as _os
                if _os.environ.get("RET_NOMO") != "1":
                    S.add("sp", lambda e, p=p, T=T, tok0=tok0: e.dma_start(out=self.MO[p, :, tok0:tok0 + T], in_=moT),
                          reads=[("moT", t) for t in range(ntile)], writes=[("MO", p)], dma_key=("mo", p % 2))
                S.barrier()

        def do_att(h, T, tok0):
            nblk = T // 128
            ntile = T // 512
            if True:
                A.off = A2
                m_h = slopes[h]
                wa = A.alloc([8, 384], BF16)
                qT = A.alloc([T], BF16)
                kTa = A.alloc([T], BF16)
                kTb = A.alloc([T], BF16)
                V = A.alloc([nblk, 128], BF16)
                moT = A.alloc([T], BF16)
                tmp = [A.alloc([512], F32) for _ in range(4)]
                PT = [A.alloc([512], BF16) for _ in range(4)]
                rd = [A.alloc([512], F32) for _ in range(2)]
                tt = [A.alloc([512], F32) for _ in range(2)]
                deferred = []
                att = A.alloc([512], F32)
                sq = A.alloc([512], BF16)
                rr = A.alloc([512], F32)
                for j in range(3):
                    c0 = 2048 + 512 * j + 128 * h
                    S.add("pool", lambda e, j=j, c0=c0: [e.dma_start(
                        out=wa[:, k, 128 * j:128 * j + 128],
                        in_=self.w_in[l, k * 128:(k + 1) * 128, c0:c0 + 128]) for k in range(8)],
                        writes=[("wa", j)], dma_key=("wsl", j), ndma=8)
                for t in range(ntile):
                    for j, (dstT, nm) in enumerate(((qT, "qT"), (kTa, "kT"))):
                        bank = (2 * t + j) % 4

                        def mm(e, t=t, j=j, bank=bank):
                            for k in range(8):
                                ins = e.matmul(ps[:, bank, :], lhsT=wa[:, k, 128 * j:128 * j + 128],
                                               rhs=hT[:, k, t * 512:(t + 1) * 512], start=(k == 0), stop=(k == 7))
                            return ins
                        S.add("pe", mm, reads=[("hT", t), ("wa", j)], writes=[("ps", bank)])
                        S.add("act", lambda e, dstT=dstT, t=t, bank=bank: e.activation(
                            out=dstT[:, t * 512:(t + 1) * 512], in_=ps[:, bank, :], func=AF.Copy),
                            reads=[("ps", bank)], writes=[(nm, t)])
                        if j == 1:
                            S.add("act", lambda e, t=t, bank=bank: e.activation(
                                out=kTb[:, t * 512:(t + 1) * 512], in_=ps[:, bank, :], func=AF.Copy),
                                reads=[("ps", bank)], writes=[("kTb", t)])
                            S.add("dve", lambda e, t=t: e.memset(kTa[64:128, t * 512:(t + 1) * 512], 0.0),
                                  reads=[("kT", t)], writes=[("kT", t)])
                            S.add("dve", lambda e, t=t: e.memset(kTb[0:64, t * 512:(t + 1) * 512], 0.0),
                                  reads=[("kTb", t)], writes=[("kTb", t)])
                for blk in range(nblk):
                    bank = 4 + blk % 4

                    def mm(e, blk=blk, bank=bank):
                        for k in range(8):
                            ins = e.matmul(ps[:, bank, 0:128], lhsT=hT[:, k, blk * 128:(blk + 1) * 128],
                                           rhs=wa[:, k, 256:384], start=(k == 0), stop=(k == 7))
                        return ins
                    S.add("pe", mm, reads=[("hT", blk // 4), ("wa", 2)], writes=[("ps", bank)])
                    S.add("dve", lambda e, blk=blk, bank=bank: e.tensor_copy(out=V[:, blk, :], in_=ps[:, bank, 0:128]),
                          reads=[("ps", bank)], writes=[("V", blk)])

                itc = [0]

                def post_chain(qt):
                    qs = slice(qt * 512, (qt + 1) * 512)
                    ops = []
                    for mp in range(2):
                        ops.append(lambda mp=mp: S.add("dve", lambda e: e.reciprocal(out=rd[mp], in_=rd[mp]),
                                                       reads=[("rd", mp)], writes=[("rd", mp)]))
                        ops.append(lambda mp=mp: S.add("dve", lambda e: e.tensor_tensor(out=tt[mp], in0=tt[mp], in1=rd[mp], op=ALU.mult),
                                                       reads=[("tt", mp), ("rd", mp)], writes=[("tt", mp)]))
                    ops.append(lambda: S.add("dve", lambda e: e.scalar_tensor_tensor(
                        out=att, in0=tt[1], scalar=neglam, in1=tt[0], op0=ALU.mult, op1=ALU.add),
                        reads=[("tt", 0), ("tt", 1)], writes=["att"]))
                    ops.append(lambda: S.add("act", lambda e: e.activation(out=sq, in_=att, func=AF.Square),
                                             reads=["att"], writes=["sq"]))

                    def ssq():
                        qbank = 2 * (itc[0] % 2)
                        S.add("pe", lambda e: e.matmul(ps[:, qbank, :], lhsT=self.ones_bf, rhs=sq, start=True, stop=True),
                              reads=["sq"], writes=[("ps", qbank)])
                        S.add("act", lambda e: e.activation(out=rr, in_=ps[:, qbank, :], func=AF.Ln, bias=self.eps_c,
                                                            scale=1.0 / 128),
                              reads=[("ps", qbank)], writes=["rr"])
                    ops.append(ssq)
                    ops.append(lambda: S.add("act", lambda e: e.activation(out=rr, in_=rr, func=AF.Exp, scale=-0.5),
                                             reads=["rr"], writes=["rr"]))
                    ops.append(lambda: S.add("dve", lambda e: e.tensor_tensor(out=att, in0=att, in1=rr, op=ALU.mult),
                                             reads=["att", "rr"], writes=["att"]))
                    ops.append(lambda: S.add("dve", lambda e: e.tensor_scalar(out=moT[:, qs], in0=att, scalar1=sg, scalar2=None,
                                                                              op0=ALU.mult),
                                             reads=["att"], writes=[("moT", qt)]))
                    return ops

                pending_pv = []

                def flush_pv():
                    while pending_pv:
                        pending_pv.pop(0)()

                for qt in range(ntile):
                    qs = slice(qt * 512, (qt + 1) * 512)
                    for kc in range(nblk):
                        ks = slice(kc * 128, (kc + 1) * 128)
                        par = itc[0] % 2
                        itc[0] += 1
                        dk_ = 4 * qt - kc
                        for mp in range(2):
                            bank = 2 * par + mp
                            sl = 2 * par + mp
                            rs_ = slice(64 * mp, 64 * mp + 64)
                            kTm = kTa if mp == 0 else kTb
                            S.add("pe", lambda e, bank=bank, kTm=kTm, ks=ks, qs=qs: e.matmul(
                                ps[:, bank, :], lhsT=kTm[:, ks], rhs=qT[:, qs], start=True, stop=True),
                                reads=[("kT", kc // 4), ("kTb", kc // 4), ("qT", qt)], writes=[("ps", bank)])
                            if dk_ > 0:
                                tbl, c1, c2 = Eq, -m_h / SCALE, -m_h * 128.0 * dk_
                            elif dk_ <= -4:
                                tbl, c1, c2 = Eq, m_h / SCALE, m_h * 128.0 * dk_
                            else:
                                tbl, c1, c2 = Dg[:, -dk_, :], -m_h / SCALE, 0.0
                            S.add("dve", lambda e, sl=sl, bank=bank, tbl=tbl, c1=c1: e.scalar_tensor_tensor(
                                out=tmp[sl], in0=tbl, scalar=c1, in1=ps[:, bank, :], op0=ALU.mult, op1=ALU.add),
                                reads=[("ps", bank)], writes=[("tmp", sl)])
                            S.add("act", lambda e, sl=sl, c2=c2: e.activation(out=PT[sl], in_=tmp[sl], func=AF.Exp,
                                                                              bias=float(c2), scale=SCALE),
                                  reads=[("tmp", sl)], writes=[("PT", sl)])

                        def mmPV(e, par=par, kc=kc, nblk=nblk):
                            for mp in range(2):
                                e.matmul(ps[:, 4 + mp, :], lhsT=V[:, kc, :], rhs=PT[2 * par + mp], start=(kc == 0), stop=(kc == nblk - 1))
                                ins = e.matmul(ps[:, 6 + mp, :], lhsT=self.ones_bf, rhs=PT[2 * par + mp], start=(kc == 0),
                                               stop=(kc == nblk - 1))
                            return ins
                        flush_pv()
                        pending_pv.append(lambda mmPV=mmPV, par=par, kc=kc: S.add(
                            "pe", mmPV, reads=[("PT", 2 * par), ("PT", 2 * par + 1), ("V", kc)],
                            writes=[("ps", 4), ("ps", 5), ("ps", 6), ("ps", 7)]))
                        if kc >= 1 and deferred:
                            deferred.pop(0)()
                    flush_pv()
                    while deferred:
                        deferred.pop(0)()
                    for mp in range(2):
                        S.add("act", lambda e, mp=mp: e.activation(out=tt[mp], in_=ps[:, 4 + mp, :], func=AF.Copy),
                              reads=[("ps", 4 + mp)], writes=[("tt", mp)])
                        S.add("act", lambda e, mp=mp: e.activation(out=rd[mp], in_=ps[:, 6 + mp, :], func=AF.Copy),
                              reads=[("ps", 6 + mp)], writes=[("rd", mp)])
                    deferred.extend(post_chain(qt))
                while deferred:
                    deferred.pop(0)()
                S.add("sp", lambda e, h=h, T=T, tok0=tok0: e.dma_start(out=self.MO[4 + h, :, tok0:tok0 + T], in_=moT),
                      reads=[("moT", t) for t in range(ntile)], writes=[("MO", 4 + h)], dma_key=("mo", h % 2))
                S.barrier()

        def do_b4(T, tok0):
            import os as _os
            nblk = T // 128
            ntile = T // 512
            A.off = A2
            mot = [A.alloc([8, 128], BF16) for _ in range(2)]
            xres = [A.alloc([D], F32) for _ in range(2)]
            yb = [A.alloc([D], F32) for _ in range(2)]
            scr = [dict(st=A.alloc([2, 6], F32), mv=A.alloc([2], F32), rs=A.alloc([1], F32)) for _ in range(2)]
            for blk in range(nblk):
                s = blk % 2
                rows = slice(tok0 + blk * 128, tok0 + (blk + 1) * 128)
                if _os.environ.get("NOMO") == "1":
                    S.add("pool", lambda e, s=s: e.memset(mot[s], 0.5), writes=[("mot", s)])
                else:
                    S.add("sp", lambda e, s=s, rows=rows: [e.dma_start(out=mot[s][:, k, :], in_=self.MO[k, :, rows])
                                                           for k in range(8)],
                          writes=[("mot", s)], dma_key=("mot", s), ndma=8)
                S.add("sp", lambda e, s=s, rows=rows: e.dma_start(out=xres[s], in_=src[rows, :]),
                      writes=[("xres", s)], dma_key=("xres", s))
                b0 = 2 * (blk % 4)

                def mm(e, s=s, b0=b0):
                    for half in range(2):
                        for k in range(8):
                            ins = e.matmul(ps[:, b0 + half, :], lhsT=mot[s][:, k, :], rhs=wout[:, k, half * 512:(half + 1) * 512],
                                           start=(k == 0), stop=(k == 7))
                    return ins
                S.add("pe", mm, reads=[("mot", s)] + [("mot", s, k) for k in range(8)], writes=[("ps", b0), ("ps", b0 + 1)])
                ysb = yb[s]
                pso = ps[:, b0:b0 + 2, :].rearrange("p a b -> p (a b)")
                S.add("act", lambda e, ysb=ysb, pso=pso: e.activation(out=ysb, in_=pso, func=AF.Copy),
                      reads=[("ps", b0), ("ps", b0 + 1)], writes=[("my", s)])
                S.add("dve", lambda e, ysb=ysb, s=s: e.scalar_tensor_tensor(
                    out=ysb, in0=xres[s], scalar=ALPHA, in1=ysb, op0=ALU.mult, op1=ALU.add),
                    reads=[("my", s), ("xres", s)], writes=[("my", s)])
                self.ln_block(ysb, s, lng, lnb, scr[s], "m")
                S.add("pool", lambda e, ysb=ysb, rows=rows: e.dma_start(out=dst[rows, :], in_=ysb),
                      reads=[("my", s)], dma_key=("yst", s))
            S.barrier()

        tok0 = 0
        for si, T in enumerate(self.seq_lens):
            import os as _os
            if self.mix_stage >= 1 and _os.environ.get("NOB1") != "1":
                do_b1(T, tok0)
            if self.mix_stage >= 2:
                for p in range(4):
                    do_ret(p, T, tok0)
            if self.mix_stage >= 3:
                for h in range(4):
                    do_att(h, T, tok0)
            if self.mix_stage >= 1 and _os.environ.get("NOB4") != "1":
                do_b4(T, tok0)
            tok0 += T
        S.barrier(new_set=True)


_PARAM_NAMES = ["w_in", "w_out", "ret_decay_f", "ret_decay_b", "ret_gn_g", "ret_gn_b",
                "diff_lq1", "diff_lk1", "diff_lq2", "diff_lk2", "diff_subln_g",
                "ffn1_w13", "ffn1_w2", "ffn2_w13", "ffn2_w2",
                "ln1_g", "ln1_b", "ln2_g", "ln2_b", "ln3_g", "ln3_b"]


def extra_layouts(params):
    f, b = params["ret_decay_f"], params["ret_decay_b"]
    dec_all = np.ascontiguousarray(np.concatenate([f, b], axis=1))
    dec_pairs = np.ascontiguousarray(np.stack(
        [np.concatenate([f[:, 0::2], b[:, 0::2]], axis=1), np.concatenate([f[:, 1::2], b[:, 1::2]], axis=1)], axis=1))
    return {"dec_all": dec_all, "dec_pairs": dec_pairs}


def kernel(**inputs):
    xp = np.asarray(inputs["x_prompt"], dtype=np.float32)
    xs = np.asarray(inputs["x_sample"], dtype=np.float32)
    params = {k: np.ascontiguousarray(np.asarray(inputs[k], dtype=np.float32)) for k in _PARAM_NAMES}
    params.update(extra_layouts(params))
    nc = Builder(SEQ_LENS).build()
    in_maps = []
    for c in range(N_CORES):
        xc = np.concatenate([xp[c].reshape(-1, D), xs[4 * c:4 * c + 4].reshape(-1, D)], axis=0)
        m = {"x": np.ascontiguousarray(xc)}
        m.update(params)
        in_maps.append(m)
    res = run_bass_kernel_spmd(nc, in_maps, core_ids=list(range(N_CORES)))
    yp = np.empty_like(xp)
    ys = np.empty_like(xs)
    for c in range(N_CORES):
        yc = np.asarray(res.results[c]["y"], dtype=np.float32)
        yp[c] = yc[:4096]
        ys[4 * c:4 * c + 4] = yc[4096:].reshape(4, 2048, D)
    return (yp, ys)
```

```python
import math
from contextlib import ExitStack

import numpy as np
import concourse.bass as bass
import concourse.mybir as mybir
from concourse.bass_utils import run_bass_kernel_spmd

F32 = mybir.dt.float32
BF16 = mybir.dt.bfloat16
U8 = mybir.dt.uint8
AF = mybir.ActivationFunctionType
ALU = mybir.AluOpType
AX = mybir.AxisListType

D = 1024
DFF = 2816
NFF = DFF // 128
INW = 3584
DEPTH = 2
EPS = 1e-5
ALPHA = (2 * DEPTH) ** 0.25
N_CORES = 8
SEQ_LENS = (4096, 2048, 2048, 2048, 2048)
ENGS = ("pe", "act", "dve", "pool", "sp")


def lambda_init(layer):
    return 0.8 - 0.6 * math.exp(-0.3 * layer)


class Op:
    __slots__ = ("eng", "fn", "dma_key", "deps", "inc", "val", "idx", "is_dma", "sset", "kind", "ndma")

    def __init__(self, eng, fn, dma_key):
        self.eng = eng
        self.fn = fn
        self.dma_key = dma_key
        self.is_dma = dma_key is not None
        self.deps = []
        self.inc = False
        self.val = None
        self.sset = 0
        self.kind = "op"
        self.ndma = 1


class Sched:
    def __init__(self):
        self.ops = []
        self.last_w = {}
        self.readers = {}
        self.final_dma = []
        self.sset = 0
        self.last_on_eng = {}
        self.last_dma = {}

    def add(self, eng, fn, reads=(), writes=(), dma_key=None, final=False, ndma=1):
        op = Op(eng, fn, dma_key)
        op.ndma = ndma
        op.idx = len(self.ops)
        op.sset = self.sset
        deps = {}
        for k in reads:
            w = self.last_w.get(k)
            if w is not None:
                deps[w.idx] = (w, True)
        for k in writes:
            w = self.last_w.get(k)
            if w is not None and w.idx not in deps:
                deps[w.idx] = (w, False)
            for r in self.readers.get(k, ()):
                if r.idx not in deps:
                    deps[r.idx] = (r, False)
        op.deps = [deps[i] for i in sorted(deps)]
        for k in reads:
            self.readers.setdefault(k, []).append(op)
        for k in writes:
            self.last_w[k] = op
            self.readers[k] = []
        self.ops.append(op)
        self.last_on_eng[eng] = op
        if op.is_dma:
            self.last_dma[dma_key] = op
        if final:
            self.final_dma.append(op)
        return op

    def barrier(self, new_set=False):
        drains = []
        for e in ENGS:
            op = self.add(e, lambda eng: eng.drain())
            op.kind = "drain"
            drains.append(op)
        dmas = list(self.last_dma.values())
        for e in ENGS:
            op = self.add(e, None)
            op.kind = "join"
            op.deps = [(d, True) for d in drains if d.eng != e] + [(d, True) for d in dmas]
        self.last_w = {}
        self.readers = {}
        self.last_dma = {}
        if new_set:
            self.sset += 1

    @staticmethod
    def _need(op, d, raw):
        if d.is_dma or d.eng != op.eng:
            return True
        if op.eng == "pe":
            return False
        return True

    def emit(self, nc):
        for op in self.ops:
            for d, raw in op.deps:
                if self._need(op, d, raw):
                    d.inc = True
        for op in self.final_dma:
            op.inc = True
        nsets = self.sset + 1
        cnt = {(e, s): 0 for e in ENGS for s in range(nsets)}
        dcnt = {}
        for op in self.ops:
            if op.is_dma:
                dcnt[op.dma_key] = dcnt.get(op.dma_key, 0) + 16 * op.ndma
                op.val = dcnt[op.dma_key]
            elif op.inc:
                cnt[(op.eng, op.sset)] += 1
                op.val = cnt[(op.eng, op.sset)]
        self.stats = dict(maxcnt=max(cnt.values()), ndma_keys=len(dcnt), nops=len(self.ops),
                          maxd=max(dcnt.values()) if dcnt else 0)
        with ExitStack() as st:
            esem = {k: st.enter_context(nc.semaphore("s_%s%d" % k)) for k in cnt}
            dsem = {k: st.enter_context(nc.semaphore("d%d" % i)) for i, k in enumerate(dcnt)}
            block = st.enter_context(nc.Block())

            def semof(o):
                return dsem[o.dma_key] if o.is_dma else esem[(o.eng, o.sset)]

            def run(engname, e):
                waited = {}
                for op in self.ops:
                    if op.eng != engname:
                        continue
                    need = {}
                    for d, raw in op.deps:
                        if not self._need(op, d, raw):
                            continue
                        s = semof(d)
                        key = id(s)
                        if key not in need or need[key][1] < d.val:
                            need[key] = (s, d.val)
                    for key, (s, v) in need.items():
                        if waited.get(key, 0) >= v:
                            continue
                        waited[key] = v
                        e.wait_ge(s, v)
                    if op.fn is None:
                        continue
                    ins = op.fn(e)
                    if op.is_dma:
                        for i_ in (ins if isinstance(ins, list) else [ins]):
                            i_.then_inc(dsem[op.dma_key], 16)
                    elif op.inc:
                        ins.then_inc(esem[(op.eng, op.sset)], 1)
                if engname == "sp":
                    for k, v in dcnt.items():
                        e.wait_ge(dsem[k], v)

            block.tensor(lambda e: run("pe", e))
            block.scalar(lambda e: run("act", e))
            block.vector(lambda e: run("dve", e))
            block.gpsimd(lambda e: run("pool", e))
            block.sync(lambda e: run("sp", e))


class Arena:
    def __init__(self, handle, nbytes):
        self.h = handle
        self.n = nbytes
        self.off = 0
        self.mark_ = 0

    def alloc(self, free_shape, dt):
        esz = 2 if dt == BF16 else 4
        n = 1
        for s in free_shape:
            n *= s
        nb = (n * esz + 31) // 32 * 32
        assert self.off + nb <= self.n, ("arena overflow", self.off, nb, self.n)
        v = self.h[:, self.off:self.off + n * esz].bitcast(dt)
        self.off += nb
        if len(free_shape) == 2:
            v = v.rearrange("p (a b) -> p a b", a=free_shape[0])
        elif len(free_shape) == 3:
            v = v.rearrange("p (a b c) -> p a b c", a=free_shape[0], b=free_shape[1])
        return v

    def mark(self):
        self.mark_ = self.off

    def reset(self):
        self.off = self.mark_


class Builder:
    def __init__(self, seq_lens, depth=DEPTH, do_mixer=True, do_ffn2=True, mix_stage=3):
        self.mix_stage = mix_stage
        self.seq_lens = tuple(seq_lens)
        self.NT = sum(seq_lens)
        self.depth = depth
        self.do_mixer = do_mixer
        self.do_ffn2 = do_ffn2
        self.nc = bass.Bass("TRN2", target_bir_lowering=False)
        self.S = Sched()

    def dram_in(self, name, shape, dt=F32):
        return self.nc.dram_tensor(name, list(shape), dt, kind="ExternalInput").ap()

    def build(self):
        nc, S = self.nc, self.S
        NT = self.NT
        self.x = self.dram_in("x", [NT, D])
        self.y = nc.dram_tensor("y", [NT, D], F32, kind="ExternalOutput").ap()
        self.w_in = self.dram_in("w_in", [DEPTH, D, INW])
        self.w_out = self.dram_in("w_out", [DEPTH, D, D])
        self.dec_f = self.dram_in("ret_decay_f", [DEPTH, 8])
        self.dec_b = self.dram_in("ret_decay_b", [DEPTH, 8])
        self.gn_g = self.dram_in("ret_gn_g", [DEPTH, 512])
        self.gn_b = self.dram_in("ret_gn_b", [DEPTH, 512])
        self.lq1 = self.dram_in("diff_lq1", [DEPTH, 64])
        self.lk1 = self.dram_in("diff_lk1", [DEPTH, 64])
        self.lq2 = self.dram_in("diff_lq2", [DEPTH, 64])
        self.lk2 = self.dram_in("diff_lk2", [DEPTH, 64])
        self.subln = self.dram_in("diff_subln_g", [DEPTH, 128])
        self.dec_all = self.dram_in("dec_all", [DEPTH, 16])
        self.dec_pairs = self.dram_in("dec_pairs", [DEPTH, 2, 8])
        self.f1w13 = self.dram_in("ffn1_w13", [DEPTH, D, 2 * DFF])
        self.f1w2 = self.dram_in("ffn1_w2", [DEPTH, DFF, D])
        self.f2w13 = self.dram_in("ffn2_w13", [DEPTH, D, 2 * DFF])
        self.f2w2 = self.dram_in("ffn2_w2", [DEPTH, DFF, D])
        self.ln = {}
        for i in (1, 2, 3):
            self.ln[(i, "g")] = self.dram_in("ln%d_g" % i, [DEPTH, D])
            self.ln[(i, "b")] = self.dram_in("ln%d_b" % i, [DEPTH, D])
        self.XA = nc.dram_tensor("xa_scr", [NT, D], F32, kind="Internal").ap()
        self.XB = nc.dram_tensor("xb_scr", [NT, D], F32, kind="Internal").ap()
        self.MO = nc.dram_tensor("mo_scr", [8, 128, NT], BF16, kind="Internal").ap()

        with ExitStack() as st:
            nbytes = 206000
            arena_h = st.enter_context(nc.sbuf_tensor("arena", [128, nbytes], U8))
            self.A = Arena(arena_h, nbytes)
            self.ps = st.enter_context(nc.psum_tensor("ps", [128, 8, 512], F32))
            self.psb = self.ps[:].bitcast(BF16)
            self.setup_consts()
            self.A.mark()
            cur = self.x
            for l in range(self.depth):
                last = (l == self.depth - 1)
                self.phase_ffn(l, self.f1w13, self.f1w2, self.ln[(1, "g")], self.ln[(1, "b")], cur, self.XA)
                cur = self.XA
                if self.do_mixer:
                    self.phase_mixer(l, self.XA, self.XB)
                    cur = self.XB
                if self.do_ffn2:
                    dst = self.y if last else self.XA
                    src = cur
                    if src is self.XA:
                        dst = self.y if last else self.XB
                    self.phase_ffn(l, self.f2w13, self.f2w2, self.ln[(3, "g")], self.ln[(3, "b")], src, dst)
                    cur = dst
            if cur is not self.y:
                self.copy_out(cur)
            S.emit(nc)
        return nc

    def setup_consts(self):
        S, A = self.S, self.A
        self.ident = A.alloc([128], BF16)
        identf = A.alloc([128], F32)
        self.ones_bf = A.alloc([128], BF16)
        self.ones_f = A.alloc([128], F32)
        self.eps_c = A.alloc([1], F32)
        S.add("pool", lambda e: e.memset(identf, 0.0), writes=["identf"])
        S.add("pool", lambda e: e.affine_select(out=identf, in_=identf, pattern=[[-1, 128]],
                                                compare_op=ALU.not_equal, fill=1.0, base=0,
                                                channel_multiplier=1), reads=["identf"], writes=["identf"])
        S.add("pool", lambda e: e.tensor_copy(out=self.ident, in_=identf), reads=["identf"], writes=["ident"])
        S.add("pool", lambda e: e.memset(self.ones_bf, 1.0), writes=["ones_bf"])
        S.add("pool", lambda e: e.memset(self.ones_f, 1.0), writes=["ones_f"])
        S.add("pool", lambda e: e.memset(self.eps_c, EPS), writes=["eps_c"])
        S.barrier()

    def copy_out(self, cur):
        S, A = self.S, self.A
        A.reset()
        buf = [A.alloc([D], F32) for _ in range(2)]
        for blk in range(self.NT // 128):
            s = blk % 2
            rows = slice(blk * 128, (blk + 1) * 128)
            S.add("sp", lambda e, s=s, rows=rows: e.dma_start(out=buf[s], in_=cur[rows, :]),
                  writes=[("cb", s)], dma_key=("cb", s))
            S.add("sp", lambda e, s=s, rows=rows: e.dma_start(out=self.y[rows, :], in_=buf[s]),
                  reads=[("cb", s)], dma_key=("co", s), final=True)
        S.barrier()

    def ln_block(self, ysb, slot, lng, lnb, scr, pre):
        S = self.S
        st, mv, rs = scr["st"], scr["mv"], scr["rs"]
        ky = (pre + "y", slot)
        kst, kmv, krs = (pre + "st", slot), (pre + "mv", slot), (pre + "rs", slot)
        S.add("dve", lambda e: e.bn_stats(out=st[:, 0, :], in_=ysb[:, 0:512]), reads=[ky], writes=[(kst, 0)])
        S.add("dve", lambda e: e.bn_stats(out=st[:, 1, :], in_=ysb[:, 512:1024]), reads=[ky], writes=[(kst, 1)])
        S.add("dve", lambda e: e.bn_aggr(out=mv, in_=st), reads=[(kst, 0), (kst, 1)], writes=[kmv])
        S.add("dve", lambda e: e.tensor_scalar(out=rs, in0=mv[:, 1:2], scalar1=EPS, scalar2=None, op0=ALU.add),
              reads=[kmv], writes=[krs])
        S.add("act", lambda e: e.activation(out=rs, in_=rs, func=AF.Sqrt), reads=[krs], writes=[krs])
        S.add("dve", lambda e: e.reciprocal(out=rs, in_=rs), reads=[krs], writes=[krs])
        S.add("dve", lambda e: e.tensor_scalar(out=ysb, in0=ysb, scalar1=mv[:, 0:1], scalar2=rs,
                                               op0=ALU.subtract, op1=ALU.mult),
              reads=[ky, kmv, krs], writes=[ky])
        S.add("pool", lambda e: e.tensor_tensor(out=ysb, in0=ysb, in1=lng, op=ALU.mult),
              reads=[ky, "lng"], writes=[ky])
        S.add("pool", lambda e: e.tensor_tensor(out=ysb, in0=ysb, in1=lnb, op=ALU.add),
              reads=[ky, "lnb"], writes=[ky])

    def phase_ffn(self, l, w13_d, w2_d, g_d, b_d, src, dst):
        S, A, ps, psb = self.S, self.A, self.ps, self.psb
        A.reset()
        W13 = A.alloc([8, 2 * DFF], BF16)
        W2 = A.alloc([NFF, D], BF16)
        lng = A.alloc([D], F32)
        lnb = A.alloc([D], F32)
        xres = [A.alloc([D], F32) for _ in range(2)]
        xbf = [A.alloc([D], BF16) for _ in range(4)]
        xT = A.alloc([8, 512], BF16)
        gT = A.alloc([NFF, 512], BF16)
        sa = [A.alloc([512], F32) for _ in range(2)]
        yb = [A.alloc([D], F32) for _ in range(2)]
        scr = [dict(st=A.alloc([2, 6], F32), mv=A.alloc([2], F32), rs=A.alloc([1], F32)) for _ in range(2)]
        ident = self.ident
        final = dst is self.y

        for j in range(11):
            cs = slice(j * 512, (j + 1) * 512)
            S.add("pool", lambda e, cs=cs: e.dma_start(
                out=W13[:, :, cs], in_=w13_d[l, :, cs].rearrange("(k p) n -> p k n", p=128)),
                writes=[("W13", j)], dma_key=("W13", j))
        for j in range(11):
            S.add("pool", lambda e, j=j: e.dma_start(
                out=W2[:, 2 * j:2 * j + 2, :],
                in_=w2_d[l, 256 * j:256 * j + 256, :].rearrange("(k p) n -> p k n", p=128)),
                writes=[("W2", j)], dma_key=("W2", j))
        S.add("sp", lambda e: e.dma_start(out=lng, in_=g_d[l:l + 1, :].broadcast_to([128, D])),
              writes=["lng"], dma_key="lng")
        S.add("sp", lambda e: e.dma_start(out=lnb, in_=b_d[l:l + 1, :].broadcast_to([128, D])),
              writes=["lnb"], dma_key="lnb")

        ntile = self.NT // 512

        def load_tile(t):
            for b in range(4):
                rows = slice((4 * t + b) * 128, (4 * t + b + 1) * 128)
                S.add("pool", lambda e, b=b, rows=rows: e.dma_start(out=xbf[b], in_=src[rows, :]),
                      writes=[("xbf", b)], dma_key=("xbf", b))

        def transposes(t):
            for k in range(8):
                bank = 6 + (k % 2)

                def tr(e, k=k, bank=bank):
                    for b in range(4):
                        ins = e.transpose(out=psb[:, bank, b * 128:(b + 1) * 128],
                                          in_=xbf[b][:, k * 128:(k + 1) * 128], identity=ident)
                    return ins
                S.add("pe", tr, reads=[("xbf", b) for b in range(4)] + ["ident"], writes=[("ps", bank)])
                if k % 2 == 0:
                    S.add("act", lambda e, k=k, bank=bank: e.activation(out=xT[:, k, :], in_=psb[:, bank, 0:512],
                                                                       func=AF.Copy),
                          reads=[("ps", bank)], writes=[("xT", k)])
                else:
                    S.add("dve", lambda e, k=k, bank=bank: e.tensor_copy(out=xT[:, k, :], in_=psb[:, bank, 0:512]),
                          reads=[("ps", bank)], writes=[("xT", k)])

        def ffn1(t):
            for c in range(NFF):
                q = c % 2
                ba, bb = 2 * q, 2 * q + 1

                def mm(e, c=c, ba=ba, bb=bb):
                    for k in range(8):
                        e.matmul(ps[:, ba, :], lhsT=W13[:, k, c * 128:(c + 1) * 128], rhs=xT[:, k, :],
                                 start=(k == 0), stop=(k == 7))
                    for k in range(8):
                        ins = e.matmul(ps[:, bb, :], lhsT=W13[:, k, DFF + c * 128:DFF + (c + 1) * 128],
                                       rhs=xT[:, k, :], start=(k == 0), stop=(k == 7))
                    return ins
                S.add("pe", mm, reads=[("xT", k) for k in range(8)] + [("W13", c // 4), ("W13", (NFF + c) // 4)],
                      writes=[("ps", ba), ("ps", bb)])
                S.add("act", lambda e, q=q, ba=ba: e.activation(out=sa[q], in_=ps[:, ba, :], func=AF.Silu),
                      reads=[("ps", ba)], writes=[("sa", q)])
                S.add("dve", lambda e, c=c, q=q, bb=bb: e.tensor_tensor(out=gT[:, c, :], in0=sa[q], in1=ps[:, bb, :],
                                                                       op=ALU.mult),
                      reads=[("sa", q), ("ps", bb)], writes=[("gT", c)])

        def ffn2(t):
            for b in range(4):
                blk = 4 * t + b
                slot = blk % 2
                rows = slice(blk * 128, (blk + 1) * 128)
                b0 = 4 if b % 2 == 0 else 6
                S.add("sp", lambda e, slot=slot, rows=rows: e.dma_start(out=xres[slot], in_=src[rows, :]),
                      writes=[("xres", slot)], dma_key=("xres", slot))

                def mm(e, b=b, b0=b0):
                    for half in range(2):
                        for c in range(NFF):
                            ins = e.matmul(ps[:, b0 + half, :], lhsT=gT[:, c, b * 128:(b + 1) * 128],
                                           rhs=W2[:, c, half * 512:(half + 1) * 512],
                                           start=(c == 0), stop=(c == NFF - 1))
                    return ins
                S.add("pe", mm, reads=[("gT", c) for c in range(NFF)] + [("W2", j) for j in range(11)],
                      writes=[("ps", b0), ("ps", b0 + 1)])
                ysb = yb[slot]
                pso = ps[:, b0:b0 + 2, :].rearrange("p a b -> p (a b)")
                S.add("act", lambda e, ysb=ysb, pso=pso: e.activation(out=ysb, in_=pso, func=AF.Identity, scale=0.5),
                      reads=[("ps", b0), ("ps", b0 + 1)], writes=[("fy", slot)])
                S.add("dve", lambda e, ysb=ysb, slot=slot: e.scalar_tensor_tensor(
                    out=ysb, in0=xres[slot], scalar=ALPHA, in1=ysb, op0=ALU.mult, op1=ALU.add),
                    reads=[("fy", slot), ("xres", slot)], writes=[("fy", slot)])
                self.ln_block(ysb, slot, lng, lnb, scr[slot], "f")
                S.add("pool", lambda e, ysb=ysb, rows=rows: e.dma_start(out=dst[rows, :], in_=ysb),
                      reads=[("fy", slot)], dma_key=("yst", slot), final=final)

        load_tile(0)
        transposes(0)
        for t in range(ntile):
            if t + 1 < ntile:
                load_tile(t + 1)
            ffn1(t)
            if t + 1 < ntile:
                transposes(t + 1)
            ffn2(t)
        S.barrier(new_set=True)

    def phase_mixer(self, l, src, dst):
        S, A, ps, psb = self.S, self.A, self.ps, self.psb
        A.reset()
        ident = self.ident
        TM = max(self.seq_lens)
        NBM = TM // 128
        I32 = mybir.dt.int32
        lam0 = lambda_init(l)
        SCALE = 64 ** -0.5
        hT = A.alloc([8, TM], BF16)
        DT = A.alloc([8, 128], F32)
        QF = A.alloc([4, 128], F32)
        QB = A.alloc([4, 128], F32)
        Eq = A.alloc([512], F32)
        Dg = A.alloc([4, 512], F32)
        gng = A.alloc([512], F32)
        gnb = A.alloc([512], F32)
        lgall = A.alloc([16], F32)
        lgp = A.alloc([8], F32)
        KFB = A.alloc([16], F32)
        GC = A.alloc([8], F32)
        c127 = A.alloc([1], F32)
        cp = A.alloc([1], F32)
        neglam = A.alloc([1], F32)
        sg = A.alloc([1], F32)
        lsum = A.alloc([2], F32)
        A2 = A.off
        lqk = A.alloc([4, 64], F32)
        tI = A.alloc([512], I32)
        tE = A.alloc([128], F32)
        tP = A.alloc([128], F32)
        tN = A.alloc([128], F32)
        tM = A.alloc([128], F32)
        tI1 = A.alloc([128], F32)

        S.add("sp", lambda e: e.dma_start(out=gng, in_=self.gn_g[l:l + 1, :].broadcast_to([128, 512])),
              writes=["gng"], dma_key="gng")
        S.add("sp", lambda e: e.dma_start(out=gnb, in_=self.gn_b[l:l + 1, :].broadcast_to([128, 512])),
              writes=["gnb"], dma_key="gnb")
        S.add("sp", lambda e: e.dma_start(out=lgall, in_=self.dec_all[l:l + 1, :].broadcast_to([128, 16])),
              writes=["lgall"], dma_key="lgall")
        for half in range(2):
            S.add("sp", lambda e, half=half: e.dma_start(
                out=lgp[64 * half:64 * half + 64, :], in_=self.dec_pairs[l, half:half + 1, :].broadcast_to([64, 8])),
                writes=[("lgp", half)], dma_key=("lgp", half))
        for i, t in enumerate((self.lq1, self.lk1, self.lq2, self.lk2)):
            S.add("sp", lambda e, i=i, t=t: e.dma_start(out=lqk[:, i, :], in_=t[l:l + 1, :].broadcast_to([128, 64])),
                  writes=[("lqk", i)], dma_key=("lqk", i))
        S.add("sp", lambda e: e.dma_start(out=sg, in_=self.subln[l, :].rearrange("(p o) -> p o", o=1)),
              writes=["sg"], dma_key="sg")
        S.barrier()
        for t_, nm in ((lgall, "lgall"), (lgp, "lgp")):
            S.add("act", lambda e, t_=t_: e.activation(out=t_, in_=t_, func=AF.Exp, scale=-1.0), reads=[nm], writes=[nm])
            S.add("act", lambda e, t_=t_: e.activation(out=t_, in_=t_, func=AF.Ln, bias=self.ones_f[:, 0:1]),
                  reads=[nm], writes=[nm])
            S.add("pool", lambda e, t_=t_: e.tensor_scalar(out=t_, in0=t_, scalar1=-1.0, scalar2=None, op0=ALU.mult),
                  reads=[nm], writes=[nm])
        S.add("pool", lambda e: e.iota(out=tI, pattern=[[1, 512]], base=0, channel_multiplier=-1), writes=["tI"])
        S.add("dve", lambda e: e.tensor_copy(out=Eq, in_=tI), reads=["tI"], writes=["Eq"])
        S.add("dve", lambda e: e.tensor_copy(out=tE, in_=Eq[:, 0:128]), reads=["Eq"], writes=["tE"])
        S.add("dve", lambda e: e.tensor_scalar(out=tP, in0=tE, scalar1=0.0, scalar2=None, op0=ALU.max),
              reads=["tE"], writes=["tP"])
        S.add("dve", lambda e: e.tensor_tensor(out=tN, in0=tP, in1=tE, op=ALU.subtract), reads=["tP", "tE"], writes=["tN"])
        for d_ in range(4):
            S.add("act", lambda e, d_=d_: e.activation(out=Dg[:, d_, :], in_=Eq, func=AF.Abs, bias=float(-128 * d_)),
                  reads=["Eq"], writes=[("Dg", d_)])
        S.add("dve", lambda e: e.tensor_copy(out=c127, in_=Eq[:, 127:128]), reads=["Eq"], writes=["c127"])
        S.add("dve", lambda e: e.tensor_scalar(out=cp, in0=Eq[:, 0:1], scalar1=-1.0, scalar2=None, op0=ALU.mult),
              reads=["Eq"], writes=["cp"])
        S.add("dve", lambda e: e.tensor_scalar(out=tI1, in0=Eq[:, 0:128], scalar1=cp, scalar2=1.0, op0=ALU.add, op1=ALU.add),
              reads=["Eq", "cp"], writes=["tI1"])
        S.barrier()
        for h in range(8):
            S.add("act", lambda e, h=h: e.activation(out=tM, in_=tP, func=AF.Exp, scale=lgall[:, h:h + 1]),
                  reads=["tM"], writes=["tM"])
            S.add("pool", lambda e, h=h: e.affine_select(out=DT[:, h, :], in_=tM, pattern=[[1, 128]], compare_op=ALU.is_ge,
                                                         fill=0.0, base=0, channel_multiplier=-1),
                  reads=["tM"], writes=[("DT", h)])
            S.add("act", lambda e, h=h: e.activation(out=tM, in_=tN, func=AF.Exp, scale=lgall[:, 8 + h:9 + h]),
                  reads=["tM"], writes=["tM"])
            S.add("pool", lambda e, h=h: e.affine_select(out=tM, in_=tM, pattern=[[-1, 128]], compare_op=ALU.is_gt,
                                                         fill=0.0, base=0, channel_multiplier=1),
                  reads=["tM"], writes=["tM"])
            S.add("pool", lambda e, h=h: e.tensor_tensor(out=DT[:, h, :], in0=DT[:, h, :], in1=tM, op=ALU.add),
                  reads=["tM", ("DT", h)], writes=[("DT", h)])
        for p in range(4):
            S.add("act", lambda e, p=p: e.activation(out=QF[:, p, :], in_=tI1, func=AF.Exp, scale=lgp[:, p:p + 1]),
                  writes=[("QF", p)])
            S.add("dve", lambda e, p=p: e.tensor_scalar(out=QB[:, p, :], in0=tI1, scalar1=-1.0, scalar2=129.0,
                                                       op0=ALU.mult, op1=ALU.add), writes=[("QB", p)])
            S.add("act", lambda e, p=p: e.activation(out=QB[:, p, :], in_=QB[:, p, :], func=AF.Exp,
                                                     scale=lgp[:, 4 + p:5 + p]), reads=[("QB", p)], writes=[("QB", p)])
        S.add("act", lambda e: e.activation(out=KFB[:, 0:8], in_=lgall[:, 0:8], func=AF.Exp, scale=c127), writes=["KF"])
        S.add("act", lambda e: e.activation(out=KFB[:, 8:16], in_=lgall[:, 8:16], func=AF.Exp, scale=cp), writes=["KB"])
        S.add("act", lambda e: e.activation(out=GC, in_=lgp, func=AF.Exp, scale=128.0), writes=["GC"])
        S.add("dve", lambda e: e.tensor_tensor(out=lqk[:, 0, :], in0=lqk[:, 0, :], in1=lqk[:, 1, :], op=ALU.mult), writes=["lq0"])
        S.add("dve", lambda e: e.tensor_tensor(out=lqk[:, 2, :], in0=lqk[:, 2, :], in1=lqk[:, 3, :], op=ALU.mult), writes=["lq2"])
        S.add("dve", lambda e: e.reduce_sum(out=lsum[:, 0:1], in_=lqk[:, 0, :], axis=AX.X), reads=["lq0"], writes=["ls0"])
        S.add("dve", lambda e: e.reduce_sum(out=lsum[:, 1:2], in_=lqk[:, 2, :], axis=AX.X), reads=["lq2"], writes=["ls1"])
        S.add("act", lambda e: e.activation(out=lsum, in_=lsum, func=AF.Exp), reads=["ls0", "ls1"], writes=["lse"])
        S.add("dve", lambda e: e.tensor_tensor(out=neglam, in0=lsum[:, 1:2], in1=lsum[:, 0:1], op=ALU.subtract),
              reads=["lse"], writes=["nl"])
        S.add("dve", lambda e: e.tensor_scalar(out=neglam, in0=neglam, scalar1=-lam0, scalar2=None, op0=ALU.add),
              reads=["nl"], writes=["nl"])
        S.add("pool", lambda e: e.tensor_scalar(out=sg, in0=sg, scalar1=1.0 - lam0, scalar2=None, op0=ALU.mult), writes=["sg2"])
        S.add("pool", lambda e: e.tensor_scalar(out=KFB, in0=KFB, scalar1=0.125, scalar2=0.0, op0=ALU.mult, op1=ALU.add),
              reads=["KF", "KB"], writes=["KF8"])
        S.barrier()

        slopes = [2.0 ** (-8.0 * (h + 1) / 4) for h in range(4)]
        def do_b1(T, tok0):
            nblk = T // 128
            ntile = T // 512
            A.off = A2
            xbf = [A.alloc([D], BF16) for _ in range(8)]
            for t in range(ntile):
                for b in range(4):
                    sl = (t % 2) * 4 + b
                    rows = slice(tok0 + (4 * t + b) * 128, tok0 + (4 * t + b + 1) * 128)
                    S.add("pool", lambda e, sl=sl, rows=rows: e.dma_start(out=xbf[sl], in_=src[rows, :]),
                          writes=[("xbf", sl)], dma_key=("xbf", sl))
                for k in range(8):
                    bank = 6 + (k % 2)

                    def tr(e, k=k, bank=bank, t=t):
                        for b in range(4):
                            ins = e.transpose(out=psb[:, bank, b * 128:(b + 1) * 128],
                                              in_=xbf[(t % 2) * 4 + b][:, k * 128:(k + 1) * 128], identity=ident)
                        return ins
                    S.add("pe", tr, reads=[("xbf", (t % 2) * 4 + b) for b in range(4)], writes=[("ps", bank)])
                    dsl = hT[:, k, t * 512:(t + 1) * 512]
                    if k % 2 == 0:
                        S.add("act", lambda e, dsl=dsl, bank=bank: e.activation(out=dsl, in_=psb[:, bank, 0:512], func=AF.Copy),
                              reads=[("ps", bank)], writes=[("hT", t)])
                    else:
                        S.add("dve", lambda e, dsl=dsl, bank=bank: e.tensor_copy(out=dsl, in_=psb[:, bank, 0:512]),
                              reads=[("ps", bank)], writes=[("hT", t)])
            S.barrier()

        def do_ret(p, T, tok0):
            nblk = T // 128
            ntile = T // 512
            if True:
                A.off = A2
                wr = A.alloc([8, 512], BF16)
                rqT = A.alloc([T], BF16)
                rkTa = A.alloc([T], BF16)
                rkTb = A.alloc([T], BF16)
                kdf_all = A.alloc([nblk, 128], BF16)
                kdb_all = A.alloc([nblk, 128], BF16)
                qf_all = A.alloc([T], BF16)
                qb_all = A.alloc([T], BF16)
                rv_tm = A.alloc([nblk, 128], BF16)
                gate = A.alloc([nblk, 128], BF16)
                Sb16a = A.alloc([nblk, 64], BF16)
                Sb16b = A.alloc([nblk, 64], BF16)
                moT = A.alloc([T], BF16)
                Sf32 = A.alloc([64], F32)
                Sb32 = A.alloc([64], F32)
                Sf16a = [A.alloc([64], BF16) for _ in range(2)]
                Sf16b = [A.alloc([64], BF16) for _ in range(2)]
                AT = [A.alloc([256], BF16) for _ in range(2)]
                ob = [A.alloc([128], F32) for _ in range(2)]
                rob = [A.alloc([128], BF16) for _ in range(2)]
                gst = [A.alloc([2, 6], F32) for _ in range(2)]
                gmv = [A.alloc([2, 2], F32) for _ in range(2)]
                grs = [A.alloc([2], F32) for _ in range(2)]
                for j, c0 in enumerate((128 * p, 512 + 128 * p, 1024 + 128 * p, 1536 + 128 * p)):
                    S.add("pool", lambda e, j=j, c0=c0: [e.dma_start(
                        out=wr[:, k, 128 * j:128 * j + 128],
                        in_=self.w_in[l, k * 128:(k + 1) * 128, c0:c0 + 128]) for k in range(8)],
                        writes=[("wr", j)], dma_key=("wsl", j), ndma=8)
                for t in range(ntile):
                    for j, (dstT, nm, sc) in enumerate(((rqT, "rqT", 1.0), (rkTa, "rkT", 0.125))):
                        bank = (2 * t + j) % 4

                        def mm(e, t=t, j=j, bank=bank):
                            for k in range(8):
                                ins = e.matmul(ps[:, bank, :], lhsT=wr[:, k, 128 * j:128 * j + 128],
                                               rhs=hT[:, k, t * 512:(t + 1) * 512], start=(k == 0), stop=(k == 7))
                            return ins
                        S.add("pe", mm, reads=[("hT", t), ("wr", j)], writes=[("ps", bank)])
                        S.add("act", lambda e, dstT=dstT, t=t, bank=bank, sc=sc: e.activation(
                            out=dstT[:, t * 512:(t + 1) * 512], in_=ps[:, bank, :], func=AF.Identity, scale=sc),
                            reads=[("ps", bank)], writes=[(nm, t)])
                        if j == 0:
                            ts_ = slice(t * 512, (t + 1) * 512)
                            for dst_, tab_, nm_ in ((qf_all, QF, "qf"), (qb_all, QB, "qb")):
                                S.add("pool", lambda e, dst_=dst_, tab_=tab_, ts_=ts_: e.tensor_tensor(
                                    out=dst_[:, ts_].rearrange("p (c i) -> p c i", c=4),
                                    in0=rqT[:, ts_].rearrange("p (c i) -> p c i", c=4),
                                    in1=tab_[:, p:p + 1, :].broadcast_to([128, 4, 128]), op=ALU.mult),
                                    reads=[("rqT", t)], writes=[(nm_, t)])
                        if j == 1:
                            S.add("act", lambda e, t=t, bank=bank, sc=sc: e.activation(
                                out=rkTb[:, t * 512:(t + 1) * 512], in_=ps[:, bank, :], func=AF.Identity, scale=sc),
                                reads=[("ps", bank)], writes=[("rkTb", t)])
                            S.add("dve", lambda e, t=t: e.memset(rkTa[64:128, t * 512:(t + 1) * 512], 0.0),
                                  reads=[("rkT", t)], writes=[("rkT", t)])
                            S.add("dve", lambda e, t=t: e.memset(rkTb[0:64, t * 512:(t + 1) * 512], 0.0),
                                  reads=[("rkTb", t)], writes=[("rkTb", t)])
                for blk in range(nblk):
                    bank = 4 + blk % 4

                    def mm(e, blk=blk, bank=bank):
                        for k in range(8):
                            ins = e.matmul(ps[:, bank, 0:384], lhsT=hT[:, k, blk * 128:(blk + 1) * 128],
                                           rhs=wr[:, k, 128:512], start=(k == 0), stop=(k == 7))
                        return ins
                    S.add("pe", mm, reads=[("hT", blk // 4), ("wr", 1), ("wr", 2), ("wr", 3)], writes=[("ps", bank)])
                    for hh in range(2):
                        hs_ = slice(64 * hh, 64 * hh + 64)
                        S.add("act", lambda e, blk=blk, bank=bank, hs_=hs_, hh=hh: e.activation(
                            out=kdf_all[:, blk, hs_], in_=ps[:, bank, hs_], func=AF.Identity,
                            scale=KFB[:, 2 * p + hh:2 * p + hh + 1]),
                            reads=[("ps", bank)], writes=[("kdf", blk, hh)])
                        S.add("act", lambda e, blk=blk, bank=bank, hs_=hs_, hh=hh: e.activation(
                            out=kdb_all[:, blk, hs_], in_=ps[:, bank, hs_], func=AF.Identity,
                            scale=KFB[:, 8 + 2 * p + hh:8 + 2 * p + hh + 1]),
                            reads=[("ps", bank)], writes=[("kdb", blk, hh)])
                    S.add("act", lambda e, blk=blk, bank=bank: e.activation(out=rv_tm[:, blk, :], in_=ps[:, bank, 128:256],
                                                                           func=AF.Copy),
                          reads=[("ps", bank)], writes=[("rv_tm", blk)])
                    S.add("act", lambda e, blk=blk, bank=bank: e.activation(out=gate[:, blk, :], in_=ps[:, bank, 256:384],
                                                                           func=AF.Silu),
                          reads=[("ps", bank)], writes=[("gate", blk)])
                S.add("pool", lambda e: e.memset(Sb32, 0.0), writes=["Sb32"])
                S.add("pool", lambda e: e.memset(Sf32, 0.0), writes=["Sf32"])
                S.add("pool", lambda e: e.memset(Sb16a, 0.0), writes=["Sb16a"])
                S.add("pool", lambda e: e.memset(Sb16b, 0.0), writes=["Sb16b"])
                for s_ in range(2):
                    S.add("pool", lambda e, s_=s_: e.memset(Sf16a[s_], 0.0), writes=[("Sf16", s_)])
                    S.add("pool", lambda e, s_=s_: e.memset(Sf16b[s_], 0.0), writes=[("Sf16", s_)])

                def state_update(c, S32, nm, kcol, gcol, ubank):
                    s = c % 2
                    kd_all = kdf_all if kcol == 0 else kdb_all
                    knm = "kdf" if kcol == 0 else "kdb"
                    S.add("pe", lambda e, c=c: e.matmul(ps[:, ubank, 0:128], lhsT=kd_all[:, c, :], rhs=rv_tm[:, c, :],
                                                         start=True, stop=True),
                          reads=[(knm, c, 0), (knm, c, 1), ("rv_tm", c)], writes=[("ps", ubank)])
                    for hh in range(2):
                        rs_ = slice(64 * hh, 64 * hh + 64)
                        S.add("dve", lambda e, rs_=rs_, hh=hh: e.scalar_tensor_tensor(
                            out=S32[rs_, :], in0=S32[rs_, :], scalar=GC[rs_, gcol + p:gcol + p + 1],
                            in1=ps[rs_, ubank, 64 * hh:64 * hh + 64], op0=ALU.mult, op1=ALU.add),
                            reads=[("ps", ubank), (nm, hh)], writes=[(nm, hh)])

                import os as _os2
                RL = int(_os2.environ.get("RET_LEVEL", "2"))
                for c in (range(nblk - 1, -1, -1) if RL >= 1 else []):
                    S.add("dve", lambda e, c=c: e.tensor_copy(out=Sb16a[0:64, c, :], in_=Sb32[0:64, :]),
                          reads=[("Sb32", 0), ("Sb32", 1), "Sb32", "Sb16a"], writes=[("Sb16", c)])
                    S.add("dve", lambda e, c=c: e.tensor_copy(out=Sb16b[64:128, c, :], in_=Sb32[64:128, :]),
                          reads=[("Sb32", 0), ("Sb32", 1), "Sb32", "Sb16b"], writes=[("Sb16", c)])
                    if c > 0:
                        state_update(c, Sb32, "Sb32", 8, 4, 4 + c % 2)
                def stage1(c):
                    s = c % 2
                    cs = slice(c * 128, (c + 1) * 128)
                    S.add("dve", lambda e, s=s: e.tensor_copy(out=Sf16a[s][0:64, :], in_=Sf32[0:64, :]),
                          reads=[("Sf32", 0), ("Sf32", 1), "Sf32"], writes=[("Sf16", s)])
                    S.add("dve", lambda e, s=s: e.tensor_copy(out=Sf16b[s][64:128, :], in_=Sf32[64:128, :]),
                          reads=[("Sf32", 0), ("Sf32", 1), "Sf32"], writes=[("Sf16", s)])
                    sbank = s

                    def mmS(e, cs=cs, sbank=sbank):
                        e.matmul(ps[:, sbank, 0:128], lhsT=rkTa[:, cs], rhs=rqT[:, cs], start=True, stop=True)
                        return e.matmul(ps[:, sbank, 128:256], lhsT=rkTb[:, cs], rhs=rqT[:, cs], start=True, stop=True)
                    S.add("pe", mmS, reads=[("rqT", c // 4), ("rkT", c // 4), ("rkTb", c // 4)], writes=[("ps", sbank)])
                    S.add("dve", lambda e, s=s, sbank=sbank: e.tensor_tensor(
                        out=AT[s], in0=ps[:, sbank, 0:256], in1=DT[:, 2 * p:2 * p + 2, :].rearrange("p a b -> p (a b)"),
                        op=ALU.mult), reads=[("ps", sbank)], writes=[("AT", s)])
                    obank = 2 + s

                    def mmO(e, s=s, c=c, cs=cs, obank=obank):
                        for hh in range(2):
                            rs_ = slice(64 * hh, 64 * hh + 64)
                            o_ = ps[:, obank, 64 * hh:64 * hh + 64]
                            e.matmul(o_, lhsT=AT[s][:, 128 * hh:128 * hh + 128], rhs=rv_tm[:, c, 64 * hh:64 * hh + 64],
                                     start=True, stop=False)
                            e.matmul(o_, lhsT=qf_all[:, cs], rhs=(Sf16a if hh == 0 else Sf16b)[s], start=False, stop=False)
                            ins = e.matmul(o_, lhsT=qb_all[:, cs], rhs=(Sb16a if hh == 0 else Sb16b)[:, c, :], start=False, stop=True)
                        return ins
                    S.add("pe", mmO, reads=[("AT", s), ("rv_tm", c), ("qf", c // 4), ("qb", c // 4), ("Sf16", s), ("Sb16", c)],
                          writes=[("ps", obank)])
                    if c < nblk - 1:
                        state_update(c, Sf32, "Sf32", 0, 0, 4 + s)

                def stage2(c):
                    s = c % 2
                    cs = slice(c * 128, (c + 1) * 128)
                    obank = 2 + s
                    S.add("act", lambda e, s=s, obank=obank: e.activation(out=ob[s], in_=ps[:, obank, 0:128], func=AF.Copy),
                          reads=[("ps", obank)], writes=[("ob", s)])
                    for hh in range(2):
                        S.add("dve", lambda e, s=s, hh=hh: e.bn_stats(out=gst[s][:, hh, :], in_=ob[s][:, 64 * hh:64 * hh + 64]),
                              reads=[("ob", s)], writes=[("gst", s, hh)])
                        S.add("dve", lambda e, s=s, hh=hh: e.bn_aggr(out=gmv[s][:, hh, :], in_=gst[s][:, hh, :]),
                              reads=[("gst", s, hh)], writes=[("gmv", s, hh)])
                    S.add("dve", lambda e, s=s: e.tensor_scalar(out=grs[s], in0=gmv[s][:, :, 1], scalar1=EPS, scalar2=None,
                                                                op0=ALU.add),
                          reads=[("gmv", s, 0), ("gmv", s, 1)], writes=[("grs", s)])
                    S.add("act", lambda e, s=s: e.activation(out=grs[s], in_=grs[s], func=AF.Ln), reads=[("grs", s)], writes=[("grs", s)])
                    S.add("act", lambda e, s=s: e.activation(out=grs[s], in_=grs[s], func=AF.Exp, scale=-0.5),
                          reads=[("grs", s)], writes=[("grs", s)])

                def stage2b(c):
                    s = c % 2
                    cs = slice(c * 128, (c + 1) * 128)
                    for hh in range(2):
                        S.add("dve", lambda e, s=s, hh=hh: e.tensor_scalar(
                            out=ob[s][:, 64 * hh:64 * hh + 64], in0=ob[s][:, 64 * hh:64 * hh + 64],
                            scalar1=gmv[s][:, hh, 0:1], scalar2=grs[s][:, hh:hh + 1], op0=ALU.subtract, op1=ALU.mult),
                            reads=[("ob", s), ("grs", s), ("gmv", s, hh)], writes=[("ob", s)])
                    S.add("pool", lambda e, s=s: e.tensor_tensor(out=ob[s], in0=ob[s], in1=gng[:, 128 * p:128 * p + 128], op=ALU.mult),
                          reads=[("ob", s)], writes=[("ob", s)])
                    S.add("pool", lambda e, s=s: e.tensor_tensor(out=ob[s], in0=ob[s], in1=gnb[:, 128 * p:128 * p + 128], op=ALU.add),
                          reads=[("ob", s)], writes=[("ob", s)])
                    S.add("pool", lambda e, s=s, c=c: e.tensor_tensor(out=rob[s], in0=ob[s], in1=gate[:, c, :], op=ALU.mult),
                          reads=[("ob", s), ("gate", c)], writes=[("rob", s)])
                    tbank = 6 + s
                    S.add("pe", lambda e, s=s, tbank=tbank: e.transpose(out=psb[:, tbank, 0:128], in_=rob[s], identity=ident),
                          reads=[("rob", s)], writes=[("ps", tbank)])
                    S.add("act", lambda e, cs=cs, tbank=tbank: e.activation(out=moT[:, cs], in_=psb[:, tbank, 0:128], func=AF.Copy),
                          reads=[("ps", tbank)], writes=[("moT", c // 4)])

                if RL >= 2:
                    stage1(0)
                    for c in range(nblk):
                        stage2(c)
                        if c + 1 < nblk:
                            stage1(c + 1)
                        stage2b(c)
                import os as _os
                if _os.environ.get("RET_NOMO") != "1":
                    S.add("sp", lambda e, p=p, T=T, tok0=tok0: e.dma_start(out=self.MO[p, :, tok0:tok0 + T], in_=moT),
                          reads=[("moT", t) for t in range(ntile)], writes=[("MO", p)], dma_key=("mo", p % 2))
                S.barrier()

        def do_att(h, T, tok0):
            nblk = T // 128
            ntile = T // 512
            if True:
                A.off = A2
                m_h = slopes[h]
                wa = A.alloc([8, 384], BF16)
                qT = A.alloc([T], BF16)
                kTa = A.alloc([T], BF16)
                kTb = A.alloc([T], BF16)
                V = A.alloc([nblk, 128], BF16)
                moT = A.alloc([T], BF16)
                tmp = [A.alloc([512], F32) for _ in range(4)]
                PT = [A.alloc([512], BF16) for _ in range(4)]
                rd = [A.alloc([512], F32) for _ in range(2)]
                tt = [A.alloc([512], F32) for _ in range(2)]
                deferred = []
                att = A.alloc([512], F32)
                sq = A.alloc([512], BF16)
                rr = A.alloc([512], F32)
                for j in range(3):
                    c0 = 2048 + 512 * j + 128 * h
                    S.add("pool", lambda e, j=j, c0=c0: [e.dma_start(
                        out=wa[:, k, 128 * j:128 * j + 128],
                        in_=self.w_in[l, k * 128:(k + 1) * 128, c0:c0 + 128]) for k in range(8)],
                        writes=[("wa", j)], dma_key=("wsl", j), ndma=8)
                for t in range(ntile):
                    for j, (dstT, nm) in enumerate(((qT, "qT"), (kTa, "kT"))):
                        bank = (2 * t + j) % 4

                        def mm(e, t=t, j=j, bank=bank):
                            for k in range(8):
                                ins = e.matmul(ps[:, bank, :], lhsT=wa[:, k, 128 * j:128 * j + 128],
                                               rhs=hT[:, k, t * 512:(t + 1) * 512], start=(k == 0), stop=(k == 7))
                            return ins
                        S.add("pe", mm, reads=[("hT", t), ("wa", j)], writes=[("ps", bank)])
                        S.add("act", lambda e, dstT=dstT, t=t, bank=bank: e.activation(
                            out=dstT[:, t * 512:(t + 1) * 512], in_=ps[:, bank, :], func=AF.Copy),
                            reads=[("ps", bank)], writes=[(nm, t)])
                        if j == 1:
                            S.add("act", lambda e, t=t, bank=bank: e.activation(
                                out=kTb[:, t * 512:(t + 1) * 512], in_=ps[:, bank, :], func=AF.Copy),
                                reads=[("ps", bank)], writes=[("kTb", t)])
                            S.add("dve", lambda e, t=t: e.memset(kTa[64:128, t * 512:(t + 1) * 512], 0.0),
                                  reads=[("kT", t)], writes=[("kT", t)])
                            S.add("dve", lambda e, t=t: e.memset(kTb[0:64, t * 512:(t + 1) * 512], 0.0),
                                  reads=[("kTb", t)], writes=[("kTb", t)])
                for blk in range(nblk):
                    bank = 4 + blk % 4

                    def mm(e, blk=blk, bank=bank):
                        for k in range(8):
                            ins = e.matmul(ps[:, bank, 0:128], lhsT=hT[:, k, blk * 128:(blk + 1) * 128],
                                           rhs=wa[:, k, 256:384], start=(k == 0), stop=(k == 7))
                        return ins
                    S.add("pe", mm, reads=[("hT", blk // 4), ("wa", 2)], writes=[("ps", bank)])
                    S.add("dve", lambda e, blk=blk, bank=bank: e.tensor_copy(out=V[:, blk, :], in_=ps[:, bank, 0:128]),
                          reads=[("ps", bank)], writes=[("V", blk)])

                itc = [0]

                def post_chain(qt):
                    qs = slice(qt * 512, (qt + 1) * 512)
                    ops = []
                    for mp in range(2):
                        ops.append(lambda mp=mp: S.add("dve", lambda e: e.reciprocal(out=rd[mp], in_=rd[mp]),
                                                       reads=[("rd", mp)], writes=[("rd", mp)]))
                        ops.append(lambda mp=mp: S.add("dve", lambda e: e.tensor_tensor(out=tt[mp], in0=tt[mp], in1=rd[mp], op=ALU.mult),
                                                       reads=[("tt", mp), ("rd", mp)], writes=[("tt", mp)]))
                    ops.append(lambda: S.add("dve", lambda e: e.scalar_tensor_tensor(
                        out=att, in0=tt[1], scalar=neglam, in1=tt[0], op0=ALU.mult, op1=ALU.add),
                        reads=[("tt", 0), ("tt", 1)], writes=["att"]))
                    ops.append(lambda: S.add("act", lambda e: e.activation(out=sq, in_=att, func=AF.Square),
                                             reads=["att"], writes=["sq"]))

                    def ssq():
                        qbank = 2 * (itc[0] % 2)
                        S.add("pe", lambda e: e.matmul(ps[:, qbank, :], lhsT=self.ones_bf, rhs=sq, start=True, stop=True),
                              reads=["sq"], writes=[("ps", qbank)])
                        S.add("act", lambda e: e.activation(out=rr, in_=ps[:, qbank, :], func=AF.Ln, bias=self.eps_c,
                                                            scale=1.0 / 128),
                              reads=[("ps", qbank)], writes=["rr"])
                    ops.append(ssq)
                    ops.append(lambda: S.add("act", lambda e: e.activation(out=rr, in_=rr, func=AF.Exp, scale=-0.5),
                                             reads=["rr"], writes=["rr"]))
                    ops.append(lambda: S.add("dve", lambda e: e.tensor_tensor(out=att, in0=att, in1=rr, op=ALU.mult),
                                             reads=["att", "rr"], writes=["att"]))
                    ops.append(lambda: S.add("dve", lambda e: e.tensor_scalar(out=moT[:, qs], in0=att, scalar1=sg, scalar2=None,
                                                                              op0=ALU.mult),
                                             reads=["att"], writes=[("moT", qt)]))
                    return ops

                pending_pv = []

                def flush_pv():
                    while pending_pv:
                        pending_pv.pop(0)()

                for qt in range(ntile):
                    qs = slice(qt * 512, (qt + 1) * 512)
                    for kc in range(nblk):
                        ks = slice(kc * 128, (kc + 1) * 128)
                        par = itc[0] % 2
                        itc[0] += 1
                        dk_ = 4 * qt - kc
                        for mp in range(2):
                            bank = 2 * par + mp
                            sl = 2 * par + mp
                            rs_ = slice(64 * mp, 64 * mp + 64)
                            kTm = kTa if mp == 0 else kTb
                            S.add("pe", lambda e, bank=bank, kTm=kTm, ks=ks, qs=qs: e.matmul(
                                ps[:, bank, :], lhsT=kTm[:, ks], rhs=qT[:, qs], start=True, stop=True),
                                reads=[("kT", kc // 4), ("kTb", kc // 4), ("qT", qt)], writes=[("ps", bank)])
                            if dk_ > 0:
                                tbl, c1, c2 = Eq, -m_h / SCALE, -m_h * 128.0 * dk_
                            elif dk_ <= -4:
                                tbl, c1, c2 = Eq, m_h / SCALE, m_h * 128.0 * dk_
                            else:
                                tbl, c1, c2 = Dg[:, -dk_, :], -m_h / SCALE, 0.0
                            S.add("dve", lambda e, sl=sl, bank=bank, tbl=tbl, c1=c1: e.scalar_tensor_tensor(
                                out=tmp[sl], in0=tbl, scalar=c1, in1=ps[:, bank, :], op0=ALU.mult, op1=ALU.add),
                                reads=[("ps", bank)], writes=[("tmp", sl)])
                            S.add("act", lambda e, sl=sl, c2=c2: e.activation(out=PT[sl], in_=tmp[sl], func=AF.Exp,
                                                                              bias=float(c2), scale=SCALE),
                                  reads=[("tmp", sl)], writes=[("PT", sl)])

                        def mmPV(e, par=par, kc=kc, nblk=nblk):
                            for mp in range(2):
                                e.matmul(ps[:, 4 + mp, :], lhsT=V[:, kc, :], rhs=PT[2 * par + mp], start=(kc == 0), stop=(kc == nblk - 1))
                                ins = e.matmul(ps[:, 6 + mp, :], lhsT=self.ones_bf, rhs=PT[2 * par + mp], start=(kc == 0),
                                               stop=(kc == nblk - 1))
                            return ins
                        flush_pv()
                        pending_pv.append(lambda mmPV=mmPV, par=par, kc=kc: S.add(
                            "pe", mmPV, reads=[("PT", 2 * par), ("PT", 2 * par + 1), ("V", kc)],
                            writes=[("ps", 4), ("ps", 5), ("ps", 6), ("ps", 7)]))
                        if kc >= 1 and deferred:
                            deferred.pop(0)()
                    flush_pv()
                    while deferred:
                        deferred.pop(0)()
                    for mp in range(2):
                        S.add("act", lambda e, mp=mp: e.activation(out=tt[mp], in_=ps[:, 4 + mp, :], func=AF.Copy),
                              reads=[("ps", 4 + mp)], writes=[("tt", mp)])
                        S.add("act", lambda e, mp=mp: e.activation(out=rd[mp], in_=ps[:, 6 + mp, :], func=AF.Copy),
                              reads=[("ps", 6 + mp)], writes=[("rd", mp)])
                    deferred.extend(post_chain(qt))
                while deferred:
                    deferred.pop(0)()
                S.add("sp", lambda e, h=h, T=T, tok0=tok0: e.dma_start(out=self.MO[4 + h, :, tok0:tok0 + T], in_=moT),
                      reads=[("moT", t) for t in range(ntile)], writes=[("MO", 4 + h)], dma_key=("mo", h % 2))
                S.barrier()

        def do_b4(T, tok0):
            import os as _os
            nblk = T // 128
            ntile = T // 512
            A.off = A2
            wout = A.alloc([8, D], BF16)
            lng = A.alloc([D], F32)
            lnb = A.alloc([D], F32)
            S.add("pool", lambda e: e.dma_start(out=wout, in_=self.w_out[l].rearrange("(k p) n -> p k n", p=128)),
                  writes=["wout"], dma_key="wout")
            S.add("sp", lambda e: e.dma_start(out=lng, in_=self.ln[(2, "g")][l:l + 1, :].broadcast_to([128, D])),
                  writes=["lng"], dma_key="lng")
            S.add("sp", lambda e: e.dma_start(out=lnb, in_=self.ln[(2, "b")][l:l + 1, :].broadcast_to([128, D])),
                  writes=["lnb"], dma_key="lnb")
            mot = [A.alloc([8, 128], BF16) for _ in range(2)]
            xres = [A.alloc([D], F32) for _ in range(2)]
            yb = [A.alloc([D], F32) for _ in range(2)]
            scr = [dict(st=A.alloc([2, 6], F32), mv=A.alloc([2], F32), rs=A.alloc([1], F32)) for _ in range(2)]
            for blk in range(nblk):
                s = blk % 2
                rows = slice(tok0 + blk * 128, tok0 + (blk + 1) * 128)
                if _os.environ.get("NOMO") == "1":
                    S.add("pool", lambda e, s=s: e.memset(mot[s], 0.5), writes=[("mot", s)])
                else:
                    S.add("sp", lambda e, s=s, rows=rows: [e.dma_start(out=mot[s][:, k, :], in_=self.MO[k, :, rows])
                                                           for k in range(8)],
                          writes=[("mot", s)], dma_key=("mot", s), ndma=8)
                S.add("sp", lambda e, s=s, rows=rows: e.dma_start(out=xres[s], in_=src[rows, :]),
                      writes=[("xres", s)], dma_key=("xres", s))
                b0 = 2 * (blk % 4)

                def mm(e, s=s, b0=b0):
                    for half in range(2):
                        for k in range(8):
                            ins = e.matmul(ps[:, b0 + half, :], lhsT=mot[s][:, k, :], rhs=wout[:, k, half * 512:(half + 1) * 512],
                                           start=(k == 0), stop=(k == 7))
                    return ins
                S.add("pe", mm, reads=[("mot", s), "wout"], writes=[("ps", b0), ("ps", b0 + 1)])
                ysb = yb[s]
                pso = ps[:, b0:b0 + 2, :].rearrange("p a b -> p (a b)")
                S.add("act", lambda e, ysb=ysb, pso=pso: e.activation(out=ysb, in_=pso, func=AF.Copy),
                      reads=[("ps", b0), ("ps", b0 + 1)], writes=[("my", s)])
                S.add("dve", lambda e, ysb=ysb, s=s: e.scalar_tensor_tensor(
                    out=ysb, in0=xres[s], scalar=ALPHA, in1=ysb, op0=ALU.mult, op1=ALU.add),
                    reads=[("my", s), ("xres", s)], writes=[("my", s)])
                self.ln_block(ysb, s, lng, lnb, scr[s], "m")
                S.add("pool", lambda e, ysb=ysb, rows=rows: e.dma_start(out=dst[rows, :], in_=ysb),
                      reads=[("my", s)], dma_key=("yst", s))
            S.barrier()

        tok0 = 0
        for si, T in enumerate(self.seq_lens):
            import os as _os
            if self.mix_stage >= 1 and _os.environ.get("NOB1") != "1":
                do_b1(T, tok0)
            if self.mix_stage >= 2:
                for p in range(4):
                    do_ret(p, T, tok0)
            if self.mix_stage >= 3:
                for h in range(4):
                    do_att(h, T, tok0)
            if self.mix_stage >= 1 and _os.environ.get("NOB4") != "1":
                do_b4(T, tok0)
            tok0 += T
        S.barrier(new_set=True)


_PARAM_NAMES = ["w_in", "w_out", "ret_decay_f", "ret_decay_b", "ret_gn_g", "ret_gn_b",
                "diff_lq1", "diff_lk1", "diff_lq2", "diff_lk2", "diff_subln_g",
                "ffn1_w13", "ffn1_w2", "ffn2_w13", "ffn2_w2",
                "ln1_g", "ln1_b", "ln2_g", "ln2_b", "ln3_g", "ln3_b"]


def extra_layouts(params):
    f, b = params["ret_decay_f"], params["ret_decay_b"]
    dec_all = np.ascontiguousarray(np.concatenate([f, b], axis=1))
    dec_pairs = np.ascontiguousarray(np.stack(
        [np.concatenate([f[:, 0::2], b[:, 0::2]], axis=1), np.concatenate([f[:, 1::2], b[:, 1::2]], axis=1)], axis=1))
    return {"dec_all": dec_all, "dec_pairs": dec_pairs}


def kernel(**inputs):
    xp = np.asarray(inputs["x_prompt"], dtype=np.float32)
    xs = np.asarray(inputs["x_sample"], dtype=np.float32)
    params = {k: np.ascontiguousarray(np.asarray(inputs[k], dtype=np.float32)) for k in _PARAM_NAMES}
    params.update(extra_layouts(params))
    nc = Builder(SEQ_LENS).build()
    in_maps = []
    for c in range(N_CORES):
        xc = np.concatenate([xp[c].reshape(-1, D), xs[4 * c:4 * c + 4].reshape(-1, D)], axis=0)
        m = {"x": np.ascontiguousarray(xc)}
        m.update(params)
        in_maps.append(m)
    res = run_bass_kernel_spmd(nc, in_maps, core_ids=list(range(N_CORES)))
    yp = np.empty_like(xp)
    ys = np.empty_like(xs)
    for c in range(N_CORES):
        yc = np.asarray(res.results[c]["y"], dtype=np.float32)
        yp[c] = yc[:4096]
        ys[4 * c:4 * c + 4] = yc[4096:].reshape(4, 2048, D)
    return (yp, ys)
```

```python
import math
from contextlib import ExitStack

import numpy as np
import concourse.bass as bass
import concourse.mybir as mybir
from concourse.bass_utils import run_bass_kernel_spmd

F32 = mybir.dt.float32
BF16 = mybir.dt.bfloat16
U8 = mybir.dt.uint8
AF = mybir.ActivationFunctionType
ALU = mybir.AluOpType
AX = mybir.AxisListType

D = 1024
DFF = 2816
NFF = DFF // 128
INW = 3584
DEPTH = 2
EPS = 1e-5
ALPHA = (2 * DEPTH) ** 0.25
N_CORES = 8
SEQ_LENS = (4096, 2048, 2048, 2048, 2048)
ENGS = ("pe", "act", "dve", "pool", "sp")


def lambda_init(layer):
    return 0.8 - 0.6 * math.exp(-0.3 * layer)


class Op:
    __slots__ = ("eng", "fn", "dma_key", "deps", "inc", "val", "idx", "is_dma", "sset", "kind", "ndma")

    def __init__(self, eng, fn, dma_key):
        self.eng = eng
        self.fn = fn
        self.dma_key = dma_key
        self.is_dma = dma_key is not None
        self.deps = []
        self.inc = False
        self.val = None
        self.sset = 0
        self.kind = "op"
        self.ndma = 1


class Sched:
    def __init__(self):
        self.ops = []
        self.last_w = {}
        self.readers = {}
        self.final_dma = []
        self.sset = 0
        self.last_on_eng = {}
        self.last_dma = {}

    def add(self, eng, fn, reads=(), writes=(), dma_key=None, final=False, ndma=1):
        op = Op(eng, fn, dma_key)
        op.ndma = ndma
        op.idx = len(self.ops)
        op.sset = self.sset
        deps = {}
        for k in reads:
            w = self.last_w.get(k)
            if w is not None:
                deps[w.idx] = (w, True)
        for k in writes:
            w = self.last_w.get(k)
            if w is not None and w.idx not in deps:
                deps[w.idx] = (w, False)
            for r in self.readers.get(k, ()):
                if r.idx not in deps:
                    deps[r.idx] = (r, False)
        op.deps = [deps[i] for i in sorted(deps)]
        for k in reads:
            self.readers.setdefault(k, []).append(op)
        for k in writes:
            self.last_w[k] = op
            self.readers[k] = []
        self.ops.append(op)
        self.last_on_eng[eng] = op
        if op.is_dma:
            self.last_dma[dma_key] = op
        if final:
            self.final_dma.append(op)
        return op

    def barrier(self, new_set=False):
        drains = []
        for e in ENGS:
            op = self.add(e, lambda eng: eng.drain())
            op.kind = "drain"
            drains.append(op)
        dmas = list(self.last_dma.values())
        for e in ENGS:
            op = self.add(e, None)
            op.kind = "join"
            op.deps = [(d, True) for d in drains if d.eng != e] + [(d, True) for d in dmas]
        self.last_w = {}
        self.readers = {}
        self.last_dma = {}
        if new_set:
            self.sset += 1

    @staticmethod
    def _need(op, d, raw):
        if d.is_dma or d.eng != op.eng:
            return True
        if op.eng == "pe":
            return False
        return True

    def emit(self, nc):
        for op in self.ops:
            for d, raw in op.deps:
                if self._need(op, d, raw):
                    d.inc = True
        for op in self.final_dma:
            op.inc = True
        nsets = self.sset + 1
        cnt = {(e, s): 0 for e in ENGS for s in range(nsets)}
        dcnt = {}
        for op in self.ops:
            if op.is_dma:
                dcnt[op.dma_key] = dcnt.get(op.dma_key, 0) + 16 * op.ndma
                op.val = dcnt[op.dma_key]
            elif op.inc:
                cnt[(op.eng, op.sset)] += 1
                op.val = cnt[(op.eng, op.sset)]
        self.stats = dict(maxcnt=max(cnt.values()), ndma_keys=len(dcnt), nops=len(self.ops),
                          maxd=max(dcnt.values()) if dcnt else 0)
        with ExitStack() as st:
            esem = {k: st.enter_context(nc.semaphore("s_%s%d" % k)) for k in cnt}
            dsem = {k: st.enter_context(nc.semaphore("d%d" % i)) for i, k in enumerate(dcnt)}
            block = st.enter_context(nc.Block())

            def semof(o):
                return dsem[o.dma_key] if o.is_dma else esem[(o.eng, o.sset)]

            def run(engname, e):
                waited = {}
                for op in self.ops:
                    if op.eng != engname:
                        continue
                    need = {}
                    for d, raw in op.deps:
                        if not self._need(op, d, raw):
                            continue
                        s = semof(d)
                        key = id(s)
                        if key not in need or need[key][1] < d.val:
                            need[key] = (s, d.val)
                    for key, (s, v) in need.items():
                        if waited.get(key, 0) >= v:
                            continue
                        waited[key] = v
                        e.wait_ge(s, v)
                    if op.fn is None:
                        continue
                    ins = op.fn(e)
                    if op.is_dma:
                        for i_ in (ins if isinstance(ins, list) else [ins]):
                            i_.then_inc(dsem[op.dma_key], 16)
                    elif op.inc:
                        ins.then_inc(esem[(op.eng, op.sset)], 1)
                if engname == "sp":
                    for k, v in dcnt.items():
                        e.wait_ge(dsem[k], v)

            block.tensor(lambda e: run("pe", e))
            block.scalar(lambda e: run("act", e))
            block.vector(lambda e: run("dve", e))
            block.gpsimd(lambda e: run("pool", e))
            block.sync(lambda e: run("sp", e))


class Arena:
    def __init__(self, handle, nbytes):
        self.h = handle
        self.n = nbytes
        self.off = 0
        self.mark_ = 0

    def alloc(self, free_shape, dt):
        esz = 2 if dt == BF16 else 4
        n = 1
        for s in free_shape:
            n *= s
        nb = (n * esz + 31) // 32 * 32
        assert self.off + nb <= self.n, ("arena overflow", self.off, nb, self.n)
        v = self.h[:, self.off:self.off + n * esz].bitcast(dt)
        self.off += nb
        if len(free_shape) == 2:
            v = v.rearrange("p (a b) -> p a b", a=free_shape[0])
        elif len(free_shape) == 3:
            v = v.rearrange("p (a b c) -> p a b c", a=free_shape[0], b=free_shape[1])
        return v

    def mark(self):
        self.mark_ = self.off

    def reset(self):
        self.off = self.mark_


class Builder:
    def __init__(self, seq_lens, depth=DEPTH, do_mixer=True, do_ffn2=True, mix_stage=3):
        self.mix_stage = mix_stage
        self.seq_lens = tuple(seq_lens)
        self.NT = sum(seq_lens)
        self.depth = depth
        self.do_mixer = do_mixer
        self.do_ffn2 = do_ffn2
        self.nc = bass.Bass("TRN2", target_bir_lowering=False)
        self.S = Sched()

    def dram_in(self, name, shape, dt=F32):
        return self.nc.dram_tensor(name, list(shape), dt, kind="ExternalInput").ap()

    def build(self):
        nc, S = self.nc, self.S
        NT = self.NT
        self.x = self.dram_in("x", [NT, D])
        self.y = nc.dram_tensor("y", [NT, D], F32, kind="ExternalOutput").ap()
        self.w_in = self.dram_in("w_in", [DEPTH, D, INW])
        self.w_out = self.dram_in("w_out", [DEPTH, D, D])
        self.dec_f = self.dram_in("ret_decay_f", [DEPTH, 8])
        self.dec_b = self.dram_in("ret_decay_b", [DEPTH, 8])
        self.gn_g = self.dram_in("ret_gn_g", [DEPTH, 512])
        self.gn_b = self.dram_in("ret_gn_b", [DEPTH, 512])
        self.lq1 = self.dram_in("diff_lq1", [DEPTH, 64])
        self.lk1 = self.dram_in("diff_lk1", [DEPTH, 64])
        self.lq2 = self.dram_in("diff_lq2", [DEPTH, 64])
        self.lk2 = self.dram_in("diff_lk2", [DEPTH, 64])
        self.subln = self.dram_in("diff_subln_g", [DEPTH, 128])
        self.dec_all = self.dram_in("dec_all", [DEPTH, 16])
        self.dec_pairs = self.dram_in("dec_pairs", [DEPTH, 2, 8])
        self.f1w13 = self.dram_in("ffn1_w13", [DEPTH, D, 2 * DFF])
        self.f1w2 = self.dram_in("ffn1_w2", [DEPTH, DFF, D])
        self.f2w13 = self.dram_in("ffn2_w13", [DEPTH, D, 2 * DFF])
        self.f2w2 = self.dram_in("ffn2_w2", [DEPTH, DFF, D])
        self.ln = {}
        for i in (1, 2, 3):
            self.ln[(i, "g")] = self.dram_in("ln%d_g" % i, [DEPTH, D])
            self.ln[(i, "b")] = self.dram_in("ln%d_b" % i, [DEPTH, D])
        self.XA = nc.dram_tensor("xa_scr", [NT, D], F32, kind="Internal").ap()
        self.XB = nc.dram_tensor("xb_scr", [NT, D], F32, kind="Internal").ap()
        self.MO = nc.dram_tensor("mo_scr", [8, 128, NT], BF16, kind="Internal").ap()

        with ExitStack() as st:
            nbytes = 206000
            arena_h = st.enter_context(nc.sbuf_tensor("arena", [128, nbytes], U8))
            self.A = Arena(arena_h, nbytes)
            self.ps = st.enter_context(nc.psum_tensor("ps", [128, 8, 512], F32))
            self.psb = self.ps[:].bitcast(BF16)
            self.setup_consts()
            self.A.mark()
            cur = self.x
            for l in range(self.depth):
                last = (l == self.depth - 1)
                self.phase_ffn(l, self.f1w13, self.f1w2, self.ln[(1, "g")], self.ln[(1, "b")], cur, self.XA)
                cur = self.XA
                if self.do_mixer:
                    self.phase_mixer(l, self.XA, self.XB)
                    cur = self.XB
                if self.do_ffn2:
                    dst = self.y if last else self.XA
                    src = cur
                    if src is self.XA:
                        dst = self.y if last else self.XB
                    self.phase_ffn(l, self.f2w13, self.f2w2, self.ln[(3, "g")], self.ln[(3, "b")], src, dst)
                    cur = dst
            if cur is not self.y:
                self.copy_out(cur)
            S.emit(nc)
        return nc

    def setup_consts(self):
        S, A = self.S, self.A
        self.ident = A.alloc([128], BF16)
        identf = A.alloc([128], F32)
        self.ones_bf = A.alloc([128], BF16)
        self.ones_f = A.alloc([128], F32)
        self.eps_c = A.alloc([1], F32)
        S.add("pool", lambda e: e.memset(identf, 0.0), writes=["identf"])
        S.add("pool", lambda e: e.affine_select(out=identf, in_=identf, pattern=[[-1, 128]],
                                                compare_op=ALU.not_equal, fill=1.0, base=0,
                                                channel_multiplier=1), reads=["identf"], writes=["identf"])
        S.add("pool", lambda e: e.tensor_copy(out=self.ident, in_=identf), reads=["identf"], writes=["ident"])
        S.add("pool", lambda e: e.memset(self.ones_bf, 1.0), writes=["ones_bf"])
        S.add("pool", lambda e: e.memset(self.ones_f, 1.0), writes=["ones_f"])
        S.add("pool", lambda e: e.memset(self.eps_c, EPS), writes=["eps_c"])
        S.barrier()

    def copy_out(self, cur):
        S, A = self.S, self.A
        A.reset()
        buf = [A.alloc([D], F32) for _ in range(2)]
        for blk in range(self.NT // 128):
            s = blk % 2
            rows = slice(blk * 128, (blk + 1) * 128)
            S.add("sp", lambda e, s=s, rows=rows: e.dma_start(out=buf[s], in_=cur[rows, :]),
                  writes=[("cb", s)], dma_key=("cb", s))
            S.add("sp", lambda e, s=s, rows=rows: e.dma_start(out=self.y[rows, :], in_=buf[s]),
                  reads=[("cb", s)], dma_key=("co", s), final=True)
        S.barrier()

    def ln_block(self, ysb, slot, lng, lnb, scr, pre):
        S = self.S
        st, mv, rs = scr["st"], scr["mv"], scr["rs"]
        ky = (pre + "y", slot)
        kst, kmv, krs = (pre + "st", slot), (pre + "mv", slot), (pre + "rs", slot)
        S.add("dve", lambda e: e.bn_stats(out=st[:, 0, :], in_=ysb[:, 0:512]), reads=[ky], writes=[(kst, 0)])
        S.add("dve", lambda e: e.bn_stats(out=st[:, 1, :], in_=ysb[:, 512:1024]), reads=[ky], writes=[(kst, 1)])
        S.add("dve", lambda e: e.bn_aggr(out=mv, in_=st), reads=[(kst, 0), (kst, 1)], writes=[kmv])
        S.add("dve", lambda e: e.tensor_scalar(out=rs, in0=mv[:, 1:2], scalar1=EPS, scalar2=None, op0=ALU.add),
              reads=[kmv], writes=[krs])
        S.add("act", lambda e: e.activation(out=rs, in_=rs, func=AF.Sqrt), reads=[krs], writes=[krs])
        S.add("dve", lambda e: e.reciprocal(out=rs, in_=rs), reads=[krs], writes=[krs])
        S.add("dve", lambda e: e.tensor_scalar(out=ysb, in0=ysb, scalar1=mv[:, 0:1], scalar2=rs,
                                               op0=ALU.subtract, op1=ALU.mult),
              reads=[ky, kmv, krs], writes=[ky])
        S.add("pool", lambda e: e.tensor_tensor(out=ysb, in0=ysb, in1=lng, op=ALU.mult),
              reads=[ky, "lng"], writes=[ky])
        S.add("pool", lambda e: e.tensor_tensor(out=ysb, in0=ysb, in1=lnb, op=ALU.add),
              reads=[ky, "lnb"], writes=[ky])

    def phase_ffn(self, l, w13_d, w2_d, g_d, b_d, src, dst):
        S, A, ps, psb = self.S, self.A, self.ps, self.psb
        A.reset()
        W13 = A.alloc([8, 2 * DFF], BF16)
        W2 = A.alloc([NFF, D], BF16)
        lng = A.alloc([D], F32)
        lnb = A.alloc([D], F32)
        xres = [A.alloc([D], F32) for _ in range(2)]
        xbf = [A.alloc([D], BF16) for _ in range(4)]
        xT = A.alloc([8, 512], BF16)
        gT = A.alloc([NFF, 512], BF16)
        sa = [A.alloc([512], F32) for _ in range(2)]
        yb = [A.alloc([D], F32) for _ in range(2)]
        scr = [dict(st=A.alloc([2, 6], F32), mv=A.alloc([2], F32), rs=A.alloc([1], F32)) for _ in range(2)]
        ident = self.ident
        final = dst is self.y

        for j in range(11):
            cs = slice(j * 512, (j + 1) * 512)
            S.add("pool", lambda e, cs=cs: e.dma_start(
                out=W13[:, :, cs], in_=w13_d[l, :, cs].rearrange("(k p) n -> p k n", p=128)),
                writes=[("W13", j)], dma_key=("W13", j))
        for j in range(11):
            S.add("pool", lambda e, j=j: e.dma_start(
                out=W2[:, 2 * j:2 * j + 2, :],
                in_=w2_d[l, 256 * j:256 * j + 256, :].rearrange("(k p) n -> p k n", p=128)),
                writes=[("W2", j)], dma_key=("W2", j))
        S.add("sp", lambda e: e.dma_start(out=lng, in_=g_d[l:l + 1, :].broadcast_to([128, D])),
              writes=["lng"], dma_key="lng")
        S.add("sp", lambda e: e.dma_start(out=lnb, in_=b_d[l:l + 1, :].broadcast_to([128, D])),
              writes=["lnb"], dma_key="lnb")

        ntile = self.NT // 512

        def load_tile(t):
            for b in range(4):
                rows = slice((4 * t + b) * 128, (4 * t + b + 1) * 128)
                S.add("pool", lambda e, b=b, rows=rows: e.dma_start(out=xbf[b], in_=src[rows, :]),
                      writes=[("xbf", b)], dma_key=("xbf", b))

        def transposes(t):
            for k in range(8):
                bank = 6 + (k % 2)

                def tr(e, k=k, bank=bank):
                    for b in range(4):
                        ins = e.transpose(out=psb[:, bank, b * 128:(b + 1) * 128],
                                          in_=xbf[b][:, k * 128:(k + 1) * 128], identity=ident)
                    return ins
                S.add("pe", tr, reads=[("xbf", b) for b in range(4)] + ["ident"], writes=[("ps", bank)])
                if k % 2 == 0:
                    S.add("act", lambda e, k=k, bank=bank: e.activation(out=xT[:, k, :], in_=psb[:, bank, 0:512],
                                                                       func=AF.Copy),
                          reads=[("ps", bank)], writes=[("xT", k)])
                else:
                    S.add("dve", lambda e, k=k, bank=bank: e.tensor_copy(out=xT[:, k, :], in_=psb[:, bank, 0:512]),
                          reads=[("ps", bank)], writes=[("xT", k)])

        def ffn1(t):
            for c in range(NFF):
                q = c % 2
                ba, bb = 2 * q, 2 * q + 1

                def mm(e, c=c, ba=ba, bb=bb):
                    for k in range(8):
                        e.matmul(ps[:, ba, :], lhsT=W13[:, k, c * 128:(c + 1) * 128], rhs=xT[:, k, :],
                                 start=(k == 0), stop=(k == 7))
                    for k in range(8):
                        ins = e.matmul(ps[:, bb, :], lhsT=W13[:, k, DFF + c * 128:DFF + (c + 1) * 128],
                                       rhs=xT[:, k, :], start=(k == 0), stop=(k == 7))
                    return ins
                S.add("pe", mm, reads=[("xT", k) for k in range(8)] + [("W13", c // 4), ("W13", (NFF + c) // 4)],
                      writes=[("ps", ba), ("ps", bb)])
                S.add("act", lambda e, q=q, ba=ba: e.activation(out=sa[q], in_=ps[:, ba, :], func=AF.Silu),
                      reads=[("ps", ba)], writes=[("sa", q)])
                S.add("dve", lambda e, c=c, q=q, bb=bb: e.tensor_tensor(out=gT[:, c, :], in0=sa[q], in1=ps[:, bb, :],
                                                                       op=ALU.mult),
                      reads=[("sa", q), ("ps", bb)], writes=[("gT", c)])

        def ffn2(t):
            for b in range(4):
                blk = 4 * t + b
                slot = blk % 2
                rows = slice(blk * 128, (blk + 1) * 128)
                b0 = 4 if b % 2 == 0 else 6
                S.add("sp", lambda e, slot=slot, rows=rows: e.dma_start(out=xres[slot], in_=src[rows, :]),
                      writes=[("xres", slot)], dma_key=("xres", slot))

                def mm(e, b=b, b0=b0):
                    for half in range(2):
                        for c in range(NFF):
                            ins = e.matmul(ps[:, b0 + half, :], lhsT=gT[:, c, b * 128:(b + 1) * 128],
                                           rhs=W2[:, c, half * 512:(half + 1) * 512],
                                           start=(c == 0), stop=(c == NFF - 1))
                    return ins
                S.add("pe", mm, reads=[("gT", c) for c in range(NFF)] + [("W2", j) for j in range(11)],
                      writes=[("ps", b0), ("ps", b0 + 1)])
                ysb = yb[slot]
                pso = ps[:, b0:b0 + 2, :].rearrange("p a b -> p (a b)")
                S.add("act", lambda e, ysb=ysb, pso=pso: e.activation(out=ysb, in_=pso, func=AF.Identity, scale=0.5),
                      reads=[("ps", b0), ("ps", b0 + 1)], writes=[("fy", slot)])
                S.add("dve", lambda e, ysb=ysb, slot=slot: e.scalar_tensor_tensor(
                    out=ysb, in0=xres[slot], scalar=ALPHA, in1=ysb, op0=ALU.mult, op1=ALU.add),
                    reads=[("fy", slot), ("xres", slot)], writes=[("fy", slot)])
                self.ln_block(ysb, slot, lng, lnb, scr[slot], "f")
                S.add("pool", lambda e, ysb=ysb, rows=rows: e.dma_start(out=dst[rows, :], in_=ysb),
                      reads=[("fy", slot)], dma_key=("yst", slot), final=final)

        load_tile(0)
        transposes(0)
        for t in range(ntile):
            if t + 1 < ntile:
                load_tile(t + 1)
            ffn1(t)
            if t + 1 < ntile:
                transposes(t + 1)
            ffn2(t)
        S.barrier(new_set=True)

    def phase_mixer(self, l, src, dst):
        S, A, ps, psb = self.S, self.A, self.ps, self.psb
        A.reset()
        ident = self.ident
        TM = max(self.seq_lens)
        NBM = TM // 128
        I32 = mybir.dt.int32
        lam0 = lambda_init(l)
        SCALE = 64 ** -0.5
        hT = A.alloc([8, TM], BF16)
        DT = A.alloc([8, 128], F32)
        QF = A.alloc([4, 128], F32)
        QB = A.alloc([4, 128], F32)
        Eq = A.alloc([512], F32)
        Dg = A.alloc([4, 512], F32)
        gng = A.alloc([512], F32)
        gnb = A.alloc([512], F32)
        lgall = A.alloc([16], F32)
        lgp = A.alloc([8], F32)
        KFB = A.alloc([16], F32)
        GC = A.alloc([8], F32)
        c127 = A.alloc([1], F32)
        cp = A.alloc([1], F32)
        neglam = A.alloc([1], F32)
        sg = A.alloc([1], F32)
        lsum = A.alloc([2], F32)
        A2 = A.off
        lqk = A.alloc([4, 64], F32)
        tI = A.alloc([512], I32)
        tE = A.alloc([128], F32)
        tP = A.alloc([128], F32)
        tN = A.alloc([128], F32)
        tM = A.alloc([128], F32)
        tI1 = A.alloc([128], F32)

        S.add("sp", lambda e: e.dma_start(out=gng, in_=self.gn_g[l:l + 1, :].broadcast_to([128, 512])),
              writes=["gng"], dma_key="gng")
        S.add("sp", lambda e: e.dma_start(out=gnb, in_=self.gn_b[l:l + 1, :].broadcast_to([128, 512])),
              writes=["gnb"], dma_key="gnb")
        S.add("sp", lambda e: e.dma_start(out=lgall, in_=self.dec_all[l:l + 1, :].broadcast_to([128, 16])),
              writes=["lgall"], dma_key="lgall")
        for half in range(2):
            S.add("sp", lambda e, half=half: e.dma_start(
                out=lgp[64 * half:64 * half + 64, :], in_=self.dec_pairs[l, half:half + 1, :].broadcast_to([64, 8])),
                writes=[("lgp", half)], dma_key=("lgp", half))
        for i, t in enumerate((self.lq1, self.lk1, self.lq2, self.lk2)):
            S.add("sp", lambda e, i=i, t=t: e.dma_start(out=lqk[:, i, :], in_=t[l:l + 1, :].broadcast_to([128, 64])),
                  writes=[("lqk", i)], dma_key=("lqk", i))
        S.add("sp", lambda e: e.dma_start(out=sg, in_=self.subln[l, :].rearrange("(p o) -> p o", o=1)),
              writes=["sg"], dma_key="sg")
        S.barrier()
        for t_, nm in ((lgall, "lgall"), (lgp, "lgp")):
            S.add("act", lambda e, t_=t_: e.activation(out=t_, in_=t_, func=AF.Exp, scale=-1.0), reads=[nm], writes=[nm])
            S.add("act", lambda e, t_=t_: e.activation(out=t_, in_=t_, func=AF.Ln, bias=self.ones_f[:, 0:1]),
                  reads=[nm], writes=[nm])
            S.add("pool", lambda e, t_=t_: e.tensor_scalar(out=t_, in0=t_, scalar1=-1.0, scalar2=None, op0=ALU.mult),
                  reads=[nm], writes=[nm])
        S.add("pool", lambda e: e.iota(out=tI, pattern=[[1, 512]], base=0, channel_multiplier=-1), writes=["tI"])
        S.add("dve", lambda e: e.tensor_copy(out=Eq, in_=tI), reads=["tI"], writes=["Eq"])
        S.add("dve", lambda e: e.tensor_copy(out=tE, in_=Eq[:, 0:128]), reads=["Eq"], writes=["tE"])
        S.add("dve", lambda e: e.tensor_scalar(out=tP, in0=tE, scalar1=0.0, scalar2=None, op0=ALU.max),
              reads=["tE"], writes=["tP"])
        S.add("dve", lambda e: e.tensor_tensor(out=tN, in0=tP, in1=tE, op=ALU.subtract), reads=["tP", "tE"], writes=["tN"])
        for d_ in range(4):
            S.add("act", lambda e, d_=d_: e.activation(out=Dg[:, d_, :], in_=Eq, func=AF.Abs, bias=float(-128 * d_)),
                  reads=["Eq"], writes=[("Dg", d_)])
        S.add("dve", lambda e: e.tensor_copy(out=c127, in_=Eq[:, 127:128]), reads=["Eq"], writes=["c127"])
        S.add("dve", lambda e: e.tensor_scalar(out=cp, in0=Eq[:, 0:1], scalar1=-1.0, scalar2=None, op0=ALU.mult),
              reads=["Eq"], writes=["cp"])
        S.add("dve", lambda e: e.tensor_scalar(out=tI1, in0=Eq[:, 0:128], scalar1=cp, scalar2=1.0, op0=ALU.add, op1=ALU.add),
              reads=["Eq", "cp"], writes=["tI1"])
        S.barrier()
        for h in range(8):
            S.add("act", lambda e, h=h: e.activation(out=tM, in_=tP, func=AF.Exp, scale=lgall[:, h:h + 1]),
                  reads=["tM"], writes=["tM"])
            S.add("pool", lambda e, h=h: e.affine_select(out=DT[:, h, :], in_=tM, pattern=[[1, 128]], compare_op=ALU.is_ge,
                                                         fill=0.0, base=0, channel_multiplier=-1),
                  reads=["tM"], writes=[("DT", h)])
            S.add("act", lambda e, h=h: e.activation(out=tM, in_=tN, func=AF.Exp, scale=lgall[:, 8 + h:9 + h]),
                  reads=["tM"], writes=["tM"])
            S.add("pool", lambda e, h=h: e.affine_select(out=tM, in_=tM, pattern=[[-1, 128]], compare_op=ALU.is_gt,
                                                         fill=0.0, base=0, channel_multiplier=1),
                  reads=["tM"], writes=["tM"])
            S.add("pool", lambda e, h=h: e.tensor_tensor(out=DT[:, h, :], in0=DT[:, h, :], in1=tM, op=ALU.add),
                  reads=["tM", ("DT", h)], writes=[("DT", h)])
        for p in range(4):
            S.add("act", lambda e, p=p: e.activation(out=QF[:, p, :], in_=tI1, func=AF.Exp, scale=lgp[:, p:p + 1]),
                  writes=[("QF", p)])
            S.add("dve", lambda e, p=p: e.tensor_scalar(out=QB[:, p, :], in0=tI1, scalar1=-1.0, scalar2=129.0,
                                                       op0=ALU.mult, op1=ALU.add), writes=[("QB", p)])
            S.add("act", lambda e, p=p: e.activation(out=QB[:, p, :], in_=QB[:, p, :], func=AF.Exp,
                                                     scale=lgp[:, 4 + p:5 + p]), reads=[("QB", p)], writes=[("QB", p)])
        S.add("act", lambda e: e.activation(out=KFB[:, 0:8], in_=lgall[:, 0:8], func=AF.Exp, scale=c127), writes=["KF"])
        S.add("act", lambda e: e.activation(out=KFB[:, 8:16], in_=lgall[:, 8:16], func=AF.Exp, scale=cp), writes=["KB"])
        S.add("act", lambda e: e.activation(out=GC, in_=lgp, func=AF.Exp, scale=128.0), writes=["GC"])
        S.add("dve", lambda e: e.tensor_tensor(out=lqk[:, 0, :], in0=lqk[:, 0, :], in1=lqk[:, 1, :], op=ALU.mult), writes=["lq0"])
        S.add("dve", lambda e: e.tensor_tensor(out=lqk[:, 2, :], in0=lqk[:, 2, :], in1=lqk[:, 3, :], op=ALU.mult), writes=["lq2"])
        S.add("dve", lambda e: e.reduce_sum(out=lsum[:, 0:1], in_=lqk[:, 0, :], axis=AX.X), reads=["lq0"], writes=["ls0"])
        S.add("dve", lambda e: e.reduce_sum(out=lsum[:, 1:2], in_=lqk[:, 2, :], axis=AX.X), reads=["lq2"], writes=["ls1"])
        S.add("act", lambda e: e.activation(out=lsum, in_=lsum, func=AF.Exp), reads=["ls0", "ls1"], writes=["lse"])
        S.add("dve", lambda e: e.tensor_tensor(out=neglam, in0=lsum[:, 1:2], in1=lsum[:, 0:1], op=ALU.subtract),
              reads=["lse"], writes=["nl"])
        S.add("dve", lambda e: e.tensor_scalar(out=neglam, in0=neglam, scalar1=-lam0, scalar2=None, op0=ALU.add),
              reads=["nl"], writes=["nl"])
        S.add("pool", lambda e: e.tensor_scalar(out=sg, in0=sg, scalar1=1.0 - lam0, scalar2=None, op0=ALU.mult), writes=["sg2"])
        S.add("pool", lambda e: e.tensor_scalar(out=KFB, in0=KFB, scalar1=0.125, scalar2=0.0, op0=ALU.mult, op1=ALU.add),
              reads=["KF", "KB"], writes=["KF8"])
        S.barrier()

        slopes = [2.0 ** (-8.0 * (h + 1) / 4) for h in range(4)]
        def do_b1(T, tok0):
            nblk = T // 128
            ntile = T // 512
            A.off = A2
            xbf = [A.alloc([D], BF16) for _ in range(8)]
            for t in range(ntile):
                for b in range(4):
                    sl = (t % 2) * 4 + b
                    rows = slice(tok0 + (4 * t + b) * 128, tok0 + (4 * t + b + 1) * 128)
                    S.add("pool", lambda e, sl=sl, rows=rows: e.dma_start(out=xbf[sl], in_=src[rows, :]),
                          writes=[("xbf", sl)], dma_key=("xbf", sl))
                for k in range(8):
                    bank = 6 + (k % 2)

                    def tr(e, k=k, bank=bank, t=t):
                        for b in range(4):
                            ins = e.transpose(out=psb[:, bank, b * 128:(b + 1) * 128],
                                              in_=xbf[(t % 2) * 4 + b][:, k * 128:(k + 1) * 128], identity=ident)
                        return ins
                    S.add("pe", tr, reads=[("xbf", (t % 2) * 4 + b) for b in range(4)], writes=[("ps", bank)])
                    dsl = hT[:, k, t * 512:(t + 1) * 512]
                    if k % 2 == 0:
                        S.add("act", lambda e, dsl=dsl, bank=bank: e.activation(out=dsl, in_=psb[:, bank, 0:512], func=AF.Copy),
                              reads=[("ps", bank)], writes=[("hT", t)])
                    else:
                        S.add("dve", lambda e, dsl=dsl, bank=bank: e.tensor_copy(out=dsl, in_=psb[:, bank, 0:512]),
                              reads=[("ps", bank)], writes=[("hT", t)])
            S.barrier()

        def do_ret(p, T, tok0):
            nblk = T // 128
            ntile = T // 512
            if True:
                A.off = A2
                wr = A.alloc([8, 512], BF16)
                rqT = A.alloc([T], BF16)
                rkTa = A.alloc([T], BF16)
                rkTb = A.alloc([T], BF16)
                kdf_all = A.alloc([nblk, 128], BF16)
                kdb_all = A.alloc([nblk, 128], BF16)
                qf_all = A.alloc([T], BF16)
                qb_all = A.alloc([T], BF16)
                rv_tm = A.alloc([nblk, 128], BF16)
                gate = A.alloc([nblk, 128], BF16)
                Sb16a = A.alloc([nblk, 64], BF16)
                Sb16b = A.alloc([nblk, 64], BF16)
                moT = A.alloc([T], BF16)
                Sf32 = A.alloc([64], F32)
                Sb32 = A.alloc([64], F32)
                Sf16a = [A.alloc([64], BF16) for _ in range(2)]
                Sf16b = [A.alloc([64], BF16) for _ in range(2)]
                AT = [A.alloc([256], BF16) for _ in range(2)]
                ob = [A.alloc([128], F32) for _ in range(2)]
                rob = [A.alloc([128], BF16) for _ in range(2)]
                gst = [A.alloc([2, 6], F32) for _ in range(2)]
                gmv = [A.alloc([2, 2], F32) for _ in range(2)]
                grs = [A.alloc([2], F32) for _ in range(2)]
                for j, c0 in enumerate((128 * p, 512 + 128 * p, 1024 + 128 * p, 1536 + 128 * p)):
                    S.add("pool", lambda e, j=j, c0=c0: [e.dma_start(
                        out=wr[:, k, 128 * j:128 * j + 128],
                        in_=self.w_in[l, k * 128:(k + 1) * 128, c0:c0 + 128]) for k in range(8)],
                        writes=[("wr", j)], dma_key=("wsl", j), ndma=8)
                for t in range(ntile):
                    for j, (dstT, nm, sc) in enumerate(((rqT, "rqT", 1.0), (rkTa, "rkT", 0.125))):
                        bank = (2 * t + j) % 4

                        def mm(e, t=t, j=j, bank=bank):
                            for k in range(8):
                                ins = e.matmul(ps[:, bank, :], lhsT=wr[:, k, 128 * j:128 * j + 128],
                                               rhs=hT[:, k, t * 512:(t + 1) * 512], start=(k == 0), stop=(k == 7))
                            return ins
                        S.add("pe", mm, reads=[("hT", t), ("wr", j)], writes=[("ps", bank)])
                        S.add("act", lambda e, dstT=dstT, t=t, bank=bank, sc=sc: e.activation(
                            out=dstT[:, t * 512:(t + 1) * 512], in_=ps[:, bank, :], func=AF.Identity, scale=sc),
                            reads=[("ps", bank)], writes=[(nm, t)])
                        if j == 0:
                            ts_ = slice(t * 512, (t + 1) * 512)
                            for dst_, tab_, nm_ in ((qf_all, QF, "qf"), (qb_all, QB, "qb")):
                                S.add("pool", lambda e, dst_=dst_, tab_=tab_, ts_=ts_: e.tensor_tensor(
                                    out=dst_[:, ts_].rearrange("p (c i) -> p c i", c=4),
                                    in0=rqT[:, ts_].rearrange("p (c i) -> p c i", c=4),
                                    in1=tab_[:, p:p + 1, :].broadcast_to([128, 4, 128]), op=ALU.mult),
                                    reads=[("rqT", t)], writes=[(nm_, t)])
                        if j == 1:
                            S.add("act", lambda e, t=t, bank=bank, sc=sc: e.activation(
                                out=rkTb[:, t * 512:(t + 1) * 512], in_=ps[:, bank, :], func=AF.Identity, scale=sc),
                                reads=[("ps", bank)], writes=[("rkTb", t)])
                            S.add("dve", lambda e, t=t: e.memset(rkTa[64:128, t * 512:(t + 1) * 512], 0.0),
                                  reads=[("rkT", t)], writes=[("rkT", t)])
                            S.add("dve", lambda e, t=t: e.memset(rkTb[0:64, t * 512:(t + 1) * 512], 0.0),
                                  reads=[("rkTb", t)], writes=[("rkTb", t)])
                for blk in range(nblk):
                    bank = 4 + blk % 4

                    def mm(e, blk=blk, bank=bank):
                        for k in range(8):
                            ins = e.matmul(ps[:, bank, 0:384], lhsT=hT[:, k, blk * 128:(blk + 1) * 128],
                                           rhs=wr[:, k, 128:512], start=(k == 0), stop=(k == 7))
                        return ins
                    S.add("pe", mm, reads=[("hT", blk // 4), ("wr", 1), ("wr", 2), ("wr", 3)], writes=[("ps", bank)])
                    for hh in range(2):
                        hs_ = slice(64 * hh, 64 * hh + 64)
                        S.add("act", lambda e, blk=blk, bank=bank, hs_=hs_, hh=hh: e.activation(
                            out=kdf_all[:, blk, hs_], in_=ps[:, bank, hs_], func=AF.Identity,
                            scale=KFB[:, 2 * p + hh:2 * p + hh + 1]),
                            reads=[("ps", bank)], writes=[("kdf", blk, hh)])
                        S.add("act", lambda e, blk=blk, bank=bank, hs_=hs_, hh=hh: e.activation(
                            out=kdb_all[:, blk, hs_], in_=ps[:, bank, hs_], func=AF.Identity,
                            scale=KFB[:, 8 + 2 * p + hh:8 + 2 * p + hh + 1]),
                            reads=[("ps", bank)], writes=[("kdb", blk, hh)])
                    S.add("act", lambda e, blk=blk, bank=bank: e.activation(out=rv_tm[:, blk, :], in_=ps[:, bank, 128:256],
                                                                           func=AF.Copy),
                          reads=[("ps", bank)], writes=[("rv_tm", blk)])
                    S.add("act", lambda e, blk=blk, bank=bank: e.activation(out=gate[:, blk, :], in_=ps[:, bank, 256:384],
                                                                           func=AF.Silu),
                          reads=[("ps", bank)], writes=[("gate", blk)])
                S.add("pool", lambda e: e.memset(Sb32, 0.0), writes=["Sb32"])
                S.add("pool", lambda e: e.memset(Sf32, 0.0), writes=["Sf32"])
                S.add("pool", lambda e: e.memset(Sb16a, 0.0), writes=["Sb16a"])
                S.add("pool", lambda e: e.memset(Sb16b, 0.0), writes=["Sb16b"])
                for s_ in range(2):
                    S.add("pool", lambda e, s_=s_: e.memset(Sf16a[s_], 0.0), writes=[("Sf16", s_)])
                    S.add("pool", lambda e, s_=s_: e.memset(Sf16b[s_], 0.0), writes=[("Sf16", s_)])

                def state_update(c, S32, nm, kcol, gcol, ubank):
                    s = c % 2
                    kd_all = kdf_all if kcol == 0 else kdb_all
                    knm = "kdf" if kcol == 0 else "kdb"
                    S.add("pe", lambda e, c=c: e.matmul(ps[:, ubank, 0:128], lhsT=kd_all[:, c, :], rhs=rv_tm[:, c, :],
                                                         start=True, stop=True),
                          reads=[(knm, c, 0), (knm, c, 1), ("rv_tm", c)], writes=[("ps", ubank)])
                    for hh in range(2):
                        rs_ = slice(64 * hh, 64 * hh + 64)
                        S.add("dve", lambda e, rs_=rs_, hh=hh: e.scalar_tensor_tensor(
                            out=S32[rs_, :], in0=S32[rs_, :], scalar=GC[rs_, gcol + p:gcol + p + 1],
                            in1=ps[rs_, ubank, 64 * hh:64 * hh + 64], op0=ALU.mult, op1=ALU.add),
                            reads=[("ps", ubank), (nm, hh)], writes=[(nm, hh)])

                import os as _os2
                RL = int(_os2.environ.get("RET_LEVEL", "2"))
                for c in (range(nblk - 1, -1, -1) if RL >= 1 else []):
                    S.add("dve", lambda e, c=c: e.tensor_copy(out=Sb16a[0:64, c, :], in_=Sb32[0:64, :]),
                          reads=[("Sb32", 0), ("Sb32", 1), "Sb32", "Sb16a"], writes=[("Sb16", c)])
                    S.add("dve", lambda e, c=c: e.tensor_copy(out=Sb16b[64:128, c, :], in_=Sb32[64:128, :]),
                          reads=[("Sb32", 0), ("Sb32", 1), "Sb32", "Sb16b"], writes=[("Sb16", c)])
                    if c > 0:
                        state_update(c, Sb32, "Sb32", 8, 4, 4 + c % 2)
                def stage1(c):
                    s = c % 2
                    cs = slice(c * 128, (c + 1) * 128)
                    S.add("dve", lambda e, s=s: e.tensor_copy(out=Sf16a[s][0:64, :], in_=Sf32[0:64, :]),
                          reads=[("Sf32", 0), ("Sf32", 1), "Sf32"], writes=[("Sf16", s)])
                    S.add("dve", lambda e, s=s: e.tensor_copy(out=Sf16b[s][64:128, :], in_=Sf32[64:128, :]),
                          reads=[("Sf32", 0), ("Sf32", 1), "Sf32"], writes=[("Sf16", s)])
                    sbank = s

                    def mmS(e, cs=cs, sbank=sbank):
                        e.matmul(ps[:, sbank, 0:128], lhsT=rkTa[:, cs], rhs=rqT[:, cs], start=True, stop=True)
                        return e.matmul(ps[:, sbank, 128:256], lhsT=rkTb[:, cs], rhs=rqT[:, cs], start=True, stop=True)
                    S.add("pe", mmS, reads=[("rqT", c // 4), ("rkT", c // 4), ("rkTb", c // 4)], writes=[("ps", sbank)])
                    S.add("dve", lambda e, s=s, sbank=sbank: e.tensor_tensor(
                        out=AT[s], in0=ps[:, sbank, 0:256], in1=DT[:, 2 * p:2 * p + 2, :].rearrange("p a b -> p (a b)"),
                        op=ALU.mult), reads=[("ps", sbank)], writes=[("AT", s)])
                    obank = 2 + s

                    def mmO(e, s=s, c=c, cs=cs, obank=obank):
                        for hh in range(2):
                            rs_ = slice(64 * hh, 64 * hh + 64)
                            o_ = ps[:, obank, 64 * hh:64 * hh + 64]
                            e.matmul(o_, lhsT=AT[s][:, 128 * hh:128 * hh + 128], rhs=rv_tm[:, c, 64 * hh:64 * hh + 64],
                                     start=True, stop=False)
                            e.matmul(o_, lhsT=qf_all[:, cs], rhs=(Sf16a if hh == 0 else Sf16b)[s], start=False, stop=False)
                            ins = e.matmul(o_, lhsT=qb_all[:, cs], rhs=(Sb16a if hh == 0 else Sb16b)[:, c, :], start=False, stop=True)
                        return ins
                    S.add("pe", mmO, reads=[("AT", s), ("rv_tm", c), ("qf", c // 4), ("qb", c // 4), ("Sf16", s), ("Sb16", c)],
                          writes=[("ps", obank)])
                    if c < nblk - 1:
                        state_update(c, Sf32, "Sf32", 0, 0, 4 + s)

                def stage2(c):
                    s = c % 2
                    cs = slice(c * 128, (c + 1) * 128)
                    obank = 2 + s
                    S.add("act", lambda e, s=s, obank=obank: e.activation(out=ob[s], in_=ps[:, obank, 0:128], func=AF.Copy),
                          reads=[("ps", obank)], writes=[("ob", s)])
                    for hh in range(2):
                        S.add("dve", lambda e, s=s, hh=hh: e.bn_stats(out=gst[s][:, hh, :], in_=ob[s][:, 64 * hh:64 * hh + 64]),
                              reads=[("ob", s)], writes=[("gst", s, hh)])
                        S.add("dve", lambda e, s=s, hh=hh: e.bn_aggr(out=gmv[s][:, hh, :], in_=gst[s][:, hh, :]),
                              reads=[("gst", s, hh)], writes=[("gmv", s, hh)])
                    S.add("dve", lambda e, s=s: e.tensor_scalar(out=grs[s], in0=gmv[s][:, :, 1], scalar1=EPS, scalar2=None,
                                                                op0=ALU.add),
                          reads=[("gmv", s, 0), ("gmv", s, 1)], writes=[("grs", s)])
                    S.add("act", lambda e, s=s: e.activation(out=grs[s], in_=grs[s], func=AF.Ln), reads=[("grs", s)], writes=[("grs", s)])
                    S.add("act", lambda e, s=s: e.activation(out=grs[s], in_=grs[s], func=AF.Exp, scale=-0.5),
                          reads=[("grs", s)], writes=[("grs", s)])

                def stage2b(c):
                    s = c % 2
                    cs = slice(c * 128, (c + 1) * 128)
                    for hh in range(2):
                        S.add("dve", lambda e, s=s, hh=hh: e.tensor_scalar(
                            out=ob[s][:, 64 * hh:64 * hh + 64], in0=ob[s][:, 64 * hh:64 * hh + 64],
                            scalar1=gmv[s][:, hh, 0:1], scalar2=grs[s][:, hh:hh + 1], op0=ALU.subtract, op1=ALU.mult),
                            reads=[("ob", s), ("grs", s), ("gmv", s, hh)], writes=[("ob", s)])
                    S.add("pool", lambda e, s=s: e.tensor_tensor(out=ob[s], in0=ob[s], in1=gng[:, 128 * p:128 * p + 128], op=ALU.mult),
                          reads=[("ob", s)], writes=[("ob", s)])
                    S.add("pool", lambda e, s=s: e.tensor_tensor(out=ob[s], in0=ob[s], in1=gnb[:, 128 * p:128 * p + 128], op=ALU.add),
                          reads=[("ob", s)], writes=[("ob", s)])
                    S.add("pool", lambda e, s=s, c=c: e.tensor_tensor(out=rob[s], in0=ob[s], in1=gate[:, c, :], op=ALU.mult),
                          reads=[("ob", s), ("gate", c)], writes=[("rob", s)])

                def stage2c(c):
                    s = c % 2
                    cs = slice(c * 128, (c + 1) * 128)
                    tbank = 6 + s
                    S.add("pe", lambda e, s=s, tbank=tbank: e.transpose(out=psb[:, tbank, 0:128], in_=rob[s], identity=ident),
                          reads=[("rob", s)], writes=[("ps", tbank)])
                    S.add("act", lambda e, cs=cs, tbank=tbank: e.activation(out=moT[:, cs], in_=psb[:, tbank, 0:128], func=AF.Copy),
                          reads=[("ps", tbank)], writes=[("moT", c // 4)])

                if RL >= 2:
                    stage1(0)
                    for c in range(nblk):
                        stage2(c)
                        if c + 1 < nblk:
                            stage1(c + 1)
                        stage2b(c)
                        if c >= 1:
                            stage2c(c - 1)
                    stage2c(nblk - 1)
                import os as _os
                if _os.environ.get("RET_NOMO") != "1":
                    S.add("sp", lambda e, p=p, T=T, tok0=tok0: e.dma_start(out=self.MO[p, :, tok0:tok0 + T], in_=moT),
                          reads=[("moT", t) for t in range(ntile)], writes=[("MO", p)], dma_key=("mo", p % 2))
                S.barrier()

        def do_att(h, T, tok0):
            nblk = T // 128
            ntile = T // 512
            if True:
                A.off = A2
                m_h = slopes[h]
                wa = A.alloc([8, 384], BF16)
                qT = A.alloc([T], BF16)
                kTa = A.alloc([T], BF16)
                kTb = A.alloc([T], BF16)
                V = A.alloc([nblk, 128], BF16)
                moT = A.alloc([T], BF16)
                tmp = [A.alloc([512], F32) for _ in range(4)]
                PT = [A.alloc([512], BF16) for _ in range(4)]
                rd = [A.alloc([512], F32) for _ in range(2)]
                tt = [A.alloc([512], F32) for _ in range(2)]
                deferred = []
                att = A.alloc([512], F32)
                sq = A.alloc([512], BF16)
                rr = A.alloc([512], F32)
                for j in range(3):
                    c0 = 2048 + 512 * j + 128 * h
                    S.add("pool", lambda e, j=j, c0=c0: [e.dma_start(
                        out=wa[:, k, 128 * j:128 * j + 128],
                        in_=self.w_in[l, k * 128:(k + 1) * 128, c0:c0 + 128]) for k in range(8)],
                        writes=[("wa", j)], dma_key=("wsl", j), ndma=8)
                for t in range(ntile):
                    for j, (dstT, nm) in enumerate(((qT, "qT"), (kTa, "kT"))):
                        bank = (2 * t + j) % 4

                        def mm(e, t=t, j=j, bank=bank):
                            for k in range(8):
                                ins = e.matmul(ps[:, bank, :], lhsT=wa[:, k, 128 * j:128 * j + 128],
                                               rhs=hT[:, k, t * 512:(t + 1) * 512], start=(k == 0), stop=(k == 7))
                            return ins
                        S.add("pe", mm, reads=[("hT", t), ("wa", j)], writes=[("ps", bank)])
                        S.add("act", lambda e, dstT=dstT, t=t, bank=bank: e.activation(
                            out=dstT[:, t * 512:(t + 1) * 512], in_=ps[:, bank, :], func=AF.Copy),
                            reads=[("ps", bank)], writes=[(nm, t)])
                        if j == 1:
                            S.add("act", lambda e, t=t, bank=bank: e.activation(
                                out=kTb[:, t * 512:(t + 1) * 512], in_=ps[:, bank, :], func=AF.Copy),
                                reads=[("ps", bank)], writes=[("kTb", t)])
                            S.add("dve", lambda e, t=t: e.memset(kTa[64:128, t * 512:(t + 1) * 512], 0.0),
                                  reads=[("kT", t)], writes=[("kT", t)])
                            S.add("dve", lambda e, t=t: e.memset(kTb[0:64, t * 512:(t + 1) * 512], 0.0),
                                  reads=[("kTb", t)], writes=[("kTb", t)])
                for blk in range(nblk):
                    bank = 4 + blk % 4

                    def mm(e, blk=blk, bank=bank):
                        for k in range(8):
                            ins = e.matmul(ps[:, bank, 0:128], lhsT=hT[:, k, blk * 128:(blk + 1) * 128],
                                           rhs=wa[:, k, 256:384], start=(k == 0), stop=(k == 7))
                        return ins
                    S.add("pe", mm, reads=[("hT", blk // 4), ("wa", 2)], writes=[("ps", bank)])
                    S.add("dve", lambda e, blk=blk, bank=bank: e.tensor_copy(out=V[:, blk, :], in_=ps[:, bank, 0:128]),
                          reads=[("ps", bank)], writes=[("V", blk)])

                itc = [0]

                def post_chain(qt):
                    qs = slice(qt * 512, (qt + 1) * 512)
                    ops = []
                    for mp in range(2):
                        ops.append(lambda mp=mp: S.add("dve", lambda e: e.reciprocal(out=rd[mp], in_=rd[mp]),
                                                       reads=[("rd", mp)], writes=[("rd", mp)]))
                        ops.append(lambda mp=mp: S.add("dve", lambda e: e.tensor_tensor(out=tt[mp], in0=tt[mp], in1=rd[mp], op=ALU.mult),
                                                       reads=[("tt", mp), ("rd", mp)], writes=[("tt", mp)]))
                    ops.append(lambda: S.add("dve", lambda e: e.scalar_tensor_tensor(
                        out=att, in0=tt[1], scalar=neglam, in1=tt[0], op0=ALU.mult, op1=ALU.add),
                        reads=[("tt", 0), ("tt", 1)], writes=["att"]))
                    ops.append(lambda: S.add("act", lambda e: e.activation(out=sq, in_=att, func=AF.Square),
                                             reads=["att"], writes=["sq"]))

                    def ssq():
                        qbank = 2 * (itc[0] % 2)
                        S.add("pe", lambda e: e.matmul(ps[:, qbank, :], lhsT=self.ones_bf, rhs=sq, start=True, stop=True),
                              reads=["sq"], writes=[("ps", qbank)])
                        S.add("act", lambda e: e.activation(out=rr, in_=ps[:, qbank, :], func=AF.Ln, bias=self.eps_c,
                                                            scale=1.0 / 128),
                              reads=[("ps", qbank)], writes=["rr"])
                    ops.append(ssq)
                    ops.append(lambda: S.add("act", lambda e: e.activation(out=rr, in_=rr, func=AF.Exp, scale=-0.5),
                                             reads=["rr"], writes=["rr"]))
                    ops.append(lambda: S.add("dve", lambda e: e.tensor_tensor(out=att, in0=att, in1=rr, op=ALU.mult),
                                             reads=["att", "rr"], writes=["att"]))
                    ops.append(lambda: S.add("dve", lambda e: e.tensor_scalar(out=moT[:, qs], in0=att, scalar1=sg, scalar2=None,
                                                                              op0=ALU.mult),
                                             reads=["att"], writes=[("moT", qt)]))
                    return ops

                pending_pv = []

                def flush_pv():
                    while pending_pv:
                        pending_pv.pop(0)()

                for qt in range(ntile):
                    qs = slice(qt * 512, (qt + 1) * 512)
                    for kc in range(nblk):
                        ks = slice(kc * 128, (kc + 1) * 128)
                        par = itc[0] % 2
                        itc[0] += 1
                        dk_ = 4 * qt - kc
                        for mp in range(2):
                            bank = 2 * par + mp
                            sl = 2 * par + mp
                            rs_ = slice(64 * mp, 64 * mp + 64)
                            kTm = kTa if mp == 0 else kTb
                            S.add("pe", lambda e, bank=bank, kTm=kTm, ks=ks, qs=qs: e.matmul(
                                ps[:, bank, :], lhsT=kTm[:, ks], rhs=qT[:, qs], start=True, stop=True),
                                reads=[("kT", kc // 4), ("kTb", kc // 4), ("qT", qt)], writes=[("ps", bank)])
                            if dk_ > 0:
                                tbl, c1, c2 = Eq, -m_h / SCALE, -m_h * 128.0 * dk_
                            elif dk_ <= -4:
                                tbl, c1, c2 = Eq, m_h / SCALE, m_h * 128.0 * dk_
                            else:
                                tbl, c1, c2 = Dg[:, -dk_, :], -m_h / SCALE, 0.0
                            S.add("dve", lambda e, sl=sl, bank=bank, tbl=tbl, c1=c1: e.scalar_tensor_tensor(
                                out=tmp[sl], in0=tbl, scalar=c1, in1=ps[:, bank, :], op0=ALU.mult, op1=ALU.add),
                                reads=[("ps", bank)], writes=[("tmp", sl)])
                            S.add("act", lambda e, sl=sl, c2=c2: e.activation(out=PT[sl], in_=tmp[sl], func=AF.Exp,
                                                                              bias=float(c2), scale=SCALE),
                                  reads=[("tmp", sl)], writes=[("PT", sl)])

                        def mmPV(e, par=par, kc=kc, nblk=nblk):
                            for mp in range(2):
                                e.matmul(ps[:, 4 + mp, :], lhsT=V[:, kc, :], rhs=PT[2 * par + mp], start=(kc == 0), stop=(kc == nblk - 1))
                                ins = e.matmul(ps[:, 6 + mp, :], lhsT=self.ones_bf, rhs=PT[2 * par + mp], start=(kc == 0),
                                               stop=(kc == nblk - 1))
                            return ins
                        flush_pv()
                        pending_pv.append(lambda mmPV=mmPV, par=par, kc=kc: S.add(
                            "pe", mmPV, reads=[("PT", 2 * par), ("PT", 2 * par + 1), ("V", kc)],
                            writes=[("ps", 4), ("ps", 5), ("ps", 6), ("ps", 7)]))
                        if kc >= 1 and deferred:
                            deferred.pop(0)()
                    flush_pv()
                    while deferred:
                        deferred.pop(0)()
                    for mp in range(2):
                        S.add("act", lambda e, mp=mp: e.activation(out=tt[mp], in_=ps[:, 4 + mp, :], func=AF.Copy),
                              reads=[("ps", 4 + mp)], writes=[("tt", mp)])
                        S.add("act", lambda e, mp=mp: e.activation(out=rd[mp], in_=ps[:, 6 + mp, :], func=AF.Copy),
                              reads=[("ps", 6 + mp)], writes=[("rd", mp)])
                    deferred.extend(post_chain(qt))
                while deferred:
                    deferred.pop(0)()
                S.add("sp", lambda e, h=h, T=T, tok0=tok0: e.dma_start(out=self.MO[4 + h, :, tok0:tok0 + T], in_=moT),
                      reads=[("moT", t) for t in range(ntile)], writes=[("MO", 4 + h)], dma_key=("mo", h % 2))
                S.barrier()

        def do_b4(T, tok0):
            import os as _os
            nblk = T // 128
            ntile = T // 512
            A.off = A2
            wout = A.alloc([8, D], BF16)
            lng = A.alloc([D], F32)
            lnb = A.alloc([D], F32)
            S.add("pool", lambda e: e.dma_start(out=wout, in_=self.w_out[l].rearrange("(k p) n -> p k n", p=128)),
                  writes=["wout"], dma_key="wout")
            S.add("sp", lambda e: e.dma_start(out=lng, in_=self.ln[(2, "g")][l:l + 1, :].broadcast_to([128, D])),
                  writes=["lng"], dma_key="lng")
            S.add("sp", lambda e: e.dma_start(out=lnb, in_=self.ln[(2, "b")][l:l + 1, :].broadcast_to([128, D])),
                  writes=["lnb"], dma_key="lnb")
            mot = [A.alloc([8, 128], BF16) for _ in range(2)]
            xres = [A.alloc([D], F32) for _ in range(2)]
            yb = [A.alloc([D], F32) for _ in range(2)]
            scr = [dict(st=A.alloc([2, 6], F32), mv=A.alloc([2], F32), rs=A.alloc([1], F32)) for _ in range(2)]
            for blk in range(nblk):
                s = blk % 2
                rows = slice(tok0 + blk * 128, tok0 + (blk + 1) * 128)
                if _os.environ.get("NOMO") == "1":
                    S.add("pool", lambda e, s=s: e.memset(mot[s], 0.5), writes=[("mot", s)])
                else:
                    S.add("sp", lambda e, s=s, rows=rows: [e.dma_start(out=mot[s][:, k, :], in_=self.MO[k, :, rows])
                                                           for k in range(8)],
                          writes=[("mot", s)], dma_key=("mot", s), ndma=8)
                S.add("sp", lambda e, s=s, rows=rows: e.dma_start(out=xres[s], in_=src[rows, :]),
                      writes=[("xres", s)], dma_key=("xres", s))
                b0 = 2 * (blk % 4)

                def mm(e, s=s, b0=b0):
                    for half in range(2):
                        for k in range(8):
                            ins = e.matmul(ps[:, b0 + half, :], lhsT=mot[s][:, k, :], rhs=wout[:, k, half * 512:(half + 1) * 512],
                                           start=(k == 0), stop=(k == 7))
                    return ins
                S.add("pe", mm, reads=[("mot", s), "wout"], writes=[("ps", b0), ("ps", b0 + 1)])
                ysb = yb[s]
                pso = ps[:, b0:b0 + 2, :].rearrange("p a b -> p (a b)")
                S.add("act", lambda e, ysb=ysb, pso=pso: e.activation(out=ysb, in_=pso, func=AF.Copy),
                      reads=[("ps", b0), ("ps", b0 + 1)], writes=[("my", s)])
                S.add("dve", lambda e, ysb=ysb, s=s: e.scalar_tensor_tensor(
                    out=ysb, in0=xres[s], scalar=ALPHA, in1=ysb, op0=ALU.mult, op1=ALU.add),
                    reads=[("my", s), ("xres", s)], writes=[("my", s)])
                self.ln_block(ysb, s, lng, lnb, scr[s], "m")
                S.add("pool", lambda e, ysb=ysb, rows=rows: e.dma_start(out=dst[rows, :], in_=ysb),
                      reads=[("my", s)], dma_key=("yst", s))
            S.barrier()

        tok0 = 0
        for si, T in enumerate(self.seq_lens):
            import os as _os
            if self.mix_stage >= 1 and _os.environ.get("NOB1") != "1":
                do_b1(T, tok0)
            if self.mix_stage >= 2:
                for p in range(4):
                    do_ret(p, T, tok0)
            if self.mix_stage >= 3:
                for h in range(4):
                    do_att(h, T, tok0)
            if self.mix_stage >= 1 and _os.environ.get("NOB4") != "1":
                do_b4(T, tok0)
            tok0 += T
        S.barrier(new_set=True)


_PARAM_NAMES = ["w_in", "w_out", "ret_decay_f", "ret_decay_b", "ret_gn_g", "ret_gn_b",
                "diff_lq1", "diff_lk1", "diff_lq2", "diff_lk2", "diff_subln_g",
                "ffn1_w13", "ffn1_w2", "ffn2_w13", "ffn2_w2",
                "ln1_g", "ln1_b", "ln2_g", "ln2_b", "ln3_g", "ln3_b"]


def extra_layouts(params):
    f, b = params["ret_decay_f"], params["ret_decay_b"]
    dec_all = np.ascontiguousarray(np.concatenate([f, b], axis=1))
    dec_pairs = np.ascontiguousarray(np.stack(
        [np.concatenate([f[:, 0::2], b[:, 0::2]], axis=1), np.concatenate([f[:, 1::2], b[:, 1::2]], axis=1)], axis=1))
    return {"dec_all": dec_all, "dec_pairs": dec_pairs}


def kernel(**inputs):
    xp = np.asarray(inputs["x_prompt"], dtype=np.float32)
    xs = np.asarray(inputs["x_sample"], dtype=np.float32)
    params = {k: np.ascontiguousarray(np.asarray(inputs[k], dtype=np.float32)) for k in _PARAM_NAMES}
    params.update(extra_layouts(params))
    nc = Builder(SEQ_LENS).build()
    in_maps = []
    for c in range(N_CORES):
        xc = np.concatenate([xp[c].reshape(-1, D), xs[4 * c:4 * c + 4].reshape(-1, D)], axis=0)
        m = {"x": np.ascontiguousarray(xc)}
        m.update(params)
        in_maps.append(m)
    res = run_bass_kernel_spmd(nc, in_maps, core_ids=list(range(N_CORES)))
    yp = np.empty_like(xp)
    ys = np.empty_like(xs)
    for c in range(N_CORES):
        yc = np.asarray(res.results[c]["y"], dtype=np.float32)
        yp[c] = yc[:4096]
        ys[4 * c:4 * c + 4] = yc[4096:].reshape(4, 2048, D)
    return (yp, ys)
```

```python
import math
from contextlib import ExitStack

import numpy as np
import concourse.bass as bass
import concourse.mybir as mybir
from concourse.bass_utils import run_bass_kernel_spmd

F32 = mybir.dt.float32
BF16 = mybir.dt.bfloat16
U8 = mybir.dt.uint8
AF = mybir.ActivationFunctionType
ALU = mybir.AluOpType
AX = mybir.AxisListType

D = 1024
DFF = 2816
NFF = DFF // 128
INW = 3584
DEPTH = 2
EPS = 1e-5
ALPHA = (2 * DEPTH) ** 0.25
N_CORES = 8
SEQ_LENS = (4096, 2048, 2048, 2048, 2048)
ENGS = ("pe", "act", "dve", "pool", "sp")


def lambda_init(layer):
    return 0.8 - 0.6 * math.exp(-0.3 * layer)


class Op:
    __slots__ = ("eng", "fn", "dma_key", "deps", "inc", "val", "idx", "is_dma", "sset", "kind", "ndma")

    def __init__(self, eng, fn, dma_key):
        self.eng = eng
        self.fn = fn
        self.dma_key = dma_key
        self.is_dma = dma_key is not None
        self.deps = []
        self.inc = False
        self.val = None
        self.sset = 0
        self.kind = "op"
        self.ndma = 1


class Sched:
    def __init__(self):
        self.ops = []
        self.last_w = {}
        self.readers = {}
        self.final_dma = []
        self.sset = 0
        self.last_on_eng = {}
        self.last_dma = {}

    def add(self, eng, fn, reads=(), writes=(), dma_key=None, final=False, ndma=1):
        op = Op(eng, fn, dma_key)
        op.ndma = ndma
        op.idx = len(self.ops)
        op.sset = self.sset
        deps = {}
        for k in reads:
            w = self.last_w.get(k)
            if w is not None:
                deps[w.idx] = (w, True)
        for k in writes:
            w = self.last_w.get(k)
            if w is not None and w.idx not in deps:
                deps[w.idx] = (w, False)
            for r in self.readers.get(k, ()):
                if r.idx not in deps:
                    deps[r.idx] = (r, False)
        op.deps = [deps[i] for i in sorted(deps)]
        for k in reads:
            self.readers.setdefault(k, []).append(op)
        for k in writes:
            self.last_w[k] = op
            self.readers[k] = []
        self.ops.append(op)
        self.last_on_eng[eng] = op
        if op.is_dma:
            self.last_dma[dma_key] = op
        if final:
            self.final_dma.append(op)
        return op

    def barrier(self, new_set=False):
        drains = []
        for e in ENGS:
            op = self.add(e, lambda eng: eng.drain())
            op.kind = "drain"
            drains.append(op)
        dmas = list(self.last_dma.values())
        for e in ENGS:
            op = self.add(e, None)
            op.kind = "join"
            op.deps = [(d, True) for d in drains if d.eng != e] + [(d, True) for d in dmas]
        self.last_w = {}
        self.readers = {}
        self.last_dma = {}
        if new_set:
            self.sset += 1

    @staticmethod
    def _need(op, d, raw):
        if d.is_dma or d.eng != op.eng:
            return True
        if op.eng == "pe":
            return False
        return True

    def emit(self, nc):
        for op in self.ops:
            for d, raw in op.deps:
                if self._need(op, d, raw):
                    d.inc = True
        for op in self.final_dma:
            op.inc = True
        nsets = self.sset + 1
        cnt = {(e, s): 0 for e in ENGS for s in range(nsets)}
        dcnt = {}
        for op in self.ops:
            if op.is_dma:
                dcnt[op.dma_key] = dcnt.get(op.dma_key, 0) + 16 * op.ndma
                op.val = dcnt[op.dma_key]
            elif op.inc:
                cnt[(op.eng, op.sset)] += 1
                op.val = cnt[(op.eng, op.sset)]
        self.stats = dict(maxcnt=max(cnt.values()), ndma_keys=len(dcnt), nops=len(self.ops),
                          maxd=max(dcnt.values()) if dcnt else 0)
        with ExitStack() as st:
            esem = {k: st.enter_context(nc.semaphore("s_%s%d" % k)) for k in cnt}
            dsem = {k: st.enter_context(nc.semaphore("d%d" % i)) for i, k in enumerate(dcnt)}
            block = st.enter_context(nc.Block())

            def semof(o):
                return dsem[o.dma_key] if o.is_dma else esem[(o.eng, o.sset)]

            def run(engname, e):
                waited = {}
                for op in self.ops:
                    if op.eng != engname:
                        continue
                    need = {}
                    for d, raw in op.deps:
                        if not self._need(op, d, raw):
                            continue
                        s = semof(d)
                        key = id(s)
                        if key not in need or need[key][1] < d.val:
                            need[key] = (s, d.val)
                    for key, (s, v) in need.items():
                        if waited.get(key, 0) >= v:
                            continue
                        waited[key] = v
                        e.wait_ge(s, v)
                    if op.fn is None:
                        continue
                    ins = op.fn(e)
                    if op.is_dma:
                        for i_ in (ins if isinstance(ins, list) else [ins]):
                            i_.then_inc(dsem[op.dma_key], 16)
                    elif op.inc:
                        ins.then_inc(esem[(op.eng, op.sset)], 1)
                if engname == "sp":
                    for k, v in dcnt.items():
                        e.wait_ge(dsem[k], v)

            block.tensor(lambda e: run("pe", e))
            block.scalar(lambda e: run("act", e))
            block.vector(lambda e: run("dve", e))
            block.gpsimd(lambda e: run("pool", e))
            block.sync(lambda e: run("sp", e))


class Arena:
    def __init__(self, handle, nbytes):
        self.h = handle
        self.n = nbytes
        self.off = 0
        self.mark_ = 0

    def alloc(self, free_shape, dt):
        esz = 2 if dt == BF16 else 4
        n = 1
        for s in free_shape:
            n *= s
        nb = (n * esz + 31) // 32 * 32
        assert self.off + nb <= self.n, ("arena overflow", self.off, nb, self.n)
        v = self.h[:, self.off:self.off + n * esz].bitcast(dt)
        self.off += nb
        if len(free_shape) == 2:
            v = v.rearrange("p (a b) -> p a b", a=free_shape[0])
        elif len(free_shape) == 3:
            v = v.rearrange("p (a b c) -> p a b c", a=free_shape[0], b=free_shape[1])
        return v

    def mark(self):
        self.mark_ = self.off

    def reset(self):
        self.off = self.mark_


class Builder:
    def __init__(self, seq_lens, depth=DEPTH, do_mixer=True, do_ffn2=True, mix_stage=3):
        self.mix_stage = mix_stage
        import os as _os0
        self.zero_thresh = float(_os0.environ.get("ZERO_THRESH", "160"))
        self.seq_lens = tuple(seq_lens)
        self.NT = sum(seq_lens)
        self.depth = depth
        self.do_mixer = do_mixer
        self.do_ffn2 = do_ffn2
        self.nc = bass.Bass("TRN2", target_bir_lowering=False)
        self.S = Sched()

    def dram_in(self, name, shape, dt=F32):
        return self.nc.dram_tensor(name, list(shape), dt, kind="ExternalInput").ap()

    def build(self):
        nc, S = self.nc, self.S
        NT = self.NT
        self.x = self.dram_in("x", [NT, D])
        self.y = nc.dram_tensor("y", [NT, D], F32, kind="ExternalOutput").ap()
        self.w_in = self.dram_in("w_in", [DEPTH, D, INW])
        self.w_out = self.dram_in("w_out", [DEPTH, D, D])
        self.dec_f = self.dram_in("ret_decay_f", [DEPTH, 8])
        self.dec_b = self.dram_in("ret_decay_b", [DEPTH, 8])
        self.gn_g = self.dram_in("ret_gn_g", [DEPTH, 512])
        self.gn_b = self.dram_in("ret_gn_b", [DEPTH, 512])
        self.lq1 = self.dram_in("diff_lq1", [DEPTH, 64])
        self.lk1 = self.dram_in("diff_lk1", [DEPTH, 64])
        self.lq2 = self.dram_in("diff_lq2", [DEPTH, 64])
        self.lk2 = self.dram_in("diff_lk2", [DEPTH, 64])
        self.subln = self.dram_in("diff_subln_g", [DEPTH, 128])
        self.dec_all = self.dram_in("dec_all", [DEPTH, 16])
        self.dec_pairs = self.dram_in("dec_pairs", [DEPTH, 2, 8])
        self.f1w13 = self.dram_in("ffn1_w13", [DEPTH, D, 2 * DFF])
        self.f1w2 = self.dram_in("ffn1_w2", [DEPTH, DFF, D])
        self.f2w13 = self.dram_in("ffn2_w13", [DEPTH, D, 2 * DFF])
        self.f2w2 = self.dram_in("ffn2_w2", [DEPTH, DFF, D])
        self.ln = {}
        for i in (1, 2, 3):
            self.ln[(i, "g")] = self.dram_in("ln%d_g" % i, [DEPTH, D])
            self.ln[(i, "b")] = self.dram_in("ln%d_b" % i, [DEPTH, D])
        self.XA = nc.dram_tensor("xa_scr", [NT, D], F32, kind="Internal").ap()
        self.XB = nc.dram_tensor("xb_scr", [NT, D], F32, kind="Internal").ap()
        self.MO = nc.dram_tensor("mo_scr", [8, 128, NT], BF16, kind="Internal").ap()

        with ExitStack() as st:
            nbytes = 206000
            arena_h = st.enter_context(nc.sbuf_tensor("arena", [128, nbytes], U8))
            self.A = Arena(arena_h, nbytes)
            self.ps = st.enter_context(nc.psum_tensor("ps", [128, 8, 512], F32))
            self.psb = self.ps[:].bitcast(BF16)
            self.setup_consts()
            self.A.mark()
            cur = self.x
            for l in range(self.depth):
                last = (l == self.depth - 1)
                self.phase_ffn(l, self.f1w13, self.f1w2, self.ln[(1, "g")], self.ln[(1, "b")], cur, self.XA)
                cur = self.XA
                if self.do_mixer:
                    self.phase_mixer(l, self.XA, self.XB)
                    cur = self.XB
                if self.do_ffn2:
                    dst = self.y if last else self.XA
                    src = cur
                    if src is self.XA:
                        dst = self.y if last else self.XB
                    self.phase_ffn(l, self.f2w13, self.f2w2, self.ln[(3, "g")], self.ln[(3, "b")], src, dst)
                    cur = dst
            if cur is not self.y:
                self.copy_out(cur)
            S.emit(nc)
        return nc

    def setup_consts(self):
        S, A = self.S, self.A
        self.ident = A.alloc([128], BF16)
        identf = A.alloc([128], F32)
        self.ones_bf = A.alloc([128], BF16)
        self.ones_f = A.alloc([128], F32)
        self.eps_c = A.alloc([1], F32)
        S.add("pool", lambda e: e.memset(identf, 0.0), writes=["identf"])
        S.add("pool", lambda e: e.affine_select(out=identf, in_=identf, pattern=[[-1, 128]],
                                                compare_op=ALU.not_equal, fill=1.0, base=0,
                                                channel_multiplier=1), reads=["identf"], writes=["identf"])
        S.add("pool", lambda e: e.tensor_copy(out=self.ident, in_=identf), reads=["identf"], writes=["ident"])
        S.add("pool", lambda e: e.memset(self.ones_bf, 1.0), writes=["ones_bf"])
        S.add("pool", lambda e: e.memset(self.ones_f, 1.0), writes=["ones_f"])
        S.add("pool", lambda e: e.memset(self.eps_c, EPS), writes=["eps_c"])
        S.barrier()

    def copy_out(self, cur):
        S, A = self.S, self.A
        A.reset()
        buf = [A.alloc([D], F32) for _ in range(2)]
        for blk in range(self.NT // 128):
            s = blk % 2
            rows = slice(blk * 128, (blk + 1) * 128)
            S.add("sp", lambda e, s=s, rows=rows: e.dma_start(out=buf[s], in_=cur[rows, :]),
                  writes=[("cb", s)], dma_key=("cb", s))
            S.add("sp", lambda e, s=s, rows=rows: e.dma_start(out=self.y[rows, :], in_=buf[s]),
                  reads=[("cb", s)], dma_key=("co", s), final=True)
        S.barrier()

    def ln_block(self, ysb, slot, lng, lnb, scr, pre):
        S = self.S
        st, mv, rs = scr["st"], scr["mv"], scr["rs"]
        ky = (pre + "y", slot)
        kst, kmv, krs = (pre + "st", slot), (pre + "mv", slot), (pre + "rs", slot)
        S.add("dve", lambda e: e.bn_stats(out=st[:, 0, :], in_=ysb[:, 0:512]), reads=[ky], writes=[(kst, 0)])
        S.add("dve", lambda e: e.bn_stats(out=st[:, 1, :], in_=ysb[:, 512:1024]), reads=[ky], writes=[(kst, 1)])
        S.add("dve", lambda e: e.bn_aggr(out=mv, in_=st), reads=[(kst, 0), (kst, 1)], writes=[kmv])
        S.add("dve", lambda e: e.tensor_scalar(out=rs, in0=mv[:, 1:2], scalar1=EPS, scalar2=None, op0=ALU.add),
              reads=[kmv], writes=[krs])
        S.add("act", lambda e: e.activation(out=rs, in_=rs, func=AF.Sqrt), reads=[krs], writes=[krs])
        S.add("dve", lambda e: e.reciprocal(out=rs, in_=rs), reads=[krs], writes=[krs])
        S.add("dve", lambda e: e.tensor_scalar(out=ysb, in0=ysb, scalar1=mv[:, 0:1], scalar2=rs,
                                               op0=ALU.subtract, op1=ALU.mult),
              reads=[ky, kmv, krs], writes=[ky])
        S.add("pool", lambda e: e.tensor_tensor(out=ysb, in0=ysb, in1=lng, op=ALU.mult),
              reads=[ky, "lng"], writes=[ky])
        S.add("pool", lambda e: e.tensor_tensor(out=ysb, in0=ysb, in1=lnb, op=ALU.add),
              reads=[ky, "lnb"], writes=[ky])

    def phase_ffn(self, l, w13_d, w2_d, g_d, b_d, src, dst):
        S, A, ps, psb = self.S, self.A, self.ps, self.psb
        A.reset()
        W13 = A.alloc([8, 2 * DFF], BF16)
        W2 = A.alloc([NFF, D], BF16)
        lng = A.alloc([D], F32)
        lnb = A.alloc([D], F32)
        xres = [A.alloc([D], F32) for _ in range(2)]
        xbf = [A.alloc([D], BF16) for _ in range(4)]
        xT = A.alloc([8, 512], BF16)
        gT = A.alloc([NFF, 512], BF16)
        sa = [A.alloc([512], F32) for _ in range(2)]
        yb = [A.alloc([D], F32) for _ in range(2)]
        scr = [dict(st=A.alloc([2, 6], F32), mv=A.alloc([2], F32), rs=A.alloc([1], F32)) for _ in range(2)]
        ident = self.ident
        final = dst is self.y

        for j in range(11):
            cs = slice(j * 512, (j + 1) * 512)
            S.add("pool", lambda e, cs=cs: e.dma_start(
                out=W13[:, :, cs], in_=w13_d[l, :, cs].rearrange("(k p) n -> p k n", p=128)),
                writes=[("W13", j)], dma_key=("W13", j))
        for j in range(11):
            S.add("pool", lambda e, j=j: e.dma_start(
                out=W2[:, 2 * j:2 * j + 2, :],
                in_=w2_d[l, 256 * j:256 * j + 256, :].rearrange("(k p) n -> p k n", p=128)),
                writes=[("W2", j)], dma_key=("W2", j))
        S.add("sp", lambda e: e.dma_start(out=lng, in_=g_d[l:l + 1, :].broadcast_to([128, D])),
              writes=["lng"], dma_key="lng")
        S.add("sp", lambda e: e.dma_start(out=lnb, in_=b_d[l:l + 1, :].broadcast_to([128, D])),
              writes=["lnb"], dma_key="lnb")

        ntile = self.NT // 512

        def load_tile(t):
            for b in range(4):
                rows = slice((4 * t + b) * 128, (4 * t + b + 1) * 128)
                S.add("pool", lambda e, b=b, rows=rows: e.dma_start(out=xbf[b], in_=src[rows, :]),
                      writes=[("xbf", b)], dma_key=("xbf", b))

        def transposes(t):
            for k in range(8):
                bank = 6 + (k % 2)

                def tr(e, k=k, bank=bank):
                    for b in range(4):
                        ins = e.transpose(out=psb[:, bank, b * 128:(b + 1) * 128],
                                          in_=xbf[b][:, k * 128:(k + 1) * 128], identity=ident)
                    return ins
                S.add("pe", tr, reads=[("xbf", b) for b in range(4)] + ["ident"], writes=[("ps", bank)])
                if k % 2 == 0:
                    S.add("act", lambda e, k=k, bank=bank: e.activation(out=xT[:, k, :], in_=psb[:, bank, 0:512],
                                                                       func=AF.Copy),
                          reads=[("ps", bank)], writes=[("xT", k)])
                else:
                    S.add("dve", lambda e, k=k, bank=bank: e.tensor_copy(out=xT[:, k, :], in_=psb[:, bank, 0:512]),
                          reads=[("ps", bank)], writes=[("xT", k)])

        def ffn1(t):
            for c in range(NFF):
                q = c % 2
                ba, bb = 2 * q, 2 * q + 1

                def mm(e, c=c, ba=ba, bb=bb):
                    for k in range(8):
                        e.matmul(ps[:, ba, :], lhsT=W13[:, k, c * 128:(c + 1) * 128], rhs=xT[:, k, :],
                                 start=(k == 0), stop=(k == 7))
                    for k in range(8):
                        ins = e.matmul(ps[:, bb, :], lhsT=W13[:, k, DFF + c * 128:DFF + (c + 1) * 128],
                                       rhs=xT[:, k, :], start=(k == 0), stop=(k == 7))
                    return ins
                S.add("pe", mm, reads=[("xT", k) for k in range(8)] + [("W13", c // 4), ("W13", (NFF + c) // 4)],
                      writes=[("ps", ba), ("ps", bb)])
                S.add("act", lambda e, q=q, ba=ba: e.activation(out=sa[q], in_=ps[:, ba, :], func=AF.Silu),
                      reads=[("ps", ba)], writes=[("sa", q)])
                S.add("dve", lambda e, c=c, q=q, bb=bb: e.tensor_tensor(out=gT[:, c, :], in0=sa[q], in1=ps[:, bb, :],
                                                                       op=ALU.mult),
                      reads=[("sa", q), ("ps", bb)], writes=[("gT", c)])

        def ffn2(t):
            for b in range(4):
                blk = 4 * t + b
                slot = blk % 2
                rows = slice(blk * 128, (blk + 1) * 128)
                b0 = 4 if b % 2 == 0 else 6
                S.add("sp", lambda e, slot=slot, rows=rows: e.dma_start(out=xres[slot], in_=src[rows, :]),
                      writes=[("xres", slot)], dma_key=("xres", slot))

                def mm(e, b=b, b0=b0):
                    for half in range(2):
                        for c in range(NFF):
                            ins = e.matmul(ps[:, b0 + half, :], lhsT=gT[:, c, b * 128:(b + 1) * 128],
                                           rhs=W2[:, c, half * 512:(half + 1) * 512],
                                           start=(c == 0), stop=(c == NFF - 1))
                    return ins
                S.add("pe", mm, reads=[("gT", c) for c in range(NFF)] + [("W2", j) for j in range(11)],
                      writes=[("ps", b0), ("ps", b0 + 1)])
                ysb = yb[slot]
                pso = ps[:, b0:b0 + 2, :].rearrange("p a b -> p (a b)")
                S.add("act", lambda e, ysb=ysb, pso=pso: e.activation(out=ysb, in_=pso, func=AF.Identity, scale=0.5),
                      reads=[("ps", b0), ("ps", b0 + 1)], writes=[("fy", slot)])
                S.add("dve", lambda e, ysb=ysb, slot=slot: e.scalar_tensor_tensor(
                    out=ysb, in0=xres[slot], scalar=ALPHA, in1=ysb, op0=ALU.mult, op1=ALU.add),
                    reads=[("fy", slot), ("xres", slot)], writes=[("fy", slot)])
                self.ln_block(ysb, slot, lng, lnb, scr[slot], "f")
                S.add("pool", lambda e, ysb=ysb, rows=rows: e.dma_start(out=dst[rows, :], in_=ysb),
                      reads=[("fy", slot)], dma_key=("yst", slot), final=final)

        load_tile(0)
        transposes(0)
        for t in range(ntile):
            if t + 1 < ntile:
                load_tile(t + 1)
            ffn1(t)
            if t + 1 < ntile:
                transposes(t + 1)
            ffn2(t)
        S.barrier(new_set=True)

    def phase_mixer(self, l, src, dst):
        S, A, ps, psb = self.S, self.A, self.ps, self.psb
        A.reset()
        ident = self.ident
        TM = max(self.seq_lens)
        NBM = TM // 128
        I32 = mybir.dt.int32
        lam0 = lambda_init(l)
        SCALE = 64 ** -0.5
        hT = A.alloc([8, TM], BF16)
        DT = A.alloc([8, 128], F32)
        QF = A.alloc([4, 128], F32)
        QB = A.alloc([4, 128], F32)
        Eq = A.alloc([512], F32)
        Dg = A.alloc([4, 512], F32)
        gng = A.alloc([512], F32)
        gnb = A.alloc([512], F32)
        lgall = A.alloc([16], F32)
        lgp = A.alloc([8], F32)
        KFB = A.alloc([16], F32)
        GC = A.alloc([8], F32)
        c127 = A.alloc([1], F32)
        cp = A.alloc([1], F32)
        neglam = A.alloc([1], F32)
        sg = A.alloc([1], F32)
        lsum = A.alloc([2], F32)
        A2 = A.off
        lqk = A.alloc([4, 64], F32)
        tI = A.alloc([512], I32)
        tE = A.alloc([128], F32)
        tP = A.alloc([128], F32)
        tN = A.alloc([128], F32)
        tM = A.alloc([128], F32)
        tI1 = A.alloc([128], F32)

        S.add("sp", lambda e: e.dma_start(out=gng, in_=self.gn_g[l:l + 1, :].broadcast_to([128, 512])),
              writes=["gng"], dma_key="gng")
        S.add("sp", lambda e: e.dma_start(out=gnb, in_=self.gn_b[l:l + 1, :].broadcast_to([128, 512])),
              writes=["gnb"], dma_key="gnb")
        S.add("sp", lambda e: e.dma_start(out=lgall, in_=self.dec_all[l:l + 1, :].broadcast_to([128, 16])),
              writes=["lgall"], dma_key="lgall")
        for half in range(2):
            S.add("sp", lambda e, half=half: e.dma_start(
                out=lgp[64 * half:64 * half + 64, :], in_=self.dec_pairs[l, half:half + 1, :].broadcast_to([64, 8])),
                writes=[("lgp", half)], dma_key=("lgp", half))
        for i, t in enumerate((self.lq1, self.lk1, self.lq2, self.lk2)):
            S.add("sp", lambda e, i=i, t=t: e.dma_start(out=lqk[:, i, :], in_=t[l:l + 1, :].broadcast_to([128, 64])),
                  writes=[("lqk", i)], dma_key=("lqk", i))
        S.add("sp", lambda e: e.dma_start(out=sg, in_=self.subln[l, :].rearrange("(p o) -> p o", o=1)),
              writes=["sg"], dma_key="sg")
        S.barrier()
        for t_, nm in ((lgall, "lgall"), (lgp, "lgp")):
            S.add("act", lambda e, t_=t_: e.activation(out=t_, in_=t_, func=AF.Exp, scale=-1.0), reads=[nm], writes=[nm])
            S.add("act", lambda e, t_=t_: e.activation(out=t_, in_=t_, func=AF.Ln, bias=self.ones_f[:, 0:1]),
                  reads=[nm], writes=[nm])
            S.add("pool", lambda e, t_=t_: e.tensor_scalar(out=t_, in0=t_, scalar1=-1.0, scalar2=None, op0=ALU.mult),
                  reads=[nm], writes=[nm])
        S.add("pool", lambda e: e.iota(out=tI, pattern=[[1, 512]], base=0, channel_multiplier=-1), writes=["tI"])
        S.add("dve", lambda e: e.tensor_copy(out=Eq, in_=tI), reads=["tI"], writes=["Eq"])
        S.add("dve", lambda e: e.tensor_copy(out=tE, in_=Eq[:, 0:128]), reads=["Eq"], writes=["tE"])
        S.add("dve", lambda e: e.tensor_scalar(out=tP, in0=tE, scalar1=0.0, scalar2=None, op0=ALU.max),
              reads=["tE"], writes=["tP"])
        S.add("dve", lambda e: e.tensor_tensor(out=tN, in0=tP, in1=tE, op=ALU.subtract), reads=["tP", "tE"], writes=["tN"])
        for d_ in range(4):
            S.add("act", lambda e, d_=d_: e.activation(out=Dg[:, d_, :], in_=Eq, func=AF.Abs, bias=float(-128 * d_)),
                  reads=["Eq"], writes=[("Dg", d_)])
        S.add("dve", lambda e: e.tensor_copy(out=c127, in_=Eq[:, 127:128]), reads=["Eq"], writes=["c127"])
        S.add("dve", lambda e: e.tensor_scalar(out=cp, in0=Eq[:, 0:1], scalar1=-1.0, scalar2=None, op0=ALU.mult),
              reads=["Eq"], writes=["cp"])
        S.add("dve", lambda e: e.tensor_scalar(out=tI1, in0=Eq[:, 0:128], scalar1=cp, scalar2=1.0, op0=ALU.add, op1=ALU.add),
              reads=["Eq", "cp"], writes=["tI1"])
        S.barrier()
        for h in range(8):
            S.add("act", lambda e, h=h: e.activation(out=tM, in_=tP, func=AF.Exp, scale=lgall[:, h:h + 1]),
                  reads=["tM"], writes=["tM"])
            S.add("pool", lambda e, h=h: e.affine_select(out=DT[:, h, :], in_=tM, pattern=[[1, 128]], compare_op=ALU.is_ge,
                                                         fill=0.0, base=0, channel_multiplier=-1),
                  reads=["tM"], writes=[("DT", h)])
            S.add("act", lambda e, h=h: e.activation(out=tM, in_=tN, func=AF.Exp, scale=lgall[:, 8 + h:9 + h]),
                  reads=["tM"], writes=["tM"])
            S.add("pool", lambda e, h=h: e.affine_select(out=tM, in_=tM, pattern=[[-1, 128]], compare_op=ALU.is_gt,
                                                         fill=0.0, base=0, channel_multiplier=1),
                  reads=["tM"], writes=["tM"])
            S.add("pool", lambda e, h=h: e.tensor_tensor(out=DT[:, h, :], in0=DT[:, h, :], in1=tM, op=ALU.add),
                  reads=["tM", ("DT", h)], writes=[("DT", h)])
        for p in range(4):
            S.add("act", lambda e, p=p: e.activation(out=QF[:, p, :], in_=tI1, func=AF.Exp, scale=lgp[:, p:p + 1]),
                  writes=[("QF", p)])
            S.add("dve", lambda e, p=p: e.tensor_scalar(out=QB[:, p, :], in0=tI1, scalar1=-1.0, scalar2=129.0,
                                                       op0=ALU.mult, op1=ALU.add), writes=[("QB", p)])
            S.add("act", lambda e, p=p: e.activation(out=QB[:, p, :], in_=QB[:, p, :], func=AF.Exp,
                                                     scale=lgp[:, 4 + p:5 + p]), reads=[("QB", p)], writes=[("QB", p)])
        S.add("act", lambda e: e.activation(out=KFB[:, 0:8], in_=lgall[:, 0:8], func=AF.Exp, scale=c127), writes=["KF"])
        S.add("act", lambda e: e.activation(out=KFB[:, 8:16], in_=lgall[:, 8:16], func=AF.Exp, scale=cp), writes=["KB"])
        S.add("act", lambda e: e.activation(out=GC, in_=lgp, func=AF.Exp, scale=128.0), writes=["GC"])
        S.add("dve", lambda e: e.tensor_tensor(out=lqk[:, 0, :], in0=lqk[:, 0, :], in1=lqk[:, 1, :], op=ALU.mult), writes=["lq0"])
        S.add("dve", lambda e: e.tensor_tensor(out=lqk[:, 2, :], in0=lqk[:, 2, :], in1=lqk[:, 3, :], op=ALU.mult), writes=["lq2"])
        S.add("dve", lambda e: e.reduce_sum(out=lsum[:, 0:1], in_=lqk[:, 0, :], axis=AX.X), reads=["lq0"], writes=["ls0"])
        S.add("dve", lambda e: e.reduce_sum(out=lsum[:, 1:2], in_=lqk[:, 2, :], axis=AX.X), reads=["lq2"], writes=["ls1"])
        S.add("act", lambda e: e.activation(out=lsum, in_=lsum, func=AF.Exp), reads=["ls0", "ls1"], writes=["lse"])
        S.add("dve", lambda e: e.tensor_tensor(out=neglam, in0=lsum[:, 1:2], in1=lsum[:, 0:1], op=ALU.subtract),
              reads=["lse"], writes=["nl"])
        S.add("dve", lambda e: e.tensor_scalar(out=neglam, in0=neglam, scalar1=-lam0, scalar2=None, op0=ALU.add),
              reads=["nl"], writes=["nl"])
        S.add("pool", lambda e: e.tensor_scalar(out=sg, in0=sg, scalar1=1.0 - lam0, scalar2=None, op0=ALU.mult), writes=["sg2"])
        S.add("pool", lambda e: e.tensor_scalar(out=KFB, in0=KFB, scalar1=0.125, scalar2=0.0, op0=ALU.mult, op1=ALU.add),
              reads=["KF", "KB"], writes=["KF8"])
        S.barrier()

        slopes = [2.0 ** (-8.0 * (h + 1) / 4) for h in range(4)]
        ZERO_THRESH = self.zero_thresh
        def do_b1(T, tok0):
            nblk = T // 128
            ntile = T // 512
            A.off = A2
            xbf = [A.alloc([D], BF16) for _ in range(8)]
            for t in range(ntile):
                for b in range(4):
                    sl = (t % 2) * 4 + b
                    rows = slice(tok0 + (4 * t + b) * 128, tok0 + (4 * t + b + 1) * 128)
                    S.add("pool", lambda e, sl=sl, rows=rows: e.dma_start(out=xbf[sl], in_=src[rows, :]),
                          writes=[("xbf", sl)], dma_key=("xbf", sl))
                for k in range(8):
                    bank = 6 + (k % 2)

                    def tr(e, k=k, bank=bank, t=t):
                        for b in range(4):
                            ins = e.transpose(out=psb[:, bank, b * 128:(b + 1) * 128],
                                              in_=xbf[(t % 2) * 4 + b][:, k * 128:(k + 1) * 128], identity=ident)
                        return ins
                    S.add("pe", tr, reads=[("xbf", (t % 2) * 4 + b) for b in range(4)], writes=[("ps", bank)])
                    dsl = hT[:, k, t * 512:(t + 1) * 512]
                    if k % 2 == 0:
                        S.add("act", lambda e, dsl=dsl, bank=bank: e.activation(out=dsl, in_=psb[:, bank, 0:512], func=AF.Copy),
                              reads=[("ps", bank)], writes=[("hT", t)])
                    else:
                        S.add("dve", lambda e, dsl=dsl, bank=bank: e.tensor_copy(out=dsl, in_=psb[:, bank, 0:512]),
                              reads=[("ps", bank)], writes=[("hT", t)])
            S.barrier()

        def do_ret(p, T, tok0):
            nblk = T // 128
            ntile = T // 512
            if True:
                A.off = A2
                wr = A.alloc([8, 512], BF16)
                rqT = A.alloc([T], BF16)
                rkTa = A.alloc([T], BF16)
                rkTb = A.alloc([T], BF16)
                kdf_all = A.alloc([nblk, 128], BF16)
                kdb_all = A.alloc([nblk, 128], BF16)
                qf_all = A.alloc([T], BF16)
                qb_all = A.alloc([T], BF16)
                rv_tm = A.alloc([nblk, 128], BF16)
                gate = A.alloc([nblk, 128], BF16)
                Sb16a = A.alloc([nblk, 64], BF16)
                Sb16b = A.alloc([nblk, 64], BF16)
                moT = A.alloc([T], BF16)
                Sf32 = A.alloc([64], F32)
                Sb32 = A.alloc([64], F32)
                Sf16a = [A.alloc([64], BF16) for _ in range(2)]
                Sf16b = [A.alloc([64], BF16) for _ in range(2)]
                AT = [A.alloc([256], BF16) for _ in range(2)]
                ob = [A.alloc([128], F32) for _ in range(2)]
                rob = [A.alloc([128], BF16) for _ in range(2)]
                gst = [A.alloc([2, 6], F32) for _ in range(2)]
                gmv = [A.alloc([2, 2], F32) for _ in range(2)]
                grs = [A.alloc([2], F32) for _ in range(2)]
                for j, c0 in enumerate((128 * p, 512 + 128 * p, 1024 + 128 * p, 1536 + 128 * p)):
                    S.add("pool", lambda e, j=j, c0=c0: [e.dma_start(
                        out=wr[:, k, 128 * j:128 * j + 128],
                        in_=self.w_in[l, k * 128:(k + 1) * 128, c0:c0 + 128]) for k in range(8)],
                        writes=[("wr", j)], dma_key=("wsl", j), ndma=8)
                for t in range(ntile):
                    for j, (dstT, nm, sc) in enumerate(((rqT, "rqT", 1.0), (rkTa, "rkT", 0.125))):
                        bank = (2 * t + j) % 4

                        def mm(e, t=t, j=j, bank=bank):
                            for k in range(8):
                                ins = e.matmul(ps[:, bank, :], lhsT=wr[:, k, 128 * j:128 * j + 128],
                                               rhs=hT[:, k, t * 512:(t + 1) * 512], start=(k == 0), stop=(k == 7))
                            return ins
                        S.add("pe", mm, reads=[("hT", t), ("wr", j)], writes=[("ps", bank)])
                        S.add("act", lambda e, dstT=dstT, t=t, bank=bank, sc=sc: e.activation(
                            out=dstT[:, t * 512:(t + 1) * 512], in_=ps[:, bank, :], func=AF.Identity, scale=sc),
                            reads=[("ps", bank)], writes=[(nm, t)])
                        if j == 0:
                            ts_ = slice(t * 512, (t + 1) * 512)
                            for dst_, tab_, nm_ in ((qf_all, QF, "qf"), (qb_all, QB, "qb")):
                                S.add("pool", lambda e, dst_=dst_, tab_=tab_, ts_=ts_: e.tensor_tensor(
                                    out=dst_[:, ts_].rearrange("p (c i) -> p c i", c=4),
                                    in0=rqT[:, ts_].rearrange("p (c i) -> p c i", c=4),
                                    in1=tab_[:, p:p + 1, :].broadcast_to([128, 4, 128]), op=ALU.mult),
                                    reads=[("rqT", t)], writes=[(nm_, t)])
                        if j == 1:
                            S.add("act", lambda e, t=t, bank=bank, sc=sc: e.activation(
                                out=rkTb[:, t * 512:(t + 1) * 512], in_=ps[:, bank, :], func=AF.Identity, scale=sc),
                                reads=[("ps", bank)], writes=[("rkTb", t)])
                            S.add("dve", lambda e, t=t: e.memset(rkTa[64:128, t * 512:(t + 1) * 512], 0.0),
                                  reads=[("rkT", t)], writes=[("rkT", t)])
                            S.add("dve", lambda e, t=t: e.memset(rkTb[0:64, t * 512:(t + 1) * 512], 0.0),
                                  reads=[("rkTb", t)], writes=[("rkTb", t)])
                for blk in range(nblk):
                    bank = 4 + blk % 4

                    def mm(e, blk=blk, bank=bank):
                        for k in range(8):
                            ins = e.matmul(ps[:, bank, 0:384], lhsT=hT[:, k, blk * 128:(blk + 1) * 128],
                                           rhs=wr[:, k, 128:512], start=(k == 0), stop=(k == 7))
                        return ins
                    S.add("pe", mm, reads=[("hT", blk // 4), ("wr", 1), ("wr", 2), ("wr", 3)], writes=[("ps", bank)])
                    for hh in range(2):
                        hs_ = slice(64 * hh, 64 * hh + 64)
                        S.add("act", lambda e, blk=blk, bank=bank, hs_=hs_, hh=hh: e.activation(
                            out=kdf_all[:, blk, hs_], in_=ps[:, bank, hs_], func=AF.Identity,
                            scale=KFB[:, 2 * p + hh:2 * p + hh + 1]),
                            reads=[("ps", bank)], writes=[("kdf", blk, hh)])
                        S.add("act", lambda e, blk=blk, bank=bank, hs_=hs_, hh=hh: e.activation(
                            out=kdb_all[:, blk, hs_], in_=ps[:, bank, hs_], func=AF.Identity,
                            scale=KFB[:, 8 + 2 * p + hh:8 + 2 * p + hh + 1]),
                            reads=[("ps", bank)], writes=[("kdb", blk, hh)])
                    S.add("act", lambda e, blk=blk, bank=bank: e.activation(out=rv_tm[:, blk, :], in_=ps[:, bank, 128:256],
                                                                           func=AF.Copy),
                          reads=[("ps", bank)], writes=[("rv_tm", blk)])
                    S.add("act", lambda e, blk=blk, bank=bank: e.activation(out=gate[:, blk, :], in_=ps[:, bank, 256:384],
                                                                           func=AF.Silu),
                          reads=[("ps", bank)], writes=[("gate", blk)])
                S.add("pool", lambda e: e.memset(Sb32, 0.0), writes=["Sb32"])
                S.add("pool", lambda e: e.memset(Sf32, 0.0), writes=["Sf32"])
                S.add("pool", lambda e: e.memset(Sb16a, 0.0), writes=["Sb16a"])
                S.add("pool", lambda e: e.memset(Sb16b, 0.0), writes=["Sb16b"])
                for s_ in range(2):
                    S.add("pool", lambda e, s_=s_: e.memset(Sf16a[s_], 0.0), writes=[("Sf16", s_)])
                    S.add("pool", lambda e, s_=s_: e.memset(Sf16b[s_], 0.0), writes=[("Sf16", s_)])

                def state_update(c, S32, nm, kcol, gcol, ubank):
                    s = c % 2
                    kd_all = kdf_all if kcol == 0 else kdb_all
                    knm = "kdf" if kcol == 0 else "kdb"
                    S.add("pe", lambda e, c=c: e.matmul(ps[:, ubank, 0:128], lhsT=kd_all[:, c, :], rhs=rv_tm[:, c, :],
                                                         start=True, stop=True),
                          reads=[(knm, c, 0), (knm, c, 1), ("rv_tm", c)], writes=[("ps", ubank)])
                    for hh in range(2):
                        rs_ = slice(64 * hh, 64 * hh + 64)
                        S.add("dve", lambda e, rs_=rs_, hh=hh: e.scalar_tensor_tensor(
                            out=S32[rs_, :], in0=S32[rs_, :], scalar=GC[rs_, gcol + p:gcol + p + 1],
                            in1=ps[rs_, ubank, 64 * hh:64 * hh + 64], op0=ALU.mult, op1=ALU.add),
                            reads=[("ps", ubank), (nm, hh)], writes=[(nm, hh)])

                import os as _os2
                RL = int(_os2.environ.get("RET_LEVEL", "2"))
                for c in (range(nblk - 1, -1, -1) if RL >= 1 else []):
                    S.add("dve", lambda e, c=c: e.tensor_copy(out=Sb16a[0:64, c, :], in_=Sb32[0:64, :]),
                          reads=[("Sb32", 0), ("Sb32", 1), "Sb32", "Sb16a"], writes=[("Sb16", c)])
                    S.add("dve", lambda e, c=c: e.tensor_copy(out=Sb16b[64:128, c, :], in_=Sb32[64:128, :]),
                          reads=[("Sb32", 0), ("Sb32", 1), "Sb32", "Sb16b"], writes=[("Sb16", c)])
                    if c > 0:
                        state_update(c, Sb32, "Sb32", 8, 4, 4 + c % 2)
                def stage1(c):
                    s = c % 2
                    cs = slice(c * 128, (c + 1) * 128)
                    S.add("dve", lambda e, s=s: e.tensor_copy(out=Sf16a[s][0:64, :], in_=Sf32[0:64, :]),
                          reads=[("Sf32", 0), ("Sf32", 1), "Sf32"], writes=[("Sf16", s)])
                    S.add("dve", lambda e, s=s: e.tensor_copy(out=Sf16b[s][64:128, :], in_=Sf32[64:128, :]),
                          reads=[("Sf32", 0), ("Sf32", 1), "Sf32"], writes=[("Sf16", s)])
                    sbank = s

                    def mmS(e, cs=cs, sbank=sbank):
                        e.matmul(ps[:, sbank, 0:128], lhsT=rkTa[:, cs], rhs=rqT[:, cs], start=True, stop=True)
                        return e.matmul(ps[:, sbank, 128:256], lhsT=rkTb[:, cs], rhs=rqT[:, cs], start=True, stop=True)
                    S.add("pe", mmS, reads=[("rqT", c // 4), ("rkT", c // 4), ("rkTb", c // 4)], writes=[("ps", sbank)])
                    S.add("dve", lambda e, s=s, sbank=sbank: e.tensor_tensor(
                        out=AT[s], in0=ps[:, sbank, 0:256], in1=DT[:, 2 * p:2 * p + 2, :].rearrange("p a b -> p (a b)"),
                        op=ALU.mult), reads=[("ps", sbank)], writes=[("AT", s)])
                    obank = 2 + s

                    def mmO(e, s=s, c=c, cs=cs, obank=obank):
                        for hh in range(2):
                            rs_ = slice(64 * hh, 64 * hh + 64)
                            o_ = ps[:, obank, 64 * hh:64 * hh + 64]
                            e.matmul(o_, lhsT=AT[s][:, 128 * hh:128 * hh + 128], rhs=rv_tm[:, c, 64 * hh:64 * hh + 64],
                                     start=True, stop=False)
                            e.matmul(o_, lhsT=qf_all[:, cs], rhs=(Sf16a if hh == 0 else Sf16b)[s], start=False, stop=False)
                            ins = e.matmul(o_, lhsT=qb_all[:, cs], rhs=(Sb16a if hh == 0 else Sb16b)[:, c, :], start=False, stop=True)
                        return ins
                    S.add("pe", mmO, reads=[("AT", s), ("rv_tm", c), ("qf", c // 4), ("qb", c // 4), ("Sf16", s), ("Sb16", c)],
                          writes=[("ps", obank)])
                    if c < nblk - 1:
                        state_update(c, Sf32, "Sf32", 0, 0, 4 + s)

                def stage2(c):
                    s = c % 2
                    cs = slice(c * 128, (c + 1) * 128)
                    obank = 2 + s
                    S.add("act", lambda e, s=s, obank=obank: e.activation(out=ob[s], in_=ps[:, obank, 0:128], func=AF.Copy),
                          reads=[("ps", obank)], writes=[("ob", s)])
                    for hh in range(2):
                        S.add("dve", lambda e, s=s, hh=hh: e.bn_stats(out=gst[s][:, hh, :], in_=ob[s][:, 64 * hh:64 * hh + 64]),
                              reads=[("ob", s)], writes=[("gst", s, hh)])
                        S.add("dve", lambda e, s=s, hh=hh: e.bn_aggr(out=gmv[s][:, hh, :], in_=gst[s][:, hh, :]),
                              reads=[("gst", s, hh)], writes=[("gmv", s, hh)])
                    S.add("dve", lambda e, s=s: e.tensor_scalar(out=grs[s], in0=gmv[s][:, :, 1], scalar1=EPS, scalar2=None,
                                                                op0=ALU.add),
                          reads=[("gmv", s, 0), ("gmv", s, 1)], writes=[("grs", s)])
                    S.add("act", lambda e, s=s: e.activation(out=grs[s], in_=grs[s], func=AF.Ln), reads=[("grs", s)], writes=[("grs", s)])
                    S.add("act", lambda e, s=s: e.activation(out=grs[s], in_=grs[s], func=AF.Exp, scale=-0.5),
                          reads=[("grs", s)], writes=[("grs", s)])

                def stage2b(c):
                    s = c % 2
                    cs = slice(c * 128, (c + 1) * 128)
                    for hh in range(2):
                        S.add("dve", lambda e, s=s, hh=hh: e.tensor_scalar(
                            out=ob[s][:, 64 * hh:64 * hh + 64], in0=ob[s][:, 64 * hh:64 * hh + 64],
                            scalar1=gmv[s][:, hh, 0:1], scalar2=grs[s][:, hh:hh + 1], op0=ALU.subtract, op1=ALU.mult),
                            reads=[("ob", s), ("grs", s), ("gmv", s, hh)], writes=[("ob", s)])
                    S.add("pool", lambda e, s=s: e.tensor_tensor(out=ob[s], in0=ob[s], in1=gng[:, 128 * p:128 * p + 128], op=ALU.mult),
                          reads=[("ob", s)], writes=[("ob", s)])
                    S.add("pool", lambda e, s=s: e.tensor_tensor(out=ob[s], in0=ob[s], in1=gnb[:, 128 * p:128 * p + 128], op=ALU.add),
                          reads=[("ob", s)], writes=[("ob", s)])
                    S.add("pool", lambda e, s=s, c=c: e.tensor_tensor(out=rob[s], in0=ob[s], in1=gate[:, c, :], op=ALU.mult),
                          reads=[("ob", s), ("gate", c)], writes=[("rob", s)])

                def stage2c(c):
                    s = c % 2
                    cs = slice(c * 128, (c + 1) * 128)
                    tbank = 6 + s
                    S.add("pe", lambda e, s=s, tbank=tbank: e.transpose(out=psb[:, tbank, 0:128], in_=rob[s], identity=ident),
                          reads=[("rob", s)], writes=[("ps", tbank)])
                    S.add("act", lambda e, cs=cs, tbank=tbank: e.activation(out=moT[:, cs], in_=psb[:, tbank, 0:128], func=AF.Copy),
                          reads=[("ps", tbank)], writes=[("moT", c // 4)])

                if RL >= 2:
                    stage1(0)
                    for c in range(nblk):
                        stage2(c)
                        if c + 1 < nblk:
                            stage1(c + 1)
                        stage2b(c)
                        if c >= 1:
                            stage2c(c - 1)
                    stage2c(nblk - 1)
                import os as _os
                if _os.environ.get("RET_NOMO") != "1":
                    S.add("sp", lambda e, p=p, T=T, tok0=tok0: e.dma_start(out=self.MO[p, :, tok0:tok0 + T], in_=moT),
                          reads=[("moT", t) for t in range(ntile)], writes=[("MO", p)], dma_key=("mo", p % 2))
                S.barrier()

        def do_att(h, T, tok0):
            nblk = T // 128
            ntile = T // 512
            if True:
                A.off = A2
                m_h = slopes[h]
                wa = A.alloc([8, 384], BF16)
                qT = A.alloc([T], BF16)
                kTa = A.alloc([T], BF16)
                kTb = A.alloc([T], BF16)
                V = A.alloc([nblk, 128], BF16)
                moT = A.alloc([T], BF16)
                tmp = [A.alloc([512], F32) for _ in range(4)]
                PT = [A.alloc([512], BF16) for _ in range(4)]
                rd = [A.alloc([512], F32) for _ in range(2)]
                tt = [A.alloc([512], F32) for _ in range(2)]
                deferred = []
                att = A.alloc([512], F32)
                sq = A.alloc([512], BF16)
                rr = A.alloc([512], F32)
                for j in range(3):
                    c0 = 2048 + 512 * j + 128 * h
                    S.add("pool", lambda e, j=j, c0=c0: [e.dma_start(
                        out=wa[:, k, 128 * j:128 * j + 128],
                        in_=self.w_in[l, k * 128:(k + 1) * 128, c0:c0 + 128]) for k in range(8)],
                        writes=[("wa", j)], dma_key=("wsl", j), ndma=8)
                for t in range(ntile):
                    for j, (dstT, nm) in enumerate(((qT, "qT"), (kTa, "kT"))):
                        bank = (2 * t + j) % 4

                        def mm(e, t=t, j=j, bank=bank):
                            for k in range(8):
                                ins = e.matmul(ps[:, bank, :], lhsT=wa[:, k, 128 * j:128 * j + 128],
                                               rhs=hT[:, k, t * 512:(t + 1) * 512], start=(k == 0), stop=(k == 7))
                            return ins
                        S.add("pe", mm, reads=[("hT", t), ("wa", j)], writes=[("ps", bank)])
                        S.add("act", lambda e, dstT=dstT, t=t, bank=bank: e.activation(
                            out=dstT[:, t * 512:(t + 1) * 512], in_=ps[:, bank, :], func=AF.Copy),
                            reads=[("ps", bank)], writes=[(nm, t)])
                        if j == 1:
                            S.add("act", lambda e, t=t, bank=bank: e.activation(
                                out=kTb[:, t * 512:(t + 1) * 512], in_=ps[:, bank, :], func=AF.Copy),
                                reads=[("ps", bank)], writes=[("kTb", t)])
                            S.add("dve", lambda e, t=t: e.memset(kTa[64:128, t * 512:(t + 1) * 512], 0.0),
                                  reads=[("kT", t)], writes=[("kT", t)])
                            S.add("dve", lambda e, t=t: e.memset(kTb[0:64, t * 512:(t + 1) * 512], 0.0),
                                  reads=[("kTb", t)], writes=[("kTb", t)])
                for blk in range(nblk):
                    bank = 4 + blk % 4

                    def mm(e, blk=blk, bank=bank):
                        for k in range(8):
                            ins = e.matmul(ps[:, bank, 0:128], lhsT=hT[:, k, blk * 128:(blk + 1) * 128],
                                           rhs=wa[:, k, 256:384], start=(k == 0), stop=(k == 7))
                        return ins
                    S.add("pe", mm, reads=[("hT", blk // 4), ("wa", 2)], writes=[("ps", bank)])
                    S.add("dve", lambda e, blk=blk, bank=bank: e.tensor_copy(out=V[:, blk, :], in_=ps[:, bank, 0:128]),
                          reads=[("ps", bank)], writes=[("V", blk)])

                itc = [0]

                def post_chain(qt):
                    qs = slice(qt * 512, (qt + 1) * 512)
                    ops = []
                    for mp in range(2):
                        ops.append(lambda mp=mp: S.add("dve", lambda e: e.reciprocal(out=rd[mp], in_=rd[mp]),
                                                       reads=[("rd", mp)], writes=[("rd", mp)]))
                        ops.append(lambda mp=mp: S.add("dve", lambda e: e.tensor_tensor(out=tt[mp], in0=tt[mp], in1=rd[mp], op=ALU.mult),
                                                       reads=[("tt", mp), ("rd", mp)], writes=[("tt", mp)]))
                    ops.append(lambda: S.add("dve", lambda e: e.scalar_tensor_tensor(
                        out=att, in0=tt[1], scalar=neglam, in1=tt[0], op0=ALU.mult, op1=ALU.add),
                        reads=[("tt", 0), ("tt", 1)], writes=["att"]))
                    ops.append(lambda: S.add("act", lambda e: e.activation(out=sq, in_=att, func=AF.Square),
                                             reads=["att"], writes=["sq"]))

                    def ssq():
                        qbank = 2 * (itc[0] % 2)
                        S.add("pe", lambda e: e.matmul(ps[:, qbank, :], lhsT=self.ones_bf, rhs=sq, start=True, stop=True),
                              reads=["sq"], writes=[("ps", qbank)])
                        S.add("act", lambda e: e.activation(out=rr, in_=ps[:, qbank, :], func=AF.Ln, bias=self.eps_c,
                                                            scale=1.0 / 128),
                              reads=[("ps", qbank)], writes=["rr"])
                    ops.append(ssq)
                    ops.append(lambda: S.add("act", lambda e: e.activation(out=rr, in_=rr, func=AF.Exp, scale=-0.5),
                                             reads=["rr"], writes=["rr"]))
                    ops.append(lambda: S.add("dve", lambda e: e.tensor_tensor(out=att, in0=att, in1=rr, op=ALU.mult),
                                             reads=["att", "rr"], writes=["att"]))
                    ops.append(lambda: S.add("dve", lambda e: e.tensor_scalar(out=moT[:, qs], in0=att, scalar1=sg, scalar2=None,
                                                                              op0=ALU.mult),
                                             reads=["att"], writes=[("moT", qt)]))
                    return ops

                pending_pv = []

                def flush_pv():
                    while pending_pv:
                        pending_pv.pop(0)()

                def min_dist(qt, kc):
                    dk_ = 4 * qt - kc
                    if dk_ > 0:
                        return 128 * dk_ - 127
                    if dk_ <= -4:
                        return -128 * dk_ - 511
                    return 0

                for qt in range(ntile):
                    qs = slice(qt * 512, (qt + 1) * 512)
                    kcs = [kc for kc in range(nblk) if m_h * min_dist(qt, kc) < ZERO_THRESH]
                    for ki, kc in enumerate(kcs):
                        first_k = (ki == 0)
                        last_k = (ki == len(kcs) - 1)
                        ks = slice(kc * 128, (kc + 1) * 128)
                        par = itc[0] % 2
                        itc[0] += 1
                        dk_ = 4 * qt - kc
                        for mp in range(2):
                            bank = 2 * par + mp
                            sl = 2 * par + mp
                            rs_ = slice(64 * mp, 64 * mp + 64)
                            kTm = kTa if mp == 0 else kTb
                            S.add("pe", lambda e, bank=bank, kTm=kTm, ks=ks, qs=qs: e.matmul(
                                ps[:, bank, :], lhsT=kTm[:, ks], rhs=qT[:, qs], start=True, stop=True),
                                reads=[("kT", kc // 4), ("kTb", kc // 4), ("qT", qt)], writes=[("ps", bank)])
                            if dk_ > 0:
                                tbl, c1, c2 = Eq, -m_h / SCALE, -m_h * 128.0 * dk_
                            elif dk_ <= -4:
                                tbl, c1, c2 = Eq, m_h / SCALE, m_h * 128.0 * dk_
                            else:
                                tbl, c1, c2 = Dg[:, -dk_, :], -m_h / SCALE, 0.0
                            S.add("dve", lambda e, sl=sl, bank=bank, tbl=tbl, c1=c1: e.scalar_tensor_tensor(
                                out=tmp[sl], in0=tbl, scalar=c1, in1=ps[:, bank, :], op0=ALU.mult, op1=ALU.add),
                                reads=[("ps", bank)], writes=[("tmp", sl)])
                            S.add("act", lambda e, sl=sl, c2=c2: e.activation(out=PT[sl], in_=tmp[sl], func=AF.Exp,
                                                                              bias=float(c2), scale=SCALE),
                                  reads=[("tmp", sl)], writes=[("PT", sl)])

                        def mmPV(e, par=par, kc=kc, first_k=first_k, last_k=last_k):
                            for mp in range(2):
                                e.matmul(ps[:, 4 + mp, :], lhsT=V[:, kc, :], rhs=PT[2 * par + mp], start=first_k, stop=last_k)
                                ins = e.matmul(ps[:, 6 + mp, :], lhsT=self.ones_bf, rhs=PT[2 * par + mp], start=first_k,
                                               stop=last_k)
                            return ins
                        flush_pv()
                        pending_pv.append(lambda mmPV=mmPV, par=par, kc=kc: S.add(
                            "pe", mmPV, reads=[("PT", 2 * par), ("PT", 2 * par + 1), ("V", kc)],
                            writes=[("ps", 4), ("ps", 5), ("ps", 6), ("ps", 7)]))
                        if ki >= 1 and deferred:
                            deferred.pop(0)()
                    flush_pv()
                    while deferred:
                        deferred.pop(0)()
                    for mp in range(2):
                        S.add("act", lambda e, mp=mp: e.activation(out=tt[mp], in_=ps[:, 4 + mp, :], func=AF.Copy),
                              reads=[("ps", 4 + mp)], writes=[("tt", mp)])
                        S.add("act", lambda e, mp=mp: e.activation(out=rd[mp], in_=ps[:, 6 + mp, :], func=AF.Copy),
                              reads=[("ps", 6 + mp)], writes=[("rd", mp)])
                    deferred.extend(post_chain(qt))
                while deferred:
                    deferred.pop(0)()
                S.add("sp", lambda e, h=h, T=T, tok0=tok0: e.dma_start(out=self.MO[4 + h, :, tok0:tok0 + T], in_=moT),
                      reads=[("moT", t) for t in range(ntile)], writes=[("MO", 4 + h)], dma_key=("mo", h % 2))
                S.barrier()

        def do_b4(T, tok0):
            import os as _os
            nblk = T // 128
            ntile = T // 512
            A.off = A2
            wout = A.alloc([8, D], BF16)
            lng = A.alloc([D], F32)
            lnb = A.alloc([D], F32)
            S.add("pool", lambda e: e.dma_start(out=wout, in_=self.w_out[l].rearrange("(k p) n -> p k n", p=128)),
                  writes=["wout"], dma_key="wout")
            S.add("sp", lambda e: e.dma_start(out=lng, in_=self.ln[(2, "g")][l:l + 1, :].broadcast_to([128, D])),
                  writes=["lng"], dma_key="lng")
            S.add("sp", lambda e: e.dma_start(out=lnb, in_=self.ln[(2, "b")][l:l + 1, :].broadcast_to([128, D])),
                  writes=["lnb"], dma_key="lnb")
            mot = [A.alloc([8, 128], BF16) for _ in range(2)]
            xres = [A.alloc([D], F32) for _ in range(2)]
            yb = [A.alloc([D], F32) for _ in range(2)]
            scr = [dict(st=A.alloc([2, 6], F32), mv=A.alloc([2], F32), rs=A.alloc([1], F32)) for _ in range(2)]
            for blk in range(nblk):
                s = blk % 2
                rows = slice(tok0 + blk * 128, tok0 + (blk + 1) * 128)
                if _os.environ.get("NOMO") == "1":
                    S.add("pool", lambda e, s=s: e.memset(mot[s], 0.5), writes=[("mot", s)])
                else:
                    S.add("sp", lambda e, s=s, rows=rows: [e.dma_start(out=mot[s][:, k, :], in_=self.MO[k, :, rows])
                                                           for k in range(8)],
                          writes=[("mot", s)], dma_key=("mot", s), ndma=8)
                S.add("sp", lambda e, s=s, rows=rows: e.dma_start(out=xres[s], in_=src[rows, :]),
                      writes=[("xres", s)], dma_key=("xres", s))
                b0 = 2 * (blk % 4)

                def mm(e, s=s, b0=b0):
                    for half in range(2):
                        for k in range(8):
                            ins = e.matmul(ps[:, b0 + half, :], lhsT=mot[s][:, k, :], rhs=wout[:, k, half * 512:(half + 1) * 512],
                                           start=(k == 0), stop=(k == 7))
                    return ins
                S.add("pe", mm, reads=[("mot", s), "wout"], writes=[("ps", b0), ("ps", b0 + 1)])
                ysb = yb[s]
                pso = ps[:, b0:b0 + 2, :].rearrange("p a b -> p (a b)")
                S.add("act", lambda e, ysb=ysb, pso=pso: e.activation(out=ysb, in_=pso, func=AF.Copy),
                      reads=[("ps", b0), ("ps", b0 + 1)], writes=[("my", s)])
                S.add("dve", lambda e, ysb=ysb, s=s: e.scalar_tensor_tensor(
                    out=ysb, in0=xres[s], scalar=ALPHA, in1=ysb, op0=ALU.mult, op1=ALU.add),
                    reads=[("my", s), ("xres", s)], writes=[("my", s)])
                self.ln_block(ysb, s, lng, lnb, scr[s], "m")
                S.add("pool", lambda e, ysb=ysb, rows=rows: e.dma_start(out=dst[rows, :], in_=ysb),
                      reads=[("my", s)], dma_key=("yst", s))
            S.barrier()

        tok0 = 0
        for si, T in enumerate(self.seq_lens):
            import os as _os
            if self.mix_stage >= 1 and _os.environ.get("NOB1") != "1":
                do_b1(T, tok0)
            if self.mix_stage >= 2:
                for p in range(4):
                    do_ret(p, T, tok0)
            if self.mix_stage >= 3:
                for h in range(4):
                    do_att(h, T, tok0)
            if self.mix_stage >= 1 and _os.environ.get("NOB4") != "1":
                do_b4(T, tok0)
            tok0 += T
        S.barrier(new_set=True)


_PARAM_NAMES = ["w_in", "w_out", "ret_decay_f", "ret_decay_b", "ret_gn_g", "ret_gn_b",
                "diff_lq1", "diff_lk1", "diff_lq2", "diff_lk2", "diff_subln_g",
                "ffn1_w13", "ffn1_w2", "ffn2_w13", "ffn2_w2",
                "ln1_g", "ln1_b", "ln2_g", "ln2_b", "ln3_g", "ln3_b"]


def extra_layouts(params):
    f, b = params["ret_decay_f"], params["ret_decay_b"]
    dec_all = np.ascontiguousarray(np.concatenate([f, b], axis=1))
    dec_pairs = np.ascontiguousarray(np.stack(
        [np.concatenate([f[:, 0::2], b[:, 0::2]], axis=1), np.concatenate([f[:, 1::2], b[:, 1::2]], axis=1)], axis=1))
    return {"dec_all": dec_all, "dec_pairs": dec_pairs}


def kernel(**inputs):
    xp = np.asarray(inputs["x_prompt"], dtype=np.float32)
    xs = np.asarray(inputs["x_sample"], dtype=np.float32)
    params = {k: np.ascontiguousarray(np.asarray(inputs[k], dtype=np.float32)) for k in _PARAM_NAMES}
    params.update(extra_layouts(params))
    nc = Builder(SEQ_LENS).build()
    in_maps = []
    for c in range(N_CORES):
        xc = np.concatenate([xp[c].reshape(-1, D), xs[4 * c:4 * c + 4].reshape(-1, D)], axis=0)
        m = {"x": np.ascontiguousarray(xc)}
        m.update(params)
        in_maps.append(m)
    res = run_bass_kernel_spmd(nc, in_maps, core_ids=list(range(N_CORES)))
    yp = np.empty_like(xp)
    ys = np.empty_like(xs)
    for c in range(N_CORES):
        yc = np.asarray(res.results[c]["y"], dtype=np.float32)
        yp[c] = yc[:4096]
        ys[4 * c:4 * c + 4] = yc[4096:].reshape(4, 2048, D)
    return (yp, ys)
```
